# Optimizing a Trainium2 kernel written in Bass

```python
import math
import jax, jax.numpy as jnp
from jax import lax
import numpy as np

D_MODEL = 2048
BATCH = 2
SEQ = 4096
DEPTH = 4

N_MIXERS = 2
N_FOX_LAYERS = (DEPTH + 1) // 2
N_RET_LAYERS = DEPTH // 2

FOX_HEAD_DIM = 128
FOX_HEADS = D_MODEL // FOX_HEAD_DIM
FOX_WIDTH = FOX_HEADS * FOX_HEAD_DIM
FOX_BLOCK = 128
FOX_IN_COLS = 4 * FOX_WIDTH + FOX_HEADS

RET_QK_DIM = 256
RET_HEADS = D_MODEL // RET_QK_DIM
RET_V_DIM = 2 * RET_QK_DIM
RET_QK_WIDTH = RET_HEADS * RET_QK_DIM
RET_V_WIDTH = RET_HEADS * RET_V_DIM
RET_CHUNK = 128
RET_IN_COLS = 2 * RET_QK_WIDTH + 2 * RET_V_WIDTH
ROPE_BASE = 10000.0

LN_EPS = 1e-5
GN_EPS = 1e-6
QK_EPS = 1e-6
DEEPNORM_ALPHA = (2.0 * DEPTH) ** 0.25
DEEPNORM_BETA = (8.0 * DEPTH) ** -0.25

kernel_name = "fox_retnet_interleaved_deepnorm"


def layer_norm(x, g, b):
    xf = x.astype(jnp.float32)
    mu = jnp.mean(xf, axis=-1, keepdims=True)
    var = jnp.mean(jnp.square(xf - mu), axis=-1, keepdims=True)
    y = (xf - mu) * lax.rsqrt(var + LN_EPS) * g.astype(jnp.float32) + b.astype(jnp.float32)
    return y.astype(x.dtype)


def rms_norm(x, g):
    xf = x.astype(jnp.float32)
    y = xf * lax.rsqrt(jnp.mean(jnp.square(xf), axis=-1, keepdims=True) + QK_EPS) * g.astype(jnp.float32)
    return y.astype(x.dtype)


def group_norm_heads(x):
    mu = jnp.mean(x, axis=-1, keepdims=True)
    var = jnp.mean(jnp.square(x - mu), axis=-1, keepdims=True)
    return (x - mu) * lax.rsqrt(var + GN_EPS)


def rotary(t):
    S, d = t.shape[1], t.shape[-1]
    inv_freq = ROPE_BASE ** (-jnp.arange(0, d, 2, dtype=jnp.float32) / d)
    ang = jnp.arange(S, dtype=jnp.float32)[:, None] * inv_freq[None, :]
    cos = jnp.cos(ang)[None, :, None, :]
    sin = jnp.sin(ang)[None, :, None, :]
    t1, t2 = t[..., : d // 2], t[..., d // 2:]
    return jnp.concatenate([t1 * cos - t2 * sin, t1 * sin + t2 * cos], axis=-1)


def fox_branch(x, w_in, b_f, q_gain, k_gain, w_out):
    B, S, _ = x.shape
    proj = jnp.einsum('bsd,de->bse', x, w_in)
    q, k, v, gate, f_logit = jnp.split(
        proj, [FOX_WIDTH, 2 * FOX_WIDTH, 3 * FOX_WIDTH, 4 * FOX_WIDTH], axis=-1)

    def heads(t):
        return t.reshape(B, S, FOX_HEADS, FOX_HEAD_DIM).transpose(0, 2, 1, 3)

    q = rms_norm(heads(q), q_gain)
    k = rms_norm(heads(k), k_gain)
    v = heads(v)
    log_f = jax.nn.log_sigmoid((f_logit + b_f).astype(jnp.float32))
    c = jnp.cumsum(log_f, axis=1).transpose(0, 2, 1)

    n_blocks = S // FOX_BLOCK
    q_blocks = q.reshape(B, FOX_HEADS, n_blocks, FOX_BLOCK, FOX_HEAD_DIM).transpose(2, 0, 1, 3, 4)
    c_blocks = c.reshape(B, FOX_HEADS, n_blocks, FOX_BLOCK).transpose(2, 0, 1, 3)
    key_pos = jnp.arange(S)
    scale = FOX_HEAD_DIM ** -0.5

    def attend_block(args):
        qb, cb, blk = args
        s = jnp.einsum('bhqd,bhkd->bhqk', qb, k).astype(jnp.float32) * scale
        s = s + cb[..., :, None] - c[..., None, :]
        q_pos = blk * FOX_BLOCK + jnp.arange(FOX_BLOCK)
        s = jnp.where(key_pos[None, :] <= q_pos[:, None], s, -jnp.inf)
        p = jax.nn.softmax(s, axis=-1).astype(v.dtype)
        return jnp.einsum('bhqk,bhkd->bhqd', p, v)

    o = lax.map(attend_block, (q_blocks, c_blocks, jnp.arange(n_blocks)))
    o = o.transpose(1, 0, 3, 2, 4).reshape(B, S, FOX_WIDTH)
    y = o * jax.nn.silu(gate)
    return jnp.einsum('bse,ed->bsd', y, w_out)


def retention_branch(x, w_in, w_out):
    B, S, _ = x.shape
    proj = jnp.einsum('bsd,de->bse', x, w_in)
    q, k, v, gate = jnp.split(
        proj, [RET_QK_WIDTH, 2 * RET_QK_WIDTH, 2 * RET_QK_WIDTH + RET_V_WIDTH], axis=-1)
    q = rotary(q.reshape(B, S, RET_HEADS, RET_QK_DIM).astype(jnp.float32))
    k = rotary(k.reshape(B, S, RET_HEADS, RET_QK_DIM).astype(jnp.float32)) * (RET_QK_DIM ** -0.5)
    v = v.reshape(B, S, RET_HEADS, RET_V_DIM).astype(jnp.float32)

    log_gamma = jnp.log1p(-jnp.exp2(-5.0 - jnp.arange(RET_HEADS, dtype=jnp.float32)))
    pos = jnp.arange(RET_CHUNK, dtype=jnp.float32)
    diff = pos[:, None] - pos[None, :]
    intra_decay = jnp.where(diff >= 0,
                            jnp.exp(jnp.maximum(diff, 0.0)[None] * log_gamma[:, None, None]),
                            0.0)
    query_decay = jnp.exp((pos[None, :] + 1.0) * log_gamma[:, None])
    key_decay = jnp.exp((RET_CHUNK - 1.0 - pos[None, :]) * log_gamma[:, None])
    chunk_decay = jnp.exp(RET_CHUNK * log_gamma)

    n_chunks = S // RET_CHUNK

    def chunks(t):
        return t.reshape(B, n_chunks, RET_CHUNK, RET_HEADS, t.shape[-1]).transpose(1, 0, 3, 2, 4)

    def step(R, inp):
        qc, kc, vc = inp
        inner = jnp.einsum('bhqd,bhkd->bhqk', qc, kc) * intra_decay[None]
        o = (jnp.einsum('bhqk,bhkv->bhqv', inner, vc)
             + jnp.einsum('bhqd,bhdv->bhqv', qc, R) * query_decay[None, :, :, None])
        R = (R * chunk_decay[None, :, None, None]
             + jnp.einsum('bhkd,bhkv->bhdv', kc * key_decay[None, :, :, None], vc))
        return R, o

    R0 = jnp.zeros((B, RET_HEADS, RET_QK_DIM, RET_V_DIM), jnp.float32)
    _, o = lax.scan(step, R0, (chunks(q), chunks(k), chunks(v)))
    o = o.transpose(1, 0, 3, 2, 4)
    o = group_norm_heads(o).reshape(B, S, RET_V_WIDTH).astype(gate.dtype)
    y = o * jax.nn.silu(gate)
    return jnp.einsum('bse,ed->bsd', y, w_out)


def setup_inputs(seed: int = 0) -> dict:
    key = jax.random.key(seed)
    ks = jax.random.split(key, 10)
    std_in = D_MODEL ** -0.5
    fox_col_scale = jnp.ones((FOX_IN_COLS,), jnp.float32).at[2 * FOX_WIDTH:3 * FOX_WIDTH].set(DEEPNORM_BETA)
    ret_col_scale = jnp.ones((RET_IN_COLS,), jnp.float32).at[
        2 * RET_QK_WIDTH:2 * RET_QK_WIDTH + RET_V_WIDTH].set(DEEPNORM_BETA)
    x = jax.random.normal(ks[0], (BATCH, SEQ, D_MODEL), jnp.float32)
    fox_w_in = jax.random.normal(ks[1], (N_FOX_LAYERS, D_MODEL, FOX_IN_COLS), jnp.float32) * std_in * fox_col_scale
    fox_b_f = 3.0 + 0.5 * jax.random.normal(ks[2], (N_FOX_LAYERS, FOX_HEADS), jnp.float32)
    fox_q_gain = 1.0 + 0.02 * jax.random.normal(ks[3], (N_FOX_LAYERS, FOX_HEAD_DIM), jnp.float32)
    fox_k_gain = 1.0 + 0.02 * jax.random.normal(ks[4], (N_FOX_LAYERS, FOX_HEAD_DIM), jnp.float32)
    fox_w_out = (jax.random.normal(ks[5], (N_FOX_LAYERS, FOX_WIDTH, D_MODEL), jnp.float32)
                 * (FOX_WIDTH ** -0.5) * DEEPNORM_BETA)
    ret_w_in = jax.random.normal(ks[6], (N_RET_LAYERS, D_MODEL, RET_IN_COLS), jnp.float32) * std_in * ret_col_scale
    ret_w_out = (jax.random.normal(ks[7], (N_RET_LAYERS, RET_V_WIDTH, D_MODEL), jnp.float32)
                 * (RET_V_WIDTH ** -0.5) * DEEPNORM_BETA)
    ln_gain = 1.0 + 0.02 * jax.random.normal(ks[8], (DEPTH, D_MODEL), jnp.float32)
    ln_bias = 0.02 * jax.random.normal(ks[9], (DEPTH, D_MODEL), jnp.float32)
    return {"x": x, "fox_w_in": fox_w_in, "fox_b_f": fox_b_f, "fox_q_gain": fox_q_gain,
            "fox_k_gain": fox_k_gain, "fox_w_out": fox_w_out, "ret_w_in": ret_w_in,
            "ret_w_out": ret_w_out, "ln_gain": ln_gain, "ln_bias": ln_bias}


def reference(x, fox_w_in, fox_b_f, fox_q_gain, fox_k_gain, fox_w_out, ret_w_in, ret_w_out,
              ln_gain, ln_bias):
    h = x
    for i in range(DEPTH):
        j = i // N_MIXERS
        if i % N_MIXERS == 0:
            y = fox_branch(h, fox_w_in[j], fox_b_f[j], fox_q_gain[j], fox_k_gain[j], fox_w_out[j])
        else:
            y = retention_branch(h, ret_w_in[j], ret_w_out[j])
        h = layer_norm(DEEPNORM_ALPHA * h + y, ln_gain[i], ln_bias[i])
    return h
```

```python
import contextlib
import numpy as np
import ml_dtypes
import concourse.bass as bass
import concourse.mybir as mybir
from concourse.bass_utils import run_bass_kernel_spmd

F32 = mybir.dt.float32
BF16 = mybir.dt.bfloat16
U8 = mybir.dt.uint8
AF = mybir.ActivationFunctionType
ALU = mybir.AluOpType
AX = mybir.AxisListType
NPBF = ml_dtypes.bfloat16

D = 2048
S = 4096
NB = 2
DEPTH = 4
TOKC = 1024
ALPHA = (2.0 * DEPTH) ** 0.25
LN_EPS = 1e-5
GN_EPS = 1e-6
QK_EPS = 1e-6
NEG = -30000.0

ENGS = ("pe", "act", "dve", "pool", "sp")


class Tok:
    __slots__ = ("eng", "idx", "sem", "val")

    def __init__(self, eng=None, idx=None, sem=None, val=None):
        self.eng, self.idx, self.sem, self.val = eng, idx, sem, val


class Prog:
    def __init__(self, nc):
        self.nc = nc
        self.ops = {e: [] for e in ENGS}
        self.dma_cnt = {}
        self.sig = {e: set() for e in ENGS}
        self.last = {e: None for e in ENGS}
        self.pending = {e: [] for e in ENGS}

    def _deps(self, eng, deps):
        deps = [d for d in deps if d is not None]
        if self.pending[eng]:
            deps = deps + self.pending[eng]
            self.pending[eng] = []
        for d in deps:
            if d.eng is not None and not (d.eng == "pe" and eng == "pe"):
                self.sig[d.eng].add(d.idx)
        return deps

    def op(self, eng, fn, deps=()):
        deps = self._deps(eng, deps)
        idx = len(self.ops[eng])
        self.ops[eng].append((fn, deps, None))
        t = Tok(eng=eng, idx=idx)
        self.last[eng] = t
        return t

    def dma(self, queue, semkey, fn, deps=(), inc=16):
        deps = self._deps(queue, deps)
        self.dma_cnt[semkey] = self.dma_cnt.get(semkey, 0) + inc
        self.ops[queue].append((fn, deps, (semkey, inc)))
        return Tok(sem=semkey, val=self.dma_cnt[semkey])

    def barrier(self, extra=()):
        best = {}
        for t in extra:
            if t is None:
                continue
            if t.sem is not None:
                if t.sem not in best or best[t.sem].val < t.val:
                    best[t.sem] = t
        toks = [t for t in self.last.values() if t is not None] + list(best.values())
        for e in ENGS:
            self.pending[e] = list(toks)

    def emit(self, final_waits=()):
        nc = self.nc
        with contextlib.ExitStack() as st:
            esem = {e: st.enter_context(nc.semaphore("s_" + e)) for e in ENGS}
            dsem = {k: st.enter_context(nc.semaphore("d_%s" % str(k))) for k in self.dma_cnt}
            sigval = {}
            for e in ENGS:
                c = 0
                for i in range(len(self.ops[e])):
                    if i in self.sig[e]:
                        c += 1
                        sigval[(e, i)] = c
            print("[prog] ops:", {e: len(self.ops[e]) for e in ENGS}, "signals:",
                  {e: len(self.sig[e]) for e in ENGS}, "dma sems:", len(self.dma_cnt),
                  "max dma cnt:", max(self.dma_cnt.values()) if self.dma_cnt else 0, flush=True)
            block = st.enter_context(nc.Block())

            def run(e, eng):
                waited = {}
                for i, (fn, deps, semkey) in enumerate(self.ops[e]):
                    for d in deps:
                        if d.eng is not None:
                            if d.eng == "pe" and e == "pe":
                                continue
                            key, v, s = ("e", d.eng), sigval[(d.eng, d.idx)], esem[d.eng]
                        else:
                            key, v, s = ("d", d.sem), d.val, dsem[d.sem]
                        if waited.get(key, 0) >= v:
                            continue
                        waited[key] = v
                        eng.wait_ge(s, v)
                    ins = fn(eng)
                    if semkey is not None:
                        ins.then_inc(dsem[semkey[0]], semkey[1])
                    elif (e, i) in sigval:
                        ins.then_inc(esem[e], 1)
                if e == "sp":
                    for t in final_waits:
                        eng.wait_ge(dsem[t.sem], t.val)

            block.tensor(lambda eng: run("pe", eng))
            block.scalar(lambda eng: run("act", eng))
            block.vector(lambda eng: run("dve", eng))
            block.gpsimd(lambda eng: run("pool", eng))
            block.sync(lambda eng: run("sp", eng))


DTSIZE = {F32: 4, BF16: 2, U8: 1}


class Arena:
    def __init__(self, ar, size):
        self.ar, self.size, self.off = ar, size, 0

    def mark(self):
        return self.off

    def reset(self, m):
        self.off = m

    def alloc(self, shape, dt):
        n = int(np.prod(shape[1:])) * DTSIZE[dt]
        off = (self.off + 63) // 64 * 64
        assert off + n <= self.size, ("SBUF arena overflow", off, n, self.size)
        self.off = off + n
        ap = self.ar[0:shape[0], off:off + n].bitcast(dt)
        dims = list(shape[1:])
        if len(dims) > 1:
            names = "abcd"[:len(dims)]
            pat = "p (%s) -> p %s" % (" ".join(names), " ".join(names))
            kw = {names[i]: dims[i] for i in range(1, len(dims))}
            ap = ap.rearrange(pat, **kw)
        return ap


class Ctx:
    pass


def mm(P, out, lhsT, rhs, start, stop, deps=()):
    return P.op("pe", lambda e: e.matmul(out, lhsT, rhs, start=start, stop=stop), deps)


def tr(P, out, in_, ident, deps=()):
    return P.op("pe", lambda e: e.transpose(out, in_, ident), deps)


def act(P, out, in_, func, deps=(), bias=None, scale=None):
    kw = {}
    if bias is not None:
        kw["bias"] = bias
    if scale is not None:
        kw["scale"] = scale
    return P.op("act", lambda e: e.activation(out=out, in_=in_, func=func, **kw), deps)


def tt(P, eng, out, in0, in1, op, deps=()):
    return P.op(eng, lambda e: e.tensor_tensor(out=out, in0=in0, in1=in1, op=op), deps)


def ts(P, eng, out, in0, s1, s2, op0, op1=None, deps=()):
    if op1 is None:
        return P.op(eng, lambda e: e.tensor_scalar(out=out, in0=in0, scalar1=s1, scalar2=None, op0=op0), deps)
    return P.op(eng, lambda e: e.tensor_scalar(out=out, in0=in0, scalar1=s1, scalar2=s2, op0=op0, op1=op1), deps)


def stt(P, out, in0, scalar, in1, op0, op1, deps=()):
    return P.op("dve", lambda e: e.scalar_tensor_tensor(out=out, in0=in0, scalar=scalar, in1=in1, op0=op0, op1=op1), deps)


def cp(P, eng, out, in_, deps=()):
    if eng == "act":
        return P.op("act", lambda e: e.activation(out=out, in_=in_, func=AF.Copy), deps)
    return P.op(eng, lambda e: e.tensor_copy(out=out, in_=in_), deps)


def bnstats(P, out, in_, deps=()):
    return P.op("dve", lambda e: e.bn_stats(out=out, in_=in_), deps)


def bnaggr(P, out, in_, deps=()):
    return P.op("dve", lambda e: e.bn_aggr(out=out, in_=in_), deps)


def coll(P, key, kind, op, groups, in_, out, deps=()):
    return P.dma("pool", key, lambda e: e.collective_compute(kind, op, replica_groups=groups, ins=[in_], outs=[out]),
                 deps, inc=1)


def dma(P, q, key, out, in_, deps=()):
    return P.dma(q, key, lambda e: e.dma_start(out=out, in_=in_), deps)


def load_consts(P, C):
    nc, A = C.nc, C.arena
    ident = np.eye(128, dtype=np.float32)
    kk = np.arange(128)[:, None]
    qq = np.arange(128)[None, :]
    negmask = np.where(kk > qq, NEG, 0.0).astype(np.float32)
    sel = np.zeros((4, 4, 128), np.float32)
    for h in range(4):
        sel[h, h, :] = 1.0
    c = Ctx()
    c.ident_bf = A.alloc([128, 128], BF16)
    c.ones_bf = A.alloc([128, 128], BF16)
    c.negmask_bf = A.alloc([128, 128], BF16)
    c.ident_f = A.alloc([128, 128], F32)
    c.sel = A.alloc([4, 4, 128], F32)
    toks = []
    d_id = nc.inline_tensor(ident.astype(NPBF), "c_identbf").ap()
    d_on = nc.inline_tensor(np.ones((128, 128), NPBF), "c_onesbf").ap()
    d_nm = nc.inline_tensor(negmask.astype(NPBF), "c_negmask").ap()
    d_if = nc.inline_tensor(ident, "c_identf").ap()
    d_sel = nc.inline_tensor(sel.transpose(1, 0, 2).copy(), "c_sel").ap()
    toks.append(dma(P, "sp", "const", c.ident_bf, d_id[:, :]))
    toks.append(dma(P, "sp", "const", c.ones_bf, d_on[:, :]))
    toks.append(dma(P, "sp", "const", c.negmask_bf, d_nm[:, :]))
    toks.append(dma(P, "sp", "const", c.ident_f, d_if[:, :]))
    toks.append(dma(P, "sp", "const", c.sel, d_sel[:, :, :]))
    c.tok = toks[-1]
    C.k = c
    C.eps_qk = A.alloc([128, 1], F32)
    C.eps_ln = A.alloc([128, 1], F32)
    C.eps_gn = A.alloc([128, 1], F32)
    P.op("pool", lambda e: e.memset(C.eps_qk, 128.0 * QK_EPS))
    P.op("pool", lambda e: e.memset(C.eps_ln, LN_EPS))
    C.t_eps = P.op("pool", lambda e: e.memset(C.eps_gn, GN_EPS))


def phase_D(P, C, xres, x_in, parts, nparts, lng, lnb, xT_out, x_out, deps=()):
    nc, A, K, PS = C.nc, C.arena, C.k, C.ps
    m0 = A.mark()
    out_toks = []
    t_x = None
    if x_in is not None:
        t_x = dma(P, "sp", "dx", xres, x_in.rearrange("(j p) d -> p j d", p=128), deps)
    if parts is not None:
        gb = A.alloc([128, D], F32)
        bb = A.alloc([128, D], F32)
        t_g = dma(P, "sp", "dgb", gb, lng.partition_broadcast(128), deps)
        t_b = dma(P, "sp", "dgb", bb, lnb.partition_broadcast(128), deps)
        zt = Rot([A.alloc([128, D], F32) for _ in range(3)])
        hb = [A.alloc([128, D], F32) for _ in range(2)]
        st6 = A.alloc([128, 4, 6], F32)
        mv = A.alloc([128, 2], F32)
        rstd = A.alloc([128, 1], F32)
        nmr = A.alloc([128, 1], F32)
    xb = [A.alloc([128, D], BF16) for _ in range(2)]
    xTs = A.alloc([128, 16, TOKC], BF16) if xT_out is not None else None
    z_free = [None, None]
    h_free = [None, None]
    xb_free = [None, None]
    ps_free = [None, None]
    t_last_xres = None
    for j in range(8):
        b2 = j % 2
        if parts is not None:
            t_h = None
            for r in range(nparts):
                zi = zt.next()
                tz = dma(P, "sp", "dz%d" % zi, zt.bufs[zi], parts[r, j * 128:(j + 1) * 128, :],
                         list(deps) + [zt.free[zi]])
                if r == 0:
                    t_h = stt(P, hb[b2], xres[:, j, :], ALPHA, zt.bufs[zi], ALU.mult, ALU.add,
                              [t_x, tz, h_free[b2]])
                else:
                    t_h = tt(P, "pool", hb[b2], hb[b2], zt.bufs[zi], ALU.add, [t_h, tz])
                zt.free[zi] = t_h
            t_s = None
            for q in range(4):
                t_s = bnstats(P, st6[:, q, :], hb[b2][:, q * 512:(q + 1) * 512], [t_h, t_s])
            t_s = bnaggr(P, mv, st6, [t_s])
            t_s = act(P, rstd, mv[:, 1:2], AF.Sqrt, [t_s, C.t_eps], bias=C.eps_ln)
            t_s = P.op("dve", lambda e: e.reciprocal(out=rstd, in_=rstd), [t_s])
            t_s = ts(P, "dve", nmr, mv[:, 0:1], rstd, -1.0, ALU.mult, ALU.mult, [t_s])
            t_n = act(P, hb[b2], hb[b2], AF.Identity, [t_s, t_h], bias=nmr, scale=rstd)
            t_n = tt(P, "dve", hb[b2], hb[b2], gb, ALU.mult, [t_n, t_b])
            t_n = tt(P, "pool", xres[:, j, :], hb[b2], bb, ALU.add, [t_n, t_b, t_x])
            h_free[b2] = t_n
            t_xj = t_n
        else:
            t_xj = t_x
        t_last_xres = t_xj
        if xT_out is not None:
            t_c = cp(P, "act", xb[b2], xres[:, j, :], [t_xj, xb_free[b2]])
            for g in range(2):
                pb = PS[g].bitcast(BF16).rearrange("p (a b) -> p a b", b=128)
                t_t = None
                for c8 in range(8):
                    cc = g * 8 + c8
                    t_t = tr(P, pb[:, c8, :], xb[b2][:, cc * 128:(cc + 1) * 128], K.ident_bf,
                             [t_c, K.tok, ps_free[g]])
                ps_free[g] = cp(P, "dve", xTs[:, g * 8:(g + 1) * 8, j * 128:(j + 1) * 128], pb, [t_t])
            xb_free[b2] = t_t
    if xT_out is not None:
        out_toks.append(dma(P, "sp", "dxT", xT_out.rearrange("(c p) t -> p c t", p=128), xTs,
                            [ps_free[0], ps_free[1]]))
    if x_out is not None:
        out_toks.append(dma(P, "sp", "dxo", x_out.rearrange("(j p) d -> p j d", p=128), xres,
                            [t_last_xres]))
    A.reset(m0)
    return out_toks


def load_w_bf16(P, key, dst, src, deps=()):
    return dma(P, "pool", key, dst, src.rearrange("(c p) n -> p c n", p=128), deps)


class Rot:
    def __init__(self, bufs):
        self.bufs = bufs
        self.free = [None] * len(bufs)
        self.i = 0

    def next(self):
        k = self.i % len(self.bufs)
        self.i += 1
        return k


def preload_A_fox(P, C, w_d, deps=()):
    A = C.arena
    W = A.alloc([128, 16, 2048], BF16)
    Wf = A.alloc([128, 16, 4], BF16)
    tw = []
    for s in range(4):
        tw.append(load_w_bf16(P, "wA%d" % s, W[:, :, s * 512:(s + 1) * 512], w_d[:, s * 512:(s + 1) * 512], deps))
    twf = load_w_bf16(P, "wAf", Wf, w_d[:, 2048:2052], deps)
    return (W, Wf, tw, twf)


def phase_A_fox(P, C, xT_d, w_d, bf_d, gq_d, gk_d, scr, deps=(), xdeps=None, pre=None):
    nc, A, K, PS = C.nc, C.arena, C.k, C.ps
    if pre is None:
        pre = preload_A_fox(P, C, w_d, deps)
    W, Wf, tw, twf = pre
    gq = A.alloc([128, 1], F32)
    gk = A.alloc([128, 1], F32)
    bfc = A.alloc([4, 1], F32)
    t_small = dma(P, "sp", "smallA", gq, gq_d, deps)
    t_small = dma(P, "sp", "smallA", gk, gk_d, deps)
    t_small = dma(P, "sp", "smallA", bfc, bf_d, deps)
    t_gk = ts(P, "dve", gk, gk, float(np.sqrt(128.0)), None, ALU.mult, None, [t_small])
    xt = Rot([A.alloc([128, 16, 512], BF16) for _ in range(2)])
    sq = Rot([A.alloc([128, 512], BF16) for _ in range(2)])
    rr = Rot([A.alloc([128, 512], F32) for _ in range(2)])
    st_qk = Rot([A.alloc([128, 512], BF16) for _ in range(3)])
    st_g = Rot([A.alloc([128, 512], BF16) for _ in range(2)])
    st_v = Rot([A.alloc([128, 4, 4, 128], BF16) for _ in range(2)])
    fz = A.alloc([4, 512], F32)
    fa = A.alloc([4, 512], F32)
    fm = A.alloc([4, 512], F32)
    pm = Rot([PS[0], PS[1], PS[2], PS[3]])
    pq = Rot([PS[4], PS[5]])
    pf = PS[6]
    pf_free = [None]
    f_free = [None]
    stores = []
    ncr = C.ncr
    pending_ep = []

    def flush():
        while pending_ep:
            pending_ep.pop(0)()

    for tt_i in range(8):
        r, half = tt_i // 2, tt_i % 2
        xk = xt.next()
        X = xt.bufs[xk]
        t_xt = dma(P, "sp", "xt%d" % xk, X,
                   xT_d[r][:, half * 512:(half + 1) * 512].rearrange("(c p) t -> p c t", p=128),
                   list(deps) + [xt.free[xk]] + ([xdeps[r]] if xdeps else []))
        last_pe = None
        units = [("q", h) for h in range(4)] + [("k", h) for h in range(4)] + [("g", h) for h in range(4)]
        for kind, h in units:
            col0 = {"q": 0, "k": 512, "g": 1536}[kind] + h * 128
            wtok = tw[{"q": 0, "k": 1, "g": 3}[kind]]
            b = pm.next()
            pacc = pm.bufs[b]
            t_m = None
            for c in range(16):
                t_m = mm(P, pacc, W[:, c, col0:col0 + 128], X[:, c, :], c == 0, c == 15,
                         [t_xt, wtok, pm.free[b]])
            last_pe = t_m
            flush()
            if kind == "g":
                sb = st_g.next()
                t_e = act(P, st_g.bufs[sb], pacc, AF.Silu, [t_m, st_g.free[sb]])
                pm.free[b] = t_e
                t_st = dma(P, "sp", "stg%d" % sb, scr.sgT[h, :, tt_i * 512:(tt_i + 1) * 512], st_g.bufs[sb], [t_e])
                st_g.free[sb] = t_st
                stores.append(t_st)
            else:
                s_i = sq.next()
                t_sq = act(P, sq.bufs[s_i], pacc, AF.Square, [t_m, sq.free[s_i]])

                def ep(kind=kind, h=h, pacc=pacc, b=b, s_i=s_i, t_sq=t_sq, tt_i=tt_i):
                    qb = pq.next()
                    t_q = mm(P, pq.bufs[qb], K.ones_bf, sq.bufs[s_i], True, True, [t_sq, K.tok, pq.free[qb]])
                    sq.free[s_i] = t_q
                    ri = rr.next()
                    t_r = act(P, rr.bufs[ri], pq.bufs[qb], AF.Sqrt, [t_q, rr.free[ri], C.t_eps], bias=C.eps_qk)
                    pq.free[qb] = t_r
                    t_r = P.op("dve", (lambda o: lambda e: e.reciprocal(out=o, in_=o))(rr.bufs[ri]), [t_r])
                    si = st_qk.next()
                    gcol = gq if kind == "q" else gk
                    t_n = stt(P, st_qk.bufs[si], pacc, gcol, rr.bufs[ri], ALU.mult, ALU.mult,
                              [t_r, t_gk, st_qk.free[si]])
                    rr.free[ri] = t_n
                    pm.free[b] = t_n
                    dst = (scr.qT if kind == "q" else scr.kT)[h, :, tt_i * 512:(tt_i + 1) * 512]
                    t_st = dma(P, "sp", "stqk%d" % si, dst, st_qk.bufs[si], [t_n])
                    st_qk.free[si] = t_st
                    stores.append(t_st)
                pending_ep.append(ep)
        vi = st_v.next()
        VS = st_v.bufs[vi]
        t_ev = None
        for jj in range(4):
            b = pm.next()
            pacc = pm.bufs[b]
            t_m = None
            for c in range(16):
                t_m = mm(P, pacc, X[:, c, jj * 128:(jj + 1) * 128], W[:, c, 1024:1536], c == 0, c == 15,
                         [t_xt, tw[2], pm.free[b]])
            flush()
            t_ev = cp(P, "act", VS[:, :, jj, :], pacc.rearrange("p (h d) -> p h d", d=128),
                      [t_m, st_v.free[vi] if jj == 0 else None])
            pm.free[b] = t_ev
        t_st = dma(P, "sp", "stv%d" % vi, scr.v[:, :, tt_i * 4:(tt_i + 1) * 4, :].rearrange("h p j d -> p h j d"),
                   VS, [t_ev])
        st_v.free[vi] = t_st
        stores.append(t_st)
        t_m = None
        for c in range(16):
            t_m = mm(P, pf[0:4, :], Wf[:, c, :], X[:, c, :], c == 0, c == 15, [t_xt, twf, pf_free[0]])
        last_pe = t_m
        xt.free[xk] = t_m
        t_z = ts(P, "dve", fz, pf[0:4, :], bfc, None, ALU.add, None, [t_m, t_small, f_free[0]])
        pf_free[0] = t_z
        t_a = act(P, fa, fz, AF.Abs, [t_z])
        t_a = act(P, fa, fa, AF.Exp, [t_a], scale=-1.0)
        t_a = act(P, fa, fa, AF.Ln, [t_a], bias=1.0)
        t_mx = ts(P, "dve", fm, fz, -1.0, 0.0, ALU.mult, ALU.max, [t_z])
        t_f = tt(P, "dve", ncr[:, tt_i * 512:(tt_i + 1) * 512], fm, fa, ALU.add, [t_mx, t_a])
        f_free[0] = t_f
    flush()
    C.t_ncr = t_f
    return stores


def phase_B_fox(P, C, gqrow_d, gkrow_d, scr, deps=()):
    nc, A, K, PS = C.nc, C.arena, C.k, C.ps
    ncr = C.ncr
    ones1 = A.alloc([4, 1], F32)
    ncc = A.alloc([4, S], F32)
    t0 = P.op("pool", lambda e: e.memset(ones1, 1.0), list(deps))
    onesr = ones1.to_broadcast([4, S])
    t_scan = P.op("dve", lambda e: e.tensor_tensor_scan(out=ncc, data0=onesr, data1=ncr, initial=0.0,
                                                        op0=ALU.mult, op1=ALU.add), [t0, C.t_ncr])
    gqb = A.alloc([128, 128], F32)
    gkb = A.alloc([128, 128], F32)
    t_g = dma(P, "sp", "gB", gqb, gqrow_d.partition_broadcast(128), deps)
    t_g = dma(P, "sp", "gB", gkb, gkrow_d.partition_broadcast(128), deps)
    mq = A.alloc([128, 1], F32)
    mk = A.alloc([128, 1], F32)
    negM = A.alloc([128, 1], F32)
    t_m = act(P, gqb, gqb, AF.Abs, [t_g])
    t_m = P.op("dve", lambda e: e.reduce_max(out=mq, in_=gqb, axis=AX.X), [t_m])
    t_m = act(P, gkb, gkb, AF.Abs, [t_m])
    t_m = P.op("dve", lambda e: e.reduce_max(out=mk, in_=gkb, axis=AX.X), [t_m])
    t_m = ts(P, "dve", negM, mq, mk, -float(128.0 ** 0.5), ALU.mult, ALU.mult, [t_m])
    biasK = A.alloc([128, 32, 4], F32)
    pT = PS[7]
    t_t = None
    for j in range(32):
        t_t = tr(P, pT[:, j * 4:(j + 1) * 4], ncc[:, j * 128:(j + 1) * 128], K.ident_f[0:4, 0:4],
                 [t_scan, K.tok])
    t_bk = ts(P, "dve", biasK, pT[:, 0:128].rearrange("p (j h) -> p j h", h=4), negM, None, ALU.add, None,
              [t_t, t_m])
    hb = []
    for i in range(2):
        h_ = Ctx()
        h_.q = A.alloc([128, S], BF16)
        h_.k = A.alloc([128, S], BF16)
        h_.v = A.alloc([128, 32, 128], BF16)
        h_.sg = A.alloc([128, S], BF16)
        h_.ncq = A.alloc([128, S], F32)
        h_.free = None
        hb.append(h_)
    sa = Rot([A.alloc([128, 512], F32) for _ in range(4)])
    pt = Rot([A.alloc([128, 512], BF16) for _ in range(4)])
    rl = A.alloc([128, 512], F32)
    of = A.alloc([128, 512], F32)
    ys = Rot([A.alloc([128, 512], BF16) for _ in range(2)])
    pS = Rot([PS[0], PS[1], PS[2], PS[7]])
    pS.free[3] = t_bk
    pO = [PS[3], PS[4]]
    pL = [PS[5], PS[6]]
    pOL_free = [None, None]
    rl_free = [None]
    stores = []
    hstores = []
    LAG = 3

    pairs = []
    for h in range(4):
        for t in range(8):
            for j in range(4 * t + 4):
                pairs.append((h, t, j))
    loaded = {}
    pv_q = []

    def load_head(h):
        H = hb[h % 2]
        dd = list(deps) + [H.free]
        tl = dma(P, "sp", "hq%d" % (h % 2), H.q, scr.qT[h], dd)
        tl = dma(P, "sp", "hq%d" % (h % 2), H.k, scr.kT[h], dd)
        tl = dma(P, "sp", "hq%d" % (h % 2), H.v, scr.v[h], dd)
        tl = dma(P, "sp", "hq%d" % (h % 2), H.sg, scr.sgT[h], dd)
        t_c = None
        for t8 in range(8):
            b = pS.next()
            t_b = mm(P, pS.bufs[b], K.sel[:, h, :], ncc[:, t8 * 512:(t8 + 1) * 512], True, True,
                     [t_scan, K.tok, pS.free[b], H.free])
            t_c = cp(P, "act", H.ncq[:, t8 * 512:(t8 + 1) * 512], pS.bufs[b], [t_b, H.free])
            pS.free[b] = t_c
        loaded[h] = (tl, t_c)

    def front(h, t, j):
        H = hb[h % 2]
        tl, t_c = loaded[h]
        diag = j >= 4 * t
        off = (j - 4 * t) * 128 if diag else 0
        N = 512 - off
        q0 = t * 512 + off
        b = pS.next()
        ps = pS.bufs[b]
        t_s = mm(P, ps[:, 0:N], H.k[:, j * 128:(j + 1) * 128], H.q[:, q0:q0 + N], True, not diag,
                 [tl, pS.free[b]])
        if diag:
            t_s = mm(P, ps[:, 0:128], K.ident_bf, K.negmask_bf, False, True, [K.tok])
        si = sa.next()
        t_d = tt(P, "dve", sa.bufs[si][:, 0:N], ps[:, 0:N], H.ncq[:, q0:q0 + N], ALU.subtract,
                 [t_s, t_c, sa.free[si]])
        pS.free[b] = t_d
        pi = pt.next()
        t_e = act(P, pt.bufs[pi][:, 0:N], sa.bufs[si][:, 0:N], AF.Exp, [t_d, t_bk, pt.free[pi]],
                  bias=biasK[:, j, h:h + 1])
        sa.free[si] = t_e
        return (h, t, j, off, N, pi, t_e)

    def back(h, t, j, off, N, pi, t_e):
        H = hb[h % 2]
        ob = t % 2
        last = (j == 4 * t + 3)
        t_o = mm(P, pO[ob][:, off:512], H.v[:, j, :], pt.bufs[pi][:, 0:N], j == 0, last,
                 [t_e, pOL_free[ob] if j == 0 else None])
        t_l = mm(P, pL[ob][:, off:512], K.ones_bf, pt.bufs[pi][:, 0:N], j == 0, last, [])
        pt.free[pi] = t_l
        if last:
            t_r = P.op("dve", lambda e: e.reciprocal(out=rl, in_=pL[ob]), [t_l, rl_free[0]])
            t_f = tt(P, "dve", of, pO[ob], rl, ALU.mult, [t_r])
            pOL_free[ob] = t_f
            yi = ys.next()
            t_y = tt(P, "pool", ys.bufs[yi], of, H.sg[:, t * 512:(t + 1) * 512], ALU.mult,
                     [t_f, ys.free[yi], loaded[h][0]])
            rl_free[0] = t_y
            t_st = dma(P, "sp", "sty%d" % yi, scr.yl[h][:, t * 512:(t + 1) * 512], ys.bufs[yi], [t_y])
            ys.free[yi] = t_st
            hstores.append(t_st)
            C.last_stores.append(t_st)
            if t == 7:
                H.free = t_y
                if scr.yg is not None:
                    stores.append(coll(P, "ccy", "AllGather", ALU.bypass, GROUPS,
                                       scr.yl[h].rearrange("p (a t) -> (p a) t", a=4),
                                       scr.yg[h].rearrange("p (a t) -> (p a) t", a=4), list(hstores)))
                else:
                    stores.extend(hstores)
                del hstores[:]

    load_head(0)
    for i, (h, t, j) in enumerate(pairs):
        if t == 0 and j == 0 and h + 1 < 4:
            pass
        pv_q.append(front(h, t, j))
        if len(pv_q) > LAG:
            back(*pv_q.pop(0))
        if t == 1 and j == 0 and h + 1 < 4:
            while pv_q and pv_q[0][0] < h:
                back(*pv_q.pop(0))
            load_head(h + 1)
    while pv_q:
        back(*pv_q.pop(0))
    return stores


def phase_C(P, C, yT_d, wo_d, part_out, ec, deps=()):
    nc, A, K, PS = C.nc, C.arena, C.k, C.ps
    Wo = A.alloc([128, ec, D], BF16)
    t_w = load_w_bf16(P, "wC", Wo, wo_d, deps)
    Y = A.alloc([128, ec, S], BF16)
    t_y = []
    for q in range(4):
        t_y.append(dma(P, "sp", "yC%d" % q, Y[:, :, q * 1024:(q + 1) * 1024],
                       yT_d[:, :, q * 1024:(q + 1) * 1024].rearrange("e p t -> p e t"), deps))
    ost = Rot([A.alloc([128, D], F32) for _ in range(2)])
    pm = Rot([PS[0], PS[1], PS[2], PS[3]])
    stores = []
    n = 0
    for j in range(32):
        oi = ost.next()
        O = ost.bufs[oi]
        t_e = None
        for dt in range(4):
            b = pm.next()
            t_m = None
            for e_ in range(ec):
                t_m = mm(P, pm.bufs[b], Y[:, e_, j * 128:(j + 1) * 128], Wo[:, e_, dt * 512:(dt + 1) * 512],
                         e_ == 0, e_ == ec - 1, [t_w, t_y[j // 8], pm.free[b]])
            eng = "act" if n % 2 == 0 else "dve"
            n += 1
            t_e = cp(P, eng, O[:, dt * 512:(dt + 1) * 512], pm.bufs[b], [t_m, ost.free[oi] if dt == 0 else None])
            pm.free[b] = t_e
            if dt == 2:
                t_e2 = t_e
        t_st = dma(P, "sp", "stC%d" % oi, part_out[j * 128:(j + 1) * 128, :], O, [t_e, t_e2])
        ost.free[oi] = t_st
        stores.append(t_st)
    return stores


ARENA_BYTES = 204 * 1024


def new_prog():
    nc = bass.Bass("TRN2", target_bir_lowering=False)
    return nc


class Scr:
    pass


def setup(nc, st):
    C = Ctx()
    C.nc = nc
    ar = st.enter_context(nc.sbuf_tensor("arena", [128, ARENA_BYTES], U8))
    C.arena = Arena(ar, ARENA_BYTES)
    C.ps = [st.enter_context(nc.psum_tensor("ps%d" % i, [128, 512], F32)) for i in range(8)]
    C.ps = [p[:, :] for p in C.ps]
    return C


def build_D(first, last):
    nc = new_prog()
    x_in = nc.dram_tensor("x_in", [TOKC, D], F32, kind="ExternalInput").ap()
    if not first:
        parts = nc.dram_tensor("parts", [4, TOKC, D], F32, kind="ExternalInput").ap()
        lng = nc.dram_tensor("lng", [1, D], F32, kind="ExternalInput").ap()
        lnb = nc.dram_tensor("lnb", [1, D], F32, kind="ExternalInput").ap()
    xT_out = None if last else nc.dram_tensor("xT_out", [D, TOKC], BF16, kind="ExternalOutput").ap()
    x_out = None if first else nc.dram_tensor("x_out", [TOKC, D], F32, kind="ExternalOutput").ap()
    with contextlib.ExitStack() as st:
        C = setup(nc, st)
        P = Prog(nc)
        load_consts(P, C)
        xres = C.arena.alloc([128, 8, D], F32)
        if first:
            outs = phase_D(P, C, xres, x_in, None, 0, None, None, xT_out, None)
        else:
            outs = phase_D(P, C, xres, x_in, parts, 4, lng[0], lnb[0], xT_out, x_out)
        P.emit(final_waits=outs)
    return nc


def build_L_fox():
    nc = new_prog()
    xT_d = nc.dram_tensor("xT", [4, D, TOKC], BF16, kind="ExternalInput").ap()
    w_d = nc.dram_tensor("w_in", [D, 2052], F32, kind="ExternalInput").ap()
    bf_d = nc.dram_tensor("bf", [4, 1], F32, kind="ExternalInput").ap()
    gq_d = nc.dram_tensor("gq", [128, 1], F32, kind="ExternalInput").ap()
    gk_d = nc.dram_tensor("gk", [128, 1], F32, kind="ExternalInput").ap()
    gqr_d = nc.dram_tensor("gqr", [1, 128], F32, kind="ExternalInput").ap()
    gkr_d = nc.dram_tensor("gkr", [1, 128], F32, kind="ExternalInput").ap()
    wo_d = nc.dram_tensor("w_out", [512, D], F32, kind="ExternalInput").ap()
    part = nc.dram_tensor("part", [S, D], F32, kind="ExternalOutput").ap()
    scr = Scr()
    scr.qT = nc.dram_tensor("s_qT", [4, 128, S], BF16).ap()
    scr.kT = nc.dram_tensor("s_kT", [4, 128, S], BF16).ap()
    scr.v = nc.dram_tensor("s_v", [4, 128, 32, 128], BF16).ap()
    scr.sgT = nc.dram_tensor("s_sgT", [4, 128, S], BF16).ap()
    scr.yT = nc.dram_tensor("s_yT", [4, 128, S], BF16).ap()
    with contextlib.ExitStack() as st:
        C = setup(nc, st)
        P = Prog(nc)
        load_consts(P, C)
        C.ncr = C.arena.alloc([4, S], F32)
        m0 = C.arena.mark()
        stA = phase_A_fox(P, C, xT_d, w_d, bf_d, gq_d, gk_d, scr)
        P.barrier()
        C.arena.reset(m0)
        stB = phase_B_fox(P, C, gqr_d[0], gkr_d[0], scr, deps=stA)
        P.barrier()
        C.arena.reset(m0)
        stC = phase_C(P, C, scr.yT, wo_d, part, 4, deps=stB)
        P.emit(final_waits=stC)
    return nc


def phase_A_ret(P, C, xT_d, w_d, cos_d, sin_d, scr, deps=(), xdeps=None, after_w=None):
    nc, A, K, PS = C.nc, C.arena, C.k, C.ps
    m0 = A.mark()
    stores = []
    Wb = [A.alloc([128, 16, 1536], BF16) for _ in range(2)]
    xt = Rot([A.alloc([128, 16, 512], BF16) for _ in range(2)])
    cs = Rot([A.alloc([128, 2, 512], F32) for _ in range(2)])
    twb = []
    for hp_ in range(2):
        tw_ = []
        for s_ in range(3):
            tw_.append(load_w_bf16(P, "wR%d_%d" % (hp_, s_), Wb[hp_][:, :, s_ * 512:(s_ + 1) * 512],
                                   w_d[:, hp_ * 1536 + s_ * 512:hp_ * 1536 + (s_ + 1) * 512], list(deps)))
        twb.append(tw_)
    if after_w is not None:
        C.t_wo = after_w()
    tmp = Rot([A.alloc([128, 2, 512], F32) for _ in range(2)])
    st_r = Rot([A.alloc([128, 512], BF16) for _ in range(4)])
    st_g = Rot([A.alloc([128, 512], BF16) for _ in range(2)])
    st_v = Rot([A.alloc([128, 4, 512], BF16) for _ in range(2)])
    pm = Rot([PS[i] for i in range(6)])
    w_free = None
    for hp in range(2):
        tw = twb[hp]
        W = Wb[hp]
        for tt_i in range(8):
            r, half = tt_i // 2, tt_i % 2
            xk = xt.next()
            X = xt.bufs[xk]
            t_xt = dma(P, "sp", "xt%d" % xk, X,
                       xT_d[r][:, half * 512:(half + 1) * 512].rearrange("(c p) t -> p c t", p=128),
                       list(deps) + [xt.free[xk]] + ([xdeps[r]] if xdeps else []))
            ci = cs.next()
            CS = cs.bufs[ci]
            t_cs = dma(P, "sp", "cs%d" % ci, CS[:, 0, :], cos_d[:, tt_i * 512:(tt_i + 1) * 512], list(deps) + [cs.free[ci]])
            t_cs = dma(P, "sp", "cs%d" % ci, CS[:, 1, :], sin_d[:, tt_i * 512:(tt_i + 1) * 512], list(deps) + [cs.free[ci]])
            t_last_cs = None
            for kind in ("q", "k"):
                col0 = 0 if kind == "q" else 256
                scale = 1.0 if kind == "q" else 1.0 / 16.0
                banks = []
                t_m = None
                for hf in range(2):
                    b = pm.next()
                    banks.append(b)
                    for c in range(16):
                        t_m = mm(P, pm.bufs[b], W[:, c, col0 + hf * 128:col0 + (hf + 1) * 128], X[:, c, :],
                                 c == 0, c == 15, [t_xt, tw[0], pm.free[b]])
                pa, pb = pm.bufs[banks[0]], pm.bufs[banks[1]]
                dst = scr.qT if kind == "q" else scr.kT
                for oi, (ca, cb, op) in enumerate(((0, 1, ALU.subtract), (1, 0, ALU.add))):
                    ti = tmp.next()
                    T = tmp.bufs[ti]
                    t1 = stt(P, T[:, 0, :], pa, scale, CS[:, ca, :], ALU.mult, ALU.mult, [t_m, t_cs, tmp.free[ti]])
                    t2 = stt(P, T[:, 1, :], pb, scale, CS[:, cb, :], ALU.mult, ALU.mult, [t_m, t_cs])
                    si = st_r.next()
                    t3 = tt(P, "pool", st_r.bufs[si], T[:, 0, :], T[:, 1, :], op, [t1, t2, st_r.free[si]])
                    tmp.free[ti] = t3
                    t_st = dma(P, "sp", "str%d" % si, dst[hp, oi, :, tt_i * 512:(tt_i + 1) * 512], st_r.bufs[si], [t3])
                    st_r.free[si] = t_st
                    stores.append(t_st)
                pm.free[banks[0]] = t2
                pm.free[banks[1]] = t2
                t_last_cs = t2
            cs.free[ci] = t_last_cs
            for gt in range(4):
                b = pm.next()
                t_m = None
                for c in range(16):
                    t_m = mm(P, pm.bufs[b], W[:, c, 1024 + gt * 128:1024 + (gt + 1) * 128], X[:, c, :],
                             c == 0, c == 15, [t_xt, tw[2], pm.free[b]])
                sb = st_g.next()
                t_e = act(P, st_g.bufs[sb], pm.bufs[b], AF.Silu, [t_m, st_g.free[sb]])
                pm.free[b] = t_e
                t_st = dma(P, "sp", "stg%d" % sb, scr.sgT[hp, gt, :, tt_i * 512:(tt_i + 1) * 512], st_g.bufs[sb], [t_e])
                st_g.free[sb] = t_st
                stores.append(t_st)
            vi = st_v.next()
            VS = st_v.bufs[vi]
            t_ev = None
            for jj in range(4):
                b = pm.next()
                t_m = None
                for c in range(16):
                    t_m = mm(P, pm.bufs[b], X[:, c, jj * 128:(jj + 1) * 128], W[:, c, 512:1024],
                             c == 0, c == 15, [t_xt, tw[1], pm.free[b]])
                t_ev = cp(P, "act", VS[:, jj, :], pm.bufs[b], [t_m, st_v.free[vi] if jj == 0 else None])
                pm.free[b] = t_ev
            xt.free[xk] = t_m
            w_free = t_m
            t_st = dma(P, "sp", "stv%d" % vi,
                       scr.v[hp, tt_i * 512:(tt_i + 1) * 512, :].rearrange("(j p) d -> p j d", p=128), VS, [t_ev])
            st_v.free[vi] = t_st
            stores.append(t_st)
    A.reset(m0)
    return stores


def phase_B_ret(P, C, dec_d, scr, deps=()):
    nc, A, K, PS = C.nc, C.arena, C.k, C.ps
    m0 = A.mark()
    stores = []
    hstores = []
    HB = []
    for hp in range(2):
        h_ = Ctx()
        h_.Q = A.alloc([128, 2, S], BF16)
        h_.Kt = A.alloc([128, 2, S], BF16)
        h_.dec = A.alloc([128, 258], F32)
        HB.append(h_)
    SGr = Rot([A.alloc([128, 4, 1024], BF16) for _ in range(3)])
    Qdr = Rot([A.alloc([128, 2, 1024], BF16) for _ in range(3)])
    Vr = Rot([A.alloc([128, 8, 512], BF16) for _ in range(3)])
    Rf = A.alloc([128, 2, 512], F32)
    Rb = A.alloc([128, 2, 512], BF16)
    iT = Rot([A.alloc([128, 128], BF16) for _ in range(2)])
    Kd = Rot([A.alloc([128, 256], BF16) for _ in range(2)])
    on = Rot([A.alloc([128, 512], BF16) for _ in range(4)])
    yst = Rot([A.alloc([128, 4, 4, 128], BF16) for _ in range(2)])
    st6 = A.alloc([128, 6], F32)
    mv = A.alloc([128, 2], F32)
    rstd = A.alloc([128, 1], F32)
    nmr = A.alloc([128, 1], F32)
    mvA = [A.alloc([128, 2], F32) for _ in range(4)]
    nmA = [A.alloc([128, 1], F32) for _ in range(4)]
    mv_free = [None] * 4
    pI, pK = PS[0], PS[1].bitcast(BF16)
    pO = [PS[2], PS[3]]
    pTt = PS[4].bitcast(BF16)
    pR = [PS[5], PS[6]]
    free = Ctx()
    free.pI = free.pK = free.pT = None
    free.pO = [None, None]
    free.pR = [None, None]
    head_free = None
    dd = list(deps)
    sg_tok = {}
    sg_buf = {}
    sg_order = [(hp_, q_) for hp_ in range(2) for q_ in range(4)]

    def load_sg(idx):
        hp_, q_ = sg_order[idx]
        k = SGr.next()
        sl_ = slice(q_ * 1024, (q_ + 1) * 1024)
        sg_buf[(hp_, q_)] = (k, SGr.bufs[k])
        sg_tok[(hp_, q_)] = dma(P, "sp", "rsg%d" % k, SGr.bufs[k],
                                scr.sgT[hp_, :, :, sl_].rearrange("g p t -> p g t"), dd + [SGr.free[k]])

    v_tok = {}
    v_buf = {}

    def load_v(idx):
        hp_, q_ = sg_order[idx]
        k = Vr.next()
        sl_ = slice(q_ * 1024, (q_ + 1) * 1024)
        v_buf[(hp_, q_)] = (k, Vr.bufs[k])
        v_tok[(hp_, q_)] = dma(P, "sp", "rv%d" % k, Vr.bufs[k],
                               scr.v[hp_, sl_, :].rearrange("(j p) d -> p j d", p=128), dd + [Vr.free[k]])

    v_next = [3]
    for hp in range(2):
        H = HB[hp]
        H.t_dec = dma(P, "sp", "dec%d" % hp, H.dec, dec_d[hp], dd)
        H.tq, H.tqd = [], []
        for q4 in range(4):
            sl = slice(q4 * 1024, (q4 + 1) * 1024)
            key = "rl%d_%d" % (hp, q4)
            t = dma(P, "sp", key, H.Q[:, :, sl], scr.qT[hp, :, :, sl].rearrange("h p t -> p h t"), dd)
            t = dma(P, "sp", key, H.Kt[:, :, sl], scr.kT[hp, :, :, sl].rearrange("h p t -> p h t"), dd)
            H.tq.append(t)
            if hp == 0 and q4 < 3:
                load_sg(q4)
                load_v(q4)
    sg_next = [3]
    qd_tok = {}
    qd_buf = {}

    def make_qd(idx):
        hp_, q_ = sg_order[idx]
        H_ = HB[hp_]
        k = Qdr.next()
        sl_ = slice(q_ * 1024, (q_ + 1) * 1024)
        t2 = None
        for hf in range(2):
            t2 = tt(P, "pool", Qdr.bufs[k][:, hf, :].rearrange("p (n q) -> p n q", q=128),
                    H_.Q[:, hf, sl_].rearrange("p (n q) -> p n q", q=128),
                    H_.dec[:, 128:256].unsqueeze(1).to_broadcast([128, 8, 128]), ALU.mult,
                    [H_.tq[q_], H_.t_dec, Qdr.free[k]])
        qd_buf[(hp_, q_)] = (k, Qdr.bufs[k])
        qd_tok[(hp_, q_)] = t2

    for i_ in range(3):
        make_qd(i_)
    qd_next = [3]
    for hp in range(2):
        H = HB[hp]
        Q, Kt, dec = H.Q, H.Kt, H.dec
        tq, t_dec = H.tq, H.t_dec
        DT = dec[:, 0:128]
        kd = dec[:, 256:257]
        cd = dec[:, 257:258]
        t_r0 = P.op("pool", lambda e: e.memset(Rf, 0.0), [head_free])
        t_rb = None
        t_rf = t_r0

        def inner(n):
            q4 = n // 8
            c0 = n * 128
            t_i = None
            for hf in range(2):
                t_i = mm(P, pI[:, 0:128], Kt[:, hf, c0:c0 + 128], Q[:, hf, c0:c0 + 128], hf == 0, hf == 1,
                         [tq[q4], free.pI])
            ii = iT.next()
            t_it = tt(P, "dve", iT.bufs[ii], pI[:, 0:128], DT, ALU.mult, [t_i, t_dec, iT.free[ii]])
            free.pI = t_it
            t_k = None
            for hf in range(2):
                t_k = tr(P, pK[:, hf * 128:(hf + 1) * 128], Kt[:, hf, c0:c0 + 128], K.ident_bf, [K.tok, free.pK])
            ki = Kd.next()
            t_kd = ts(P, "dve", Kd.bufs[ki], pK[:, 0:256], kd, None, ALU.mult, None, [t_k, Kd.free[ki]])
            free.pK = t_kd
            return (ii, t_it, ki, t_kd)

        nxt = inner(0)
        pend = []

        def finish(n, oi, t_n):
            nonlocal yi_cur
            c0 = n * 128
            q4 = n // 8
            t_t = None
            for vc in range(4):
                t_t = tr(P, pTt[:, vc * 128:(vc + 1) * 128], on.bufs[oi][:, vc * 128:(vc + 1) * 128], K.ident_bf,
                         [t_n, free.pT])
            on.free[oi] = t_t
            nn = n % 4
            if nn == 0:
                yi_cur = yst.next()
            yi = yi_cur
            Y = yst.bufs[yi]
            sgk, SGq = sg_buf[(hp, q4)]
            cq = c0 - q4 * 1024
            t_y = tt(P, "dve", Y[:, :, nn, :], pTt[:, 0:512].rearrange("p (v q) -> p v q", q=128),
                     SGq[:, :, cq:cq + 128], ALU.mult, [t_t, sg_tok[(hp, q4)], yst.free[yi] if nn == 0 else None])
            free.pT = t_y
            if n % 8 == 7:
                SGr.free[sgk] = t_y
                if sg_next[0] < 8:
                    load_sg(sg_next[0])
                    sg_next[0] += 1
            if nn == 3:
                t0 = (n - 3) * 128
                qq, toff = t0 // 1024, t0 % 1024
                dstp = scr.yl[hp][qq].rearrange("(v p) t -> p v t", p=128)[:, :, toff:toff + 512]
                t_st = dma(P, "sp", "sty%d" % yi, dstp.rearrange("p v (n q) -> p v n q", q=128), Y, [t_y])
                yst.free[yi] = t_st
                hstores.append(t_st)
                C.last_stores.append(t_st)
                if toff == 512:
                    if scr.yg is not None:
                        stores.append(coll(P, "ccy", "AllGather", ALU.bypass, GROUPS, scr.yl[hp][qq], scr.yg[hp][qq],
                                           list(hstores)))
                    else:
                        stores.extend(hstores)
                    del hstores[:]
            return t_y

        yi_cur = 0
        t_y = None
        for n in range(32):
            q4 = n // 8
            c0 = n * 128
            ii, t_it, ki, t_kd = nxt
            ob = n % 2
            vk, Vq = v_buf[(hp, q4)]
            t_o = mm(P, pO[ob], iT.bufs[ii], Vq[:, n % 8, :], True, n == 0, [t_it, free.pO[ob], v_tok[(hp, q4)]])
            iT.free[ii] = t_o
            t_u = None
            if n < 31:
                for hf in range(2):
                    t_u = mm(P, pR[hf], Kd.bufs[ki][:, hf * 128:(hf + 1) * 128], Vq[:, n % 8, :], True, True,
                             [t_kd, free.pR[hf]])
                Kd.free[ki] = t_u
                nxt = inner(n + 1)
            if n % 8 == 7:
                Vr.free[vk] = t_u if n < 31 else t_o
                if v_next[0] < 8:
                    load_v(v_next[0])
                    v_next[0] += 1
            if len(pend) == 2:
                t_y = finish(*pend.pop(0))
            qk, Qdq = qd_buf[(hp, q4)]
            cq = c0 - q4 * 1024
            if n > 0:
                for hf in range(2):
                    t_o = mm(P, pO[ob], Qdq[:, hf, cq:cq + 128], Rb[:, hf, :], False, hf == 1,
                             [qd_tok[(hp, q4)], t_rb])
            t_ocross = t_o
            if n % 8 == 7:
                Qdr.free[qk] = t_ocross
                if qd_next[0] < 8:
                    make_qd(qd_next[0])
                    qd_next[0] += 1
            if n < 31:
                for hf in range(2):
                    t_rf = stt(P, Rf[:, hf, :], Rf[:, hf, :], cd, pR[hf], ALU.mult, ALU.add, [t_u, t_rf, t_dec])
                    free.pR[hf] = t_rf
                t_rb = cp(P, "act", Rb, Rf, [t_rf, t_ocross])
            sl = n % 4
            mv_, nm_ = mvA[sl], nmA[sl]
            t_s = bnstats(P, st6, pO[ob], [t_ocross])
            t_s = bnaggr(P, mv_, st6, [t_s, mv_free[sl]])
            t_s = ts(P, "dve", nm_, mv_[:, 0:1], -1.0, None, ALU.mult, None, [t_s])
            t_a = act(P, rstd, mv_[:, 1:2], AF.Ln, [t_s, C.t_eps], bias=C.eps_gn)
            t_a = act(P, rstd, rstd, AF.Exp, [t_a], scale=-0.5)
            t_a = act(P, nmr, nm_, AF.Identity, [t_a], scale=rstd)
            mv_free[sl] = t_a
            oi = on.next()
            t_n = act(P, on.bufs[oi], pO[ob], AF.Identity, [t_a, on.free[oi]], bias=nmr, scale=rstd)
            free.pO[ob] = t_n
            pend.append((n, oi, t_n))
        while pend:
            t_y = finish(*pend.pop(0))
        head_free = t_y
    A.reset(m0)
    return stores


def build_L_ret():
    nc = new_prog()
    xT_d = nc.dram_tensor("xT", [4, D, TOKC], BF16, kind="ExternalInput").ap()
    w_d = nc.dram_tensor("w_in", [D, 3072], F32, kind="ExternalInput").ap()
    cos_d = nc.dram_tensor("cos", [128, S], F32, kind="ExternalInput").ap()
    sin_d = nc.dram_tensor("sin", [128, S], F32, kind="ExternalInput").ap()
    dec_d = nc.dram_tensor("dec", [2, 128, 258], F32, kind="ExternalInput").ap()
    wo_d = nc.dram_tensor("w_out", [1024, D], F32, kind="ExternalInput").ap()
    part = nc.dram_tensor("part", [S, D], F32, kind="ExternalOutput").ap()
    scr = Scr()
    scr.qT = nc.dram_tensor("s_qT", [2, 2, 128, S], BF16).ap()
    scr.kT = nc.dram_tensor("s_kT", [2, 2, 128, S], BF16).ap()
    scr.v = nc.dram_tensor("s_v", [2, S, 512], BF16).ap()
    scr.sgT = nc.dram_tensor("s_sgT", [2, 4, 128, S], BF16).ap()
    scr.yT = nc.dram_tensor("s_yT", [8, 128, S], BF16).ap()
    with contextlib.ExitStack() as st:
        C = setup(nc, st)
        P = Prog(nc)
        load_consts(P, C)
        stA = phase_A_ret(P, C, xT_d, w_d, cos_d, sin_d, scr)
        P.barrier()
        stB = phase_B_ret(P, C, dec_d, scr, deps=stA)
        P.barrier()
        stC = phase_C(P, C, scr.yT, wo_d, part, 8, deps=stB)
        P.emit(final_waits=stC)
    return nc


def rotary_tables():
    inv_freq = (10000.0 ** (-np.arange(0, 256, 2, dtype=np.float32) / np.float32(256))).astype(np.float32)
    ang = np.arange(S, dtype=np.float32)[:, None] * inv_freq[None, :]
    return np.ascontiguousarray(np.cos(ang).T.astype(np.float32)), np.ascontiguousarray(np.sin(ang).T.astype(np.float32))


def decay_tables(head):
    lg = np.log1p(-np.exp2(np.float32(-5.0 - head))).astype(np.float32)
    pos = np.arange(128, dtype=np.float32)
    diff = pos[:, None] - pos[None, :]
    intra = np.where(diff >= 0, np.exp(np.maximum(diff, 0.0) * lg), 0.0).astype(np.float32)
    qd = np.exp((pos + 1.0) * lg).astype(np.float32)
    kd = np.exp((128 - 1.0 - pos) * lg).astype(np.float32)
    cdv = np.exp(np.float32(128) * lg).astype(np.float32)
    out = np.zeros((128, 258), np.float32)
    out[:, 0:128] = intra.T
    out[:, 128:256] = qd[None, :]
    out[:, 256] = kd
    out[:, 257] = cdv
    return out


CORES = list(range(8))
_PROGS = {}


def _prog(name, fn):
    if name not in _PROGS:
        _PROGS[name] = fn()
    return _PROGS[name]


def _run(nc, maps):
    return run_bass_kernel_spmd(nc, maps, core_ids=CORES).results


def _ca(a):
    return np.ascontiguousarray(a)


def fox_maps(xTf, w, bfv, gq, gk, wo):
    maps = []
    for c in CORES:
        b, g = c // 4, c % 4
        wl = np.concatenate([w[:, 512 * g:512 * g + 512], w[:, 2048 + 512 * g:2048 + 512 * g + 512],
                             w[:, 4096 + 512 * g:4096 + 512 * g + 512], w[:, 6144 + 512 * g:6144 + 512 * g + 512],
                             w[:, 8192 + 4 * g:8192 + 4 * g + 4]], axis=1)
        maps.append({"xT": xTf[b], "w_in": _ca(wl), "bf": _ca(bfv[4 * g:4 * g + 4].reshape(4, 1)),
                     "gq": _ca(gq.reshape(128, 1)), "gk": _ca(gk.reshape(128, 1)),
                     "gqr": _ca(gq.reshape(1, 128)), "gkr": _ca(gk.reshape(1, 128)),
                     "w_out": _ca(wo[512 * g:512 * g + 512, :])})
    return maps


def ret_wcols(w, g):
    cols = []
    for hp in range(2):
        h = 2 * g + hp
        cols += [w[:, h * 256:(h + 1) * 256], w[:, 2048 + h * 256:2048 + (h + 1) * 256],
                 w[:, 4096 + h * 512:4096 + (h + 1) * 512], w[:, 8192 + h * 512:8192 + (h + 1) * 512]]
    return _ca(np.concatenate(cols, 1))


def ret_maps(xTf, w, wo, cosT, sinT):
    maps = []
    for c in CORES:
        b, g = c // 4, c % 4
        maps.append({"xT": xTf[b], "w_in": ret_wcols(w, g), "cos": cosT, "sin": sinT,
                     "dec": _ca(np.stack([decay_tables(2 * g), decay_tables(2 * g + 1)], 0)),
                     "w_out": _ca(wo[1024 * g:1024 * (g + 1), :])})
    return maps


def kernel_unfused(x, fox_w_in, fox_b_f, fox_q_gain, fox_k_gain, fox_w_out, ret_w_in, ret_w_out,
                   ln_gain, ln_bias):
    x = np.asarray(x, np.float32)
    xs = [_ca(x[c // 4, (c % 4) * 1024:(c % 4 + 1) * 1024, :]) for c in CORES]
    cosT, sinT = rotary_tables()
    r = _run(_prog("D0", lambda: build_D(True, False)), [{"x_in": xs[c]} for c in CORES])
    xT = [r[c]["xT_out"] for c in CORES]
    for i in range(DEPTH):
        j = i // 2
        xTf = [_ca(np.stack(xT[4 * b:4 * b + 4], 0)) for b in range(2)]
        if i % 2 == 0:
            maps = fox_maps(xTf, np.asarray(fox_w_in[j]), np.asarray(fox_b_f[j]), np.asarray(fox_q_gain[j]),
                            np.asarray(fox_k_gain[j]), np.asarray(fox_w_out[j]))
            r = _run(_prog("Lfox", build_L_fox), maps)
        else:
            maps = ret_maps(xTf, np.asarray(ret_w_in[j]), np.asarray(ret_w_out[j]), cosT, sinT)
            r = _run(_prog("Lret", build_L_ret), maps)
        parts = [r[c]["part"] for c in CORES]
        last = (i == DEPTH - 1)
        maps = []
        for c in CORES:
            b, g = c // 4, c % 4
            pp = _ca(np.stack([parts[4 * b + q][g * 1024:(g + 1) * 1024] for q in range(4)], 0))
            maps.append({"x_in": xs[c], "parts": pp, "lng": _ca(np.asarray(ln_gain)[i:i + 1]),
                         "lnb": _ca(np.asarray(ln_bias)[i:i + 1])})
        r = _run(_prog("Dlast" if last else "D", (lambda: build_D(False, True)) if last else (lambda: build_D(False, False))), maps)
        xs = [r[c]["x_out"] for c in CORES]
        if not last:
            xT = [r[c]["xT_out"] for c in CORES]
    out = np.zeros((NB, S, D), np.float32)
    for c in CORES:
        out[c // 4, (c % 4) * 1024:(c % 4 + 1) * 1024, :] = xs[c]
    return out


def fused_maps(x, fox_w_in, fox_b_f, fox_q_gain, fox_k_gain, fox_w_out, ret_w_in, ret_w_out, ln_gain, ln_bias):
    cosT, sinT = rotary_tables()
    maps = []
    for c in CORES:
        b, g = c // 4, c % 4
        dsl = slice(g * DL, (g + 1) * DL)
        m = {"x_in": _ca(x[b][:, dsl]), "cos": cosT, "sin": sinT,
             "dec": _ca(np.stack([decay_tables(2 * g), decay_tables(2 * g + 1)], 0)),
             "lng": _ca(ln_gain[:, dsl]), "lnb": _ca(ln_bias[:, dsl])}
        for j in range(2):
            w = fox_w_in[j]
            m["fw_in%d" % j] = _ca(np.concatenate(
                [w[:, 512 * g:512 * g + 512], w[:, 2048 + 512 * g:2048 + 512 * g + 512],
                 w[:, 4096 + 512 * g:4096 + 512 * g + 512], w[:, 6144 + 512 * g:6144 + 512 * g + 512],
                 w[:, 8192 + 4 * g:8192 + 4 * g + 4]], axis=1))
            m["fbf%d" % j] = _ca(fox_b_f[j][4 * g:4 * g + 4].reshape(4, 1))
            m["fgq%d" % j] = _ca(fox_q_gain[j].reshape(128, 1))
            m["fgk%d" % j] = _ca(fox_k_gain[j].reshape(128, 1))
            m["fgqr%d" % j] = _ca(fox_q_gain[j].reshape(1, 128))
            m["fgkr%d" % j] = _ca(fox_k_gain[j].reshape(1, 128))
            m["fwo%d" % j] = _ca(fox_w_out[j][:, dsl])
            m["rw_in%d" % j] = ret_wcols(ret_w_in[j], g)
            m["rwo%d" % j] = _ca(ret_w_out[j][:, dsl])
        maps.append(m)
    return maps


def kernel(**inputs):
    inp = {k: np.asarray(v, np.float32) for k, v in inputs.items()}
    maps = fused_maps(**inp)
    nc = _prog("fused", build_fused)
    r = _run(nc, maps)
    out = np.zeros((NB, S, D), np.float32)
    for c in CORES:
        out[c // 4][:, (c % 4) * DL:(c % 4 + 1) * DL] = r[c]["out"]
    return out


GROUPS = [[0, 1, 2, 3], [4, 5, 6, 7]]
DL = 512


def phase_E(P, C, xres, mode, ysrc, ec, wo_d, lng_d, lnb_d, x_in, xs_l, xg_l, st_l, st_g, x_out, deps=(), ydeps=(), pre_wo=None):
    nc, A, K, PS = C.nc, C.arena, C.k, C.ps
    m0 = A.mark()
    outs = []
    xres = A.alloc([128, 32, DL], F32)
    t_x = dma(P, "sp", "ex", xres, x_in.rearrange("(j p) d -> p j d", p=128), deps)
    C.t_xres = t_x
    if mode != "init":
        if pre_wo is None:
            Wo = A.alloc([128, ec, DL], BF16)
            t_w = load_w_bf16(P, "wE", Wo, wo_d, deps)
        else:
            Wo, t_w = pre_wo
        gb = A.alloc([128, DL], F32)
        bb = A.alloc([128, DL], F32)
        t_gb = dma(P, "sp", "egb", gb, lng_d.partition_broadcast(128), deps)
        t_gb = dma(P, "sp", "egb", bb, lnb_d.partition_broadcast(128), deps)
        Y = Rot([A.alloc([128, ec, 512], BF16) for _ in range(2)])
        S12 = [A.alloc([128, 16], F32) for _ in range(2)]
        SG_ = [A.alloc([128, 4, 16], F32) for _ in range(2)]
        ssum = A.alloc([128, 16], F32)
        mean = A.alloc([128, 8], F32)
        var = A.alloc([128, 8], F32)
        rstd = [A.alloc([128, 8], F32) for _ in range(2)]
        nmr = [A.alloc([128, 8], F32) for _ in range(2)]
        junk = A.alloc([128, DL], F32)
        tmpn = Rot([A.alloc([128, DL], F32) for _ in range(3)])
    xTs = Rot([A.alloc([128, 4, 1024], BF16) for _ in range(2)])
    pm = Rot([PS[0], PS[1], PS[2], PS[3]])
    pt = Rot([PS[4], PS[5]])
    st_free = [None, None]
    t_stats = [None] * 4
    t_h = [None] * 32

    def out_proj(q):
        for s2 in range(2):
            q2 = q * 2 + s2
            yi = Y.next()
            t_y = C.yload(P, "ey%d" % yi, q2, Y.bufs[yi], list(deps) + C.ydeps(q2) + [Y.free[yi]])
            for jj in range(4):
                j = q2 * 4 + jj
                b = pm.next()
                t_m = None
                for e_ in range(ec):
                    t_m = mm(P, pm.bufs[b], Y.bufs[yi][:, e_, jj * 128:(jj + 1) * 128], Wo[:, e_, :],
                             e_ == 0, e_ == ec - 1, [t_w, t_y, pm.free[b]])
                t_hh = stt(P, xres[:, j, :], xres[:, j, :], ALPHA, pm.bufs[b], ALU.mult, ALU.add, [t_m, C.t_xres])
                pm.free[b] = t_hh
                jq = j % 8
                sb = S12[q % 2]
                t_a = P.op("act", (lambda o, i, a: lambda e: e.activation(out=o, in_=i, func=AF.Identity, accum_out=a))(
                    junk, xres[:, j, :], sb[:, jq:jq + 1]), [t_hh, st_free[q % 2] if jq == 0 else None])
                t_a = P.op("act", (lambda o, i, a: lambda e: e.activation(out=o, in_=i, func=AF.Square, accum_out=a))(
                    junk, xres[:, j, :], sb[:, 8 + jq:9 + jq]), [t_a])
                t_h[j] = t_a
            Y.free[yi] = t_m
        t1 = dma(P, "sp", "est%d" % (q % 2), st_l[q], S12[q % 2], [t_h[q * 8 + 7]])
        st_free[q % 2] = t1
        t2 = coll(P, "ccst", "AllGather", ALU.bypass, GROUPS, st_l[q], st_g[q], [t1])
        t3 = dma(P, "sp", "esg%d" % (q % 2), SG_[q % 2], st_g[q].rearrange("(r p) s -> p r s", p=128), [t2])
        t_stats[q] = t3

    def normalize(q):
        if mode != "init":
            G_ = SG_[q % 2]
            t_s = tt(P, "dve", ssum, G_[:, 0, :], G_[:, 1, :], ALU.add, [t_stats[q]])
            t_s = tt(P, "dve", ssum, ssum, G_[:, 2, :], ALU.add, [t_s])
            t_s = tt(P, "dve", ssum, ssum, G_[:, 3, :], ALU.add, [t_s])
            t_s = ts(P, "dve", mean, ssum[:, 0:8], 1.0 / D, None, ALU.mult, None, [t_s])
            t_s = tt(P, "dve", var, mean, mean, ALU.mult, [t_s])
            t_s = stt(P, var, ssum[:, 8:16], 1.0 / D, var, ALU.mult, ALU.subtract, [t_s])
            t_s = act(P, rstd[q % 2], var, AF.Sqrt, [t_s, C.t_eps], bias=C.eps_ln)
            t_s = P.op("dve", (lambda o: lambda e: e.reciprocal(out=o, in_=o))(rstd[q % 2]), [t_s])
            t_s = stt(P, nmr[q % 2], mean, -1.0, rstd[q % 2], ALU.mult, ALU.mult, [t_s])
        xi = xTs.next() if mode != "last" else None
        t_c = None
        t_fin = None
        st1 = {}

        def stage1(jq):
            j = q * 8 + jq
            if mode != "init":
                ti = tmpn.next()
                T = tmpn.bufs[ti]
                t_n = act(P, T, xres[:, j, :], AF.Identity, [t_s, tmpn.free[ti]],
                          bias=nmr[q % 2][:, jq:jq + 1], scale=rstd[q % 2][:, jq:jq + 1])
                t_n = tt(P, "dve", T, T, gb, ALU.mult, [t_n, t_gb])
                t_n = tt(P, "dve", xres[:, j, :], T, bb, ALU.add, [t_n])
                tmpn.free[ti] = t_n
                C.t_xres = t_n
                st1[jq] = t_n
            else:
                st1[jq] = t_x

        def stage2(jq):
            j = q * 8 + jq
            pi = pt.next()
            pb = pt.bufs[pi].rearrange("p (a b) -> p a b", b=128)
            t_t = None
            for c4 in range(4):
                t_t = tr(P, pb[:, c4, :], xres[:, j, c4 * 128:(c4 + 1) * 128], K.ident_f,
                         [st1[jq], K.tok, pt.free[pi]])
            t_cc = cp(P, "dve", xTs.bufs[xi][:, :, jq * 128:(jq + 1) * 128], pb,
                      [t_t, xTs.free[xi] if jq == 0 else None])
            pt.free[pi] = t_cc
            return t_cc

        for jq in range(8 + 2):
            if jq < 8:
                stage1(jq)
                t_fin = st1[jq]
            if mode != "last" and jq >= 2:
                t_c = stage2(jq - 2)
        if mode != "last":
            t1 = dma(P, "sp", "exs%d" % xi, xs_l[q].rearrange("(c p) t -> p c t", p=128), xTs.bufs[xi], [t_c])
            xTs.free[xi] = t1
            C.e_stores.append(t1)
            outs.append(coll(P, "ccx", "AllGather", ALU.bypass, GROUPS, xs_l[q], xg_l[q], [t1]))
            if mode == "mid":
                C.x_spill.append(dma(P, "sp", "espill", C.xsp[q * 1024:(q + 1) * 1024, :].rearrange("(j p) d -> p j d", p=128),
                                     xres[:, q * 8:(q + 1) * 8, :], [t_fin]))
        else:
            outs.append(dma(P, "sp", "eout", x_out[q * 1024:(q + 1) * 1024, :].rearrange("(j p) d -> p j d", p=128),
                            xres[:, q * 8:(q + 1) * 8, :], [t_fin]))

    if mode == "init":
        for q in range(4):
            normalize(q)
    else:
        out_proj(0)
        for q in range(1, 4):
            out_proj(q)
            normalize(q - 1)
        normalize(3)
    A.reset(m0)
    return outs


def build_fused(nlayers=DEPTH):
    nc = new_prog()
    def ext(name, shape, dt=F32):
        return nc.dram_tensor(name, shape, dt, kind="ExternalInput").ap()
    def scratch(name, shape, dt=BF16):
        return nc.dram_tensor(name, shape, dt).ap()
    x_in = ext("x_in", [S, DL])
    fox = []
    for j in range(2):
        f = Ctx()
        f.w = ext("fw_in%d" % j, [D, 2052]); f.bf = ext("fbf%d" % j, [4, 1])
        f.gq = ext("fgq%d" % j, [128, 1]); f.gk = ext("fgk%d" % j, [128, 1])
        f.gqr = ext("fgqr%d" % j, [1, 128]); f.gkr = ext("fgkr%d" % j, [1, 128])
        f.wo = ext("fwo%d" % j, [D, DL])
        fox.append(f)
    ret = []
    for j in range(2):
        r_ = Ctx()
        r_.w = ext("rw_in%d" % j, [D, 3072]); r_.wo = ext("rwo%d" % j, [2 * D, DL])
        ret.append(r_)
    cos_d = ext("cos", [128, S]); sin_d = ext("sin", [128, S]); dec_d = ext("dec", [2, 128, 258])
    lng = ext("lng", [DEPTH, DL]); lnb = ext("lnb", [DEPTH, DL])
    out = nc.dram_tensor("out", [S, DL], F32, kind="ExternalOutput").ap()
    fs = Scr()
    fs.qT = scratch("f_qT", [4, 128, S]); fs.kT = scratch("f_kT", [4, 128, S])
    fs.v = scratch("f_v", [4, 128, 32, 128]); fs.sgT = scratch("f_sgT", [4, 128, S])
    fs.yl = [scratch("f_yl%d" % h, [128, S]) for h in range(4)]
    fs.yg = [scratch("f_yg%d" % h, [512, S]) for h in range(4)]
    rs = Scr()
    rs.qT = scratch("r_qT", [2, 2, 128, S]); rs.kT = scratch("r_kT", [2, 2, 128, S])
    rs.v = scratch("r_v", [2, S, 512]); rs.sgT = scratch("r_sgT", [2, 4, 128, S])
    rs.yl = [[scratch("r_yl%d_%d" % (hp, q), [512, 1024]) for q in range(4)] for hp in range(2)]
    rs.yg = [[scratch("r_yg%d_%d" % (hp, q), [2048, 1024]) for q in range(4)] for hp in range(2)]
    xs_l = [scratch("xs%d" % q, [DL, 1024]) for q in range(4)]
    xg_l = [scratch("xg%d" % q, [D, 1024]) for q in range(4)]
    st_l = [scratch("stl%d" % q, [128, 16], F32) for q in range(4)]
    st_g = [scratch("stg%d" % q, [512, 16], F32) for q in range(4)]
    xsp = scratch("xsp", [S, DL], F32)

    def yload_fox(P, key, q2, Yb, deps):
        t = None
        Yv = Yb.rearrange("p (r h) t -> p r h t", h=4)
        for h in range(4):
            t = dma(P, "sp", key, Yv[:, :, h, :],
                    fs.yg[h][:, q2 * 512:(q2 + 1) * 512].rearrange("(r p) t -> p r t", p=128), deps)
        return t

    def yload_ret(P, key, q2, Yb, deps):
        t = None
        Yv = Yb.rearrange("p (r h v) t -> p r h v t", h=2, v=4)
        q, s2 = q2 // 2, q2 % 2
        for hp in range(2):
            for r in range(4):
                t = dma(P, "sp", key, Yv[:, r, hp, :, :],
                        rs.yg[hp][q][r * 512:(r + 1) * 512, s2 * 512:(s2 + 1) * 512].rearrange("(v p) t -> p v t", p=128),
                        deps)
        return t

    with contextlib.ExitStack() as st:
        C = setup(nc, st)
        P = Prog(nc)
        load_consts(P, C)
        C.xsp = xsp
        C.x_spill = []
        C.e_stores = []
        C.last_stores = []
        C.t_xres = None
        mtop = C.arena.mark()
        Wo_cur = C.arena.alloc([128, 16, DL], BF16)
        m_base = C.arena.mark()
        C.ncr = C.arena.alloc([4, S], F32)
        m_ncr = C.arena.mark()
        pre0 = preload_A_fox(P, C, fox[0].w)
        t_wo = load_w_bf16(P, "wE", Wo_cur, fox[0].wo)
        xg_tok = phase_E(P, C, None, "init", None, 0, None, None, None, x_in, xs_l, xg_l, st_l, st_g, None)
        P.barrier(extra=C.e_stores)
        C.e_stores = []
        outs = xg_tok
        for i in range(nlayers):
            j = i // 2
            last = (i == nlayers - 1)
            if i > 0:
                C.arena.reset(mtop)
                Wo_cur = C.arena.alloc([128, 16 if i % 2 == 0 else 32, DL], BF16)
                m_base = C.arena.mark()
            m0 = m_base
            if i % 2 == 0:
                f = fox[j]
                if i > 0:
                    C.ncr = C.arena.alloc([4, S], F32)
                    assert C.arena.mark() == m_ncr
                    pre_i = preload_A_fox(P, C, f.w)
                    t_wo = load_w_bf16(P, "wE", Wo_cur, f.wo)
                else:
                    pre_i = pre0
                m1 = m_ncr
                stA = phase_A_fox(P, C, xg_l, f.w, f.bf, f.gq, f.gk, fs, xdeps=xg_tok, pre=pre_i)
                P.barrier(extra=stA)
                C.arena.reset(m1)
                stB = phase_B_fox(P, C, f.gqr[0], f.gkr[0], fs, deps=stA)
                C.yload = yload_fox
                C.ydeps = (lambda toks: lambda q2: list(toks))(list(stB))
                ec, wo = 16, f.wo
            else:
                r_ = ret[j]
                stA = phase_A_ret(P, C, xg_l, r_.w, cos_d, sin_d, rs, xdeps=xg_tok,
                                  after_w=(lambda wo_=r_.wo, Wb_=Wo_cur: load_w_bf16(P, "wE", Wb_, wo_)))
                t_wo = C.t_wo
                P.barrier(extra=stA)
                stB = phase_B_ret(P, C, dec_d, rs, deps=stA)
                C.yload = yload_ret
                C.ydeps = (lambda toks: lambda q2: [toks[q2 // 2], toks[4 + q2 // 2]])(list(stB))
                ec, wo = 32, r_.wo
            assert len(stB) == (4 if i % 2 == 0 else 8)
            P.barrier(extra=C.last_stores)
            C.last_stores = []
            C.arena.reset(m0)
            spill = list(C.x_spill)
            C.x_spill = []
            outs = phase_E(P, C, None, "last" if last else "mid", None, ec, wo, lng[i], lnb[i],
                           x_in if i == 0 else xsp, xs_l, xg_l, st_l, st_g, out, deps=spill, pre_wo=(Wo_cur, t_wo))
            xg_tok = outs
            P.barrier(extra=list(C.x_spill) + list(C.e_stores))
            C.e_stores = []
        P.emit(final_waits=outs)
    return nc
```

```python
import contextlib
import numpy as np
import ml_dtypes
import concourse.bass as bass
import concourse.mybir as mybir
from concourse.bass_utils import run_bass_kernel_spmd

F32 = mybir.dt.float32
BF16 = mybir.dt.bfloat16
U8 = mybir.dt.uint8
AF = mybir.ActivationFunctionType
ALU = mybir.AluOpType
AX = mybir.AxisListType
NPBF = ml_dtypes.bfloat16

D = 2048
S = 4096
NB = 2
DEPTH = 4
TOKC = 1024
ALPHA = (2.0 * DEPTH) ** 0.25
LN_EPS = 1e-5
GN_EPS = 1e-6
QK_EPS = 1e-6
NEG = -30000.0

ENGS = ("pe", "act", "dve", "pool", "sp")


class Tok:
    __slots__ = ("eng", "idx", "sem", "val")

    def __init__(self, eng=None, idx=None, sem=None, val=None):
        self.eng, self.idx, self.sem, self.val = eng, idx, sem, val


class Prog:
    def __init__(self, nc):
        self.nc = nc
        self.ops = {e: [] for e in ENGS}
        self.dma_cnt = {}
        self.sig = {e: set() for e in ENGS}
        self.last = {e: None for e in ENGS}
        self.pending = {e: [] for e in ENGS}

    def _deps(self, eng, deps):
        deps = [d for d in deps if d is not None]
        if self.pending[eng]:
            deps = deps + self.pending[eng]
            self.pending[eng] = []
        for d in deps:
            if d.eng is not None and not (d.eng == "pe" and eng == "pe"):
                self.sig[d.eng].add(d.idx)
        return deps

    def op(self, eng, fn, deps=()):
        deps = self._deps(eng, deps)
        idx = len(self.ops[eng])
        self.ops[eng].append((fn, deps, None))
        t = Tok(eng=eng, idx=idx)
        self.last[eng] = t
        return t

    def dma(self, queue, semkey, fn, deps=(), inc=16):
        deps = self._deps(queue, deps)
        self.dma_cnt[semkey] = self.dma_cnt.get(semkey, 0) + inc
        self.ops[queue].append((fn, deps, (semkey, inc)))
        return Tok(sem=semkey, val=self.dma_cnt[semkey])

    def barrier(self, extra=()):
        best = {}
        for t in extra:
            if t is None:
                continue
            if t.sem is not None:
                if t.sem not in best or best[t.sem].val < t.val:
                    best[t.sem] = t
        toks = [t for t in self.last.values() if t is not None] + list(best.values())
        for e in ENGS:
            self.pending[e] = list(toks)

    def emit(self, final_waits=()):
        nc = self.nc
        with contextlib.ExitStack() as st:
            esem = {e: st.enter_context(nc.semaphore("s_" + e)) for e in ENGS}
            dsem = {k: st.enter_context(nc.semaphore("d_%s" % str(k))) for k in self.dma_cnt}
            sigval = {}
            for e in ENGS:
                c = 0
                for i in range(len(self.ops[e])):
                    if i in self.sig[e]:
                        c += 1
                        sigval[(e, i)] = c
            print("[prog] ops:", {e: len(self.ops[e]) for e in ENGS}, "signals:",
                  {e: len(self.sig[e]) for e in ENGS}, "dma sems:", len(self.dma_cnt),
                  "max dma cnt:", max(self.dma_cnt.values()) if self.dma_cnt else 0, flush=True)
            block = st.enter_context(nc.Block())

            def run(e, eng):
                waited = {}
                for i, (fn, deps, semkey) in enumerate(self.ops[e]):
                    for d in deps:
                        if d.eng is not None:
                            if d.eng == "pe" and e == "pe":
                                continue
                            key, v, s = ("e", d.eng), sigval[(d.eng, d.idx)], esem[d.eng]
                        else:
                            key, v, s = ("d", d.sem), d.val, dsem[d.sem]
                        if waited.get(key, 0) >= v:
                            continue
                        waited[key] = v
                        eng.wait_ge(s, v)
                    ins = fn(eng)
                    if semkey is not None:
                        ins.then_inc(dsem[semkey[0]], semkey[1])
                    elif (e, i) in sigval:
                        ins.then_inc(esem[e], 1)
                if e == "sp":
                    for t in final_waits:
                        eng.wait_ge(dsem[t.sem], t.val)

            block.tensor(lambda eng: run("pe", eng))
            block.scalar(lambda eng: run("act", eng))
            block.vector(lambda eng: run("dve", eng))
            block.gpsimd(lambda eng: run("pool", eng))
            block.sync(lambda eng: run("sp", eng))


DTSIZE = {F32: 4, BF16: 2, U8: 1}


class Arena:
    def __init__(self, ar, size):
        self.ar, self.size, self.off = ar, size, 0

    def mark(self):
        return self.off

    def reset(self, m):
        self.off = m

    def alloc(self, shape, dt):
        n = int(np.prod(shape[1:])) * DTSIZE[dt]
        off = (self.off + 63) // 64 * 64
        assert off + n <= self.size, ("SBUF arena overflow", off, n, self.size)
        self.off = off + n
        ap = self.ar[0:shape[0], off:off + n].bitcast(dt)
        dims = list(shape[1:])
        if len(dims) > 1:
            names = "abcd"[:len(dims)]
            pat = "p (%s) -> p %s" % (" ".join(names), " ".join(names))
            kw = {names[i]: dims[i] for i in range(1, len(dims))}
            ap = ap.rearrange(pat, **kw)
        return ap


class Ctx:
    pass


def mm(P, out, lhsT, rhs, start, stop, deps=()):
    return P.op("pe", lambda e: e.matmul(out, lhsT, rhs, start=start, stop=stop), deps)


def tr(P, out, in_, ident, deps=()):
    return P.op("pe", lambda e: e.transpose(out, in_, ident), deps)


def act(P, out, in_, func, deps=(), bias=None, scale=None):
    kw = {}
    if bias is not None:
        kw["bias"] = bias
    if scale is not None:
        kw["scale"] = scale
    return P.op("act", lambda e: e.activation(out=out, in_=in_, func=func, **kw), deps)


def tt(P, eng, out, in0, in1, op, deps=()):
    return P.op(eng, lambda e: e.tensor_tensor(out=out, in0=in0, in1=in1, op=op), deps)


def ts(P, eng, out, in0, s1, s2, op0, op1=None, deps=()):
    if op1 is None:
        return P.op(eng, lambda e: e.tensor_scalar(out=out, in0=in0, scalar1=s1, scalar2=None, op0=op0), deps)
    return P.op(eng, lambda e: e.tensor_scalar(out=out, in0=in0, scalar1=s1, scalar2=s2, op0=op0, op1=op1), deps)


def stt(P, out, in0, scalar, in1, op0, op1, deps=()):
    return P.op("dve", lambda e: e.scalar_tensor_tensor(out=out, in0=in0, scalar=scalar, in1=in1, op0=op0, op1=op1), deps)


def cp(P, eng, out, in_, deps=()):
    if eng == "act":
        return P.op("act", lambda e: e.activation(out=out, in_=in_, func=AF.Copy), deps)
    return P.op(eng, lambda e: e.tensor_copy(out=out, in_=in_), deps)


def bnstats(P, out, in_, deps=()):
    return P.op("dve", lambda e: e.bn_stats(out=out, in_=in_), deps)


def bnaggr(P, out, in_, deps=()):
    return P.op("dve", lambda e: e.bn_aggr(out=out, in_=in_), deps)


def coll(P, key, kind, op, groups, in_, out, deps=()):
    return P.dma("pool", key, lambda e: e.collective_compute(kind, op, replica_groups=groups, ins=[in_], outs=[out]),
                 deps, inc=1)


def dma(P, q, key, out, in_, deps=()):
    return P.dma(q, key, lambda e: e.dma_start(out=out, in_=in_), deps)


def load_consts(P, C):
    nc, A = C.nc, C.arena
    ident = np.eye(128, dtype=np.float32)
    kk = np.arange(128)[:, None]
    qq = np.arange(128)[None, :]
    negmask = np.where(kk > qq, NEG, 0.0).astype(np.float32)
    sel = np.zeros((4, 4, 128), np.float32)
    for h in range(4):
        sel[h, h, :] = 1.0
    c = Ctx()
    c.ident_bf = A.alloc([128, 128], BF16)
    c.ones_bf = A.alloc([128, 128], BF16)
    c.negmask_bf = A.alloc([128, 128], BF16)
    c.ident_f = A.alloc([128, 128], F32)
    c.sel = A.alloc([4, 4, 128], F32)
    toks = []
    d_id = nc.inline_tensor(ident.astype(NPBF), "c_identbf").ap()
    d_on = nc.inline_tensor(np.ones((128, 128), NPBF), "c_onesbf").ap()
    d_nm = nc.inline_tensor(negmask.astype(NPBF), "c_negmask").ap()
    d_if = nc.inline_tensor(ident, "c_identf").ap()
    d_sel = nc.inline_tensor(sel.transpose(1, 0, 2).copy(), "c_sel").ap()
    toks.append(dma(P, "sp", "const", c.ident_bf, d_id[:, :]))
    toks.append(dma(P, "sp", "const", c.ones_bf, d_on[:, :]))
    toks.append(dma(P, "sp", "const", c.negmask_bf, d_nm[:, :]))
    toks.append(dma(P, "sp", "const", c.ident_f, d_if[:, :]))
    toks.append(dma(P, "sp", "const", c.sel, d_sel[:, :, :]))
    c.tok = toks[-1]
    C.k = c
    C.eps_qk = A.alloc([128, 1], F32)
    C.eps_ln = A.alloc([128, 1], F32)
    C.eps_gn = A.alloc([128, 1], F32)
    P.op("pool", lambda e: e.memset(C.eps_qk, 128.0 * QK_EPS))
    P.op("pool", lambda e: e.memset(C.eps_ln, LN_EPS))
    C.t_eps = P.op("pool", lambda e: e.memset(C.eps_gn, GN_EPS))


def phase_D(P, C, xres, x_in, parts, nparts, lng, lnb, xT_out, x_out, deps=()):
    nc, A, K, PS = C.nc, C.arena, C.k, C.ps
    m0 = A.mark()
    out_toks = []
    t_x = None
    if x_in is not None:
        t_x = dma(P, "sp", "dx", xres, x_in.rearrange("(j p) d -> p j d", p=128), deps)
    if parts is not None:
        gb = A.alloc([128, D], F32)
        bb = A.alloc([128, D], F32)
        t_g = dma(P, "sp", "dgb", gb, lng.partition_broadcast(128), deps)
        t_b = dma(P, "sp", "dgb", bb, lnb.partition_broadcast(128), deps)
        zt = Rot([A.alloc([128, D], F32) for _ in range(3)])
        hb = [A.alloc([128, D], F32) for _ in range(2)]
        st6 = A.alloc([128, 4, 6], F32)
        mv = A.alloc([128, 2], F32)
        rstd = A.alloc([128, 1], F32)
        nmr = A.alloc([128, 1], F32)
    xb = [A.alloc([128, D], BF16) for _ in range(2)]
    xTs = A.alloc([128, 16, TOKC], BF16) if xT_out is not None else None
    z_free = [None, None]
    h_free = [None, None]
    xb_free = [None, None]
    ps_free = [None, None]
    t_last_xres = None
    for j in range(8):
        b2 = j % 2
        if parts is not None:
            t_h = None
            for r in range(nparts):
                zi = zt.next()
                tz = dma(P, "sp", "dz%d" % zi, zt.bufs[zi], parts[r, j * 128:(j + 1) * 128, :],
                         list(deps) + [zt.free[zi]])
                if r == 0:
                    t_h = stt(P, hb[b2], xres[:, j, :], ALPHA, zt.bufs[zi], ALU.mult, ALU.add,
                              [t_x, tz, h_free[b2]])
                else:
                    t_h = tt(P, "pool", hb[b2], hb[b2], zt.bufs[zi], ALU.add, [t_h, tz])
                zt.free[zi] = t_h
            t_s = None
            for q in range(4):
                t_s = bnstats(P, st6[:, q, :], hb[b2][:, q * 512:(q + 1) * 512], [t_h, t_s])
            t_s = bnaggr(P, mv, st6, [t_s])
            t_s = act(P, rstd, mv[:, 1:2], AF.Sqrt, [t_s, C.t_eps], bias=C.eps_ln)
            t_s = P.op("dve", lambda e: e.reciprocal(out=rstd, in_=rstd), [t_s])
            t_s = ts(P, "dve", nmr, mv[:, 0:1], rstd, -1.0, ALU.mult, ALU.mult, [t_s])
            t_n = act(P, hb[b2], hb[b2], AF.Identity, [t_s, t_h], bias=nmr, scale=rstd)
            t_n = tt(P, "dve", hb[b2], hb[b2], gb, ALU.mult, [t_n, t_b])
            t_n = tt(P, "pool", xres[:, j, :], hb[b2], bb, ALU.add, [t_n, t_b, t_x])
            h_free[b2] = t_n
            t_xj = t_n
        else:
            t_xj = t_x
        t_last_xres = t_xj
        if xT_out is not None:
            t_c = cp(P, "act", xb[b2], xres[:, j, :], [t_xj, xb_free[b2]])
            for g in range(2):
                pb = PS[g].bitcast(BF16).rearrange("p (a b) -> p a b", b=128)
                t_t = None
                for c8 in range(8):
                    cc = g * 8 + c8
                    t_t = tr(P, pb[:, c8, :], xb[b2][:, cc * 128:(cc + 1) * 128], K.ident_bf,
                             [t_c, K.tok, ps_free[g]])
                ps_free[g] = cp(P, "dve", xTs[:, g * 8:(g + 1) * 8, j * 128:(j + 1) * 128], pb, [t_t])
            xb_free[b2] = t_t
    if xT_out is not None:
        out_toks.append(dma(P, "sp", "dxT", xT_out.rearrange("(c p) t -> p c t", p=128), xTs,
                            [ps_free[0], ps_free[1]]))
    if x_out is not None:
        out_toks.append(dma(P, "sp", "dxo", x_out.rearrange("(j p) d -> p j d", p=128), xres,
                            [t_last_xres]))
    A.reset(m0)
    return out_toks


def load_w_bf16(P, key, dst, src, deps=()):
    return dma(P, "pool", key, dst, src.rearrange("(c p) n -> p c n", p=128), deps)


class Rot:
    def __init__(self, bufs):
        self.bufs = bufs
        self.free = [None] * len(bufs)
        self.i = 0

    def next(self):
        k = self.i % len(self.bufs)
        self.i += 1
        return k


def preload_A_fox(P, C, w_d, deps=()):
    A = C.arena
    W = A.alloc([128, 16, 2048], BF16)
    Wf = A.alloc([128, 16, 4], BF16)
    tw = []
    for s in range(4):
        tw.append(load_w_bf16(P, "wA%d" % s, W[:, :, s * 512:(s + 1) * 512], w_d[:, s * 512:(s + 1) * 512], deps))
    twf = load_w_bf16(P, "wAf", Wf, w_d[:, 2048:2052], deps)
    return (W, Wf, tw, twf)


def phase_A_fox(P, C, xT_d, w_d, bf_d, gq_d, gk_d, scr, deps=(), xdeps=None, pre=None):
    nc, A, K, PS = C.nc, C.arena, C.k, C.ps
    if pre is None:
        pre = preload_A_fox(P, C, w_d, deps)
    W, Wf, tw, twf = pre
    gq = A.alloc([128, 1], F32)
    gk = A.alloc([128, 1], F32)
    bfc = A.alloc([4, 1], F32)
    t_small = dma(P, "sp", "smallA", gq, gq_d, deps)
    t_small = dma(P, "sp", "smallA", gk, gk_d, deps)
    t_small = dma(P, "sp", "smallA", bfc, bf_d, deps)
    t_gk = ts(P, "dve", gk, gk, float(np.sqrt(128.0)), None, ALU.mult, None, [t_small])
    xt = Rot([A.alloc([128, 16, 512], BF16) for _ in range(2)])
    sq = Rot([A.alloc([128, 512], BF16) for _ in range(2)])
    rr = Rot([A.alloc([128, 512], F32) for _ in range(2)])
    st_qk = Rot([A.alloc([128, 512], BF16) for _ in range(3)])
    st_g = Rot([A.alloc([128, 512], BF16) for _ in range(2)])
    st_v = Rot([A.alloc([128, 4, 4, 128], BF16) for _ in range(2)])
    fz = A.alloc([4, 512], F32)
    fa = A.alloc([4, 512], F32)
    fm = A.alloc([4, 512], F32)
    pm = Rot([PS[0], PS[1], PS[2], PS[3]])
    pq = Rot([PS[4], PS[5]])
    pf = PS[6]
    pf_free = [None]
    f_free = [None]
    stores = []
    ncr = C.ncr
    pending_ep = []

    def flush():
        while pending_ep:
            pending_ep.pop(0)()

    for tt_i in range(8):
        r, half = tt_i // 2, tt_i % 2
        xk = xt.next()
        X = xt.bufs[xk]
        t_xt = dma(P, "sp", "xt%d" % xk, X,
                   xT_d[r][:, half * 512:(half + 1) * 512].rearrange("(c p) t -> p c t", p=128),
                   list(deps) + [xt.free[xk]] + ([xdeps[r]] if xdeps else []))
        last_pe = None
        units = [("q", h) for h in range(4)] + [("k", h) for h in range(4)] + [("g", h) for h in range(4)]
        for kind, h in units:
            col0 = {"q": 0, "k": 512, "g": 1536}[kind] + h * 128
            wtok = tw[{"q": 0, "k": 1, "g": 3}[kind]]
            b = pm.next()
            pacc = pm.bufs[b]
            t_m = None
            for c in range(16):
                t_m = mm(P, pacc, W[:, c, col0:col0 + 128], X[:, c, :], c == 0, c == 15,
                         [t_xt, wtok, pm.free[b]])
            last_pe = t_m
            flush()
            if kind == "g":
                sb = st_g.next()
                t_e = act(P, st_g.bufs[sb], pacc, AF.Silu, [t_m, st_g.free[sb]])
                pm.free[b] = t_e
                t_st = dma(P, "sp", "stg%d" % sb, scr.sgT[h, :, tt_i * 512:(tt_i + 1) * 512], st_g.bufs[sb], [t_e])
                st_g.free[sb] = t_st
                stores.append(t_st)
            else:
                s_i = sq.next()
                t_sq = act(P, sq.bufs[s_i], pacc, AF.Square, [t_m, sq.free[s_i]])

                def ep(kind=kind, h=h, pacc=pacc, b=b, s_i=s_i, t_sq=t_sq, tt_i=tt_i):
                    qb = pq.next()
                    t_q = mm(P, pq.bufs[qb], K.ones_bf, sq.bufs[s_i], True, True, [t_sq, K.tok, pq.free[qb]])
                    sq.free[s_i] = t_q
                    ri = rr.next()
                    t_r = act(P, rr.bufs[ri], pq.bufs[qb], AF.Sqrt, [t_q, rr.free[ri], C.t_eps], bias=C.eps_qk)
                    pq.free[qb] = t_r
                    t_r = P.op("dve", (lambda o: lambda e: e.reciprocal(out=o, in_=o))(rr.bufs[ri]), [t_r])
                    si = st_qk.next()
                    gcol = gq if kind == "q" else gk
                    t_n = stt(P, st_qk.bufs[si], pacc, gcol, rr.bufs[ri], ALU.mult, ALU.mult,
                              [t_r, t_gk, st_qk.free[si]])
                    rr.free[ri] = t_n
                    pm.free[b] = t_n
                    dst = (scr.qT if kind == "q" else scr.kT)[h, :, tt_i * 512:(tt_i + 1) * 512]
                    t_st = dma(P, "sp", "stqk%d" % si, dst, st_qk.bufs[si], [t_n])
                    st_qk.free[si] = t_st
                    stores.append(t_st)
                pending_ep.append(ep)
        vi = st_v.next()
        VS = st_v.bufs[vi]
        t_ev = None
        for jj in range(4):
            b = pm.next()
            pacc = pm.bufs[b]
            t_m = None
            for c in range(16):
                t_m = mm(P, pacc, X[:, c, jj * 128:(jj + 1) * 128], W[:, c, 1024:1536], c == 0, c == 15,
                         [t_xt, tw[2], pm.free[b]])
            flush()
            t_ev = cp(P, "act", VS[:, :, jj, :], pacc.rearrange("p (h d) -> p h d", d=128),
                      [t_m, st_v.free[vi] if jj == 0 else None])
            pm.free[b] = t_ev
        t_st = dma(P, "sp", "stv%d" % vi, scr.v[:, :, tt_i * 4:(tt_i + 1) * 4, :].rearrange("h p j d -> p h j d"),
                   VS, [t_ev])
        st_v.free[vi] = t_st
        stores.append(t_st)
        t_m = None
        for c in range(16):
            t_m = mm(P, pf[0:4, :], Wf[:, c, :], X[:, c, :], c == 0, c == 15, [t_xt, twf, pf_free[0]])
        last_pe = t_m
        xt.free[xk] = t_m
        t_z = ts(P, "dve", fz, pf[0:4, :], bfc, None, ALU.add, None, [t_m, t_small, f_free[0]])
        pf_free[0] = t_z
        t_a = act(P, fa, fz, AF.Abs, [t_z])
        t_a = act(P, fa, fa, AF.Exp, [t_a], scale=-1.0)
        t_a = act(P, fa, fa, AF.Ln, [t_a], bias=1.0)
        t_mx = ts(P, "dve", fm, fz, -1.0, 0.0, ALU.mult, ALU.max, [t_z])
        t_f = tt(P, "dve", ncr[:, tt_i * 512:(tt_i + 1) * 512], fm, fa, ALU.add, [t_mx, t_a])
        f_free[0] = t_f
    flush()
    C.t_ncr = t_f
    return stores


def phase_B_fox(P, C, gqrow_d, gkrow_d, scr, deps=()):
    nc, A, K, PS = C.nc, C.arena, C.k, C.ps
    ncr = C.ncr
    ones1 = A.alloc([4, 1], F32)
    ncc = A.alloc([4, S], F32)
    t0 = P.op("pool", lambda e: e.memset(ones1, 1.0), list(deps))
    onesr = ones1.to_broadcast([4, S])
    t_scan = P.op("dve", lambda e: e.tensor_tensor_scan(out=ncc, data0=onesr, data1=ncr, initial=0.0,
                                                        op0=ALU.mult, op1=ALU.add), [t0, C.t_ncr])
    gqb = A.alloc([128, 128], F32)
    gkb = A.alloc([128, 128], F32)
    t_g = dma(P, "sp", "gB", gqb, gqrow_d.partition_broadcast(128), deps)
    t_g = dma(P, "sp", "gB", gkb, gkrow_d.partition_broadcast(128), deps)
    mq = A.alloc([128, 1], F32)
    mk = A.alloc([128, 1], F32)
    negM = A.alloc([128, 1], F32)
    t_m = act(P, gqb, gqb, AF.Abs, [t_g])
    t_m = P.op("dve", lambda e: e.reduce_max(out=mq, in_=gqb, axis=AX.X), [t_m])
    t_m = act(P, gkb, gkb, AF.Abs, [t_m])
    t_m = P.op("dve", lambda e: e.reduce_max(out=mk, in_=gkb, axis=AX.X), [t_m])
    t_m = ts(P, "dve", negM, mq, mk, -float(128.0 ** 0.5), ALU.mult, ALU.mult, [t_m])
    biasK = A.alloc([128, 32, 4], F32)
    pT = PS[7]
    t_t = None
    for j in range(32):
        t_t = tr(P, pT[:, j * 4:(j + 1) * 4], ncc[:, j * 128:(j + 1) * 128], K.ident_f[0:4, 0:4],
                 [t_scan, K.tok])
    t_bk = ts(P, "dve", biasK, pT[:, 0:128].rearrange("p (j h) -> p j h", h=4), negM, None, ALU.add, None,
              [t_t, t_m])
    hb = []
    for i in range(2):
        h_ = Ctx()
        h_.q = A.alloc([128, S], BF16)
        h_.k = A.alloc([128, S], BF16)
        h_.v = A.alloc([128, 32, 128], BF16)
        h_.sg = A.alloc([128, S], BF16)
        h_.ncq = A.alloc([128, S], F32)
        h_.free = None
        hb.append(h_)
    sa = Rot([A.alloc([128, 512], F32) for _ in range(4)])
    pt = Rot([A.alloc([128, 512], BF16) for _ in range(4)])
    rl = A.alloc([128, 512], F32)
    of = A.alloc([128, 512], F32)
    ys = Rot([A.alloc([128, 512], BF16) for _ in range(2)])
    pS = Rot([PS[0], PS[1], PS[2], PS[7]])
    pS.free[3] = t_bk
    pO = [PS[3], PS[4]]
    pL = [PS[5], PS[6]]
    pOL_free = [None, None]
    rl_free = [None]
    stores = []
    hstores = []
    LAG = 3

    pairs = []
    for h in range(4):
        for t in range(8):
            for j in range(4 * t + 4):
                pairs.append((h, t, j))
    loaded = {}
    pv_q = []

    def load_head(h):
        H = hb[h % 2]
        dd = list(deps) + [H.free]
        tl = dma(P, "sp", "hq%d" % (h % 2), H.q, scr.qT[h], dd)
        tl = dma(P, "sp", "hq%d" % (h % 2), H.k, scr.kT[h], dd)
        tl = dma(P, "sp", "hq%d" % (h % 2), H.v, scr.v[h], dd)
        tl = dma(P, "sp", "hq%d" % (h % 2), H.sg, scr.sgT[h], dd)
        t_c = None
        for t8 in range(8):
            b = pS.next()
            t_b = mm(P, pS.bufs[b], K.sel[:, h, :], ncc[:, t8 * 512:(t8 + 1) * 512], True, True,
                     [t_scan, K.tok, pS.free[b], H.free])
            t_c = cp(P, "act", H.ncq[:, t8 * 512:(t8 + 1) * 512], pS.bufs[b], [t_b, H.free])
            pS.free[b] = t_c
        loaded[h] = (tl, t_c)

    def front(h, t, j):
        H = hb[h % 2]
        tl, t_c = loaded[h]
        diag = j >= 4 * t
        off = (j - 4 * t) * 128 if diag else 0
        N = 512 - off
        q0 = t * 512 + off
        b = pS.next()
        ps = pS.bufs[b]
        t_s = mm(P, ps[:, 0:N], H.k[:, j * 128:(j + 1) * 128], H.q[:, q0:q0 + N], True, not diag,
                 [tl, pS.free[b]])
        if diag:
            t_s = mm(P, ps[:, 0:128], K.ident_bf, K.negmask_bf, False, True, [K.tok])
        si = sa.next()
        t_d = tt(P, "dve", sa.bufs[si][:, 0:N], ps[:, 0:N], H.ncq[:, q0:q0 + N], ALU.subtract,
                 [t_s, t_c, sa.free[si]])
        pS.free[b] = t_d
        pi = pt.next()
        t_e = act(P, pt.bufs[pi][:, 0:N], sa.bufs[si][:, 0:N], AF.Exp, [t_d, t_bk, pt.free[pi]],
                  bias=biasK[:, j, h:h + 1])
        sa.free[si] = t_e
        return (h, t, j, off, N, pi, t_e)

    def back(h, t, j, off, N, pi, t_e):
        H = hb[h % 2]
        ob = t % 2
        last = (j == 4 * t + 3)
        t_o = mm(P, pO[ob][:, off:512], H.v[:, j, :], pt.bufs[pi][:, 0:N], j == 0, last,
                 [t_e, pOL_free[ob] if j == 0 else None])
        t_l = mm(P, pL[ob][:, off:512], K.ones_bf, pt.bufs[pi][:, 0:N], j == 0, last, [])
        pt.free[pi] = t_l
        if last:
            t_r = P.op("dve", lambda e: e.reciprocal(out=rl, in_=pL[ob]), [t_l, rl_free[0]])
            t_f = tt(P, "dve", of, pO[ob], rl, ALU.mult, [t_r])
            pOL_free[ob] = t_f
            yi = ys.next()
            t_y = tt(P, "dve", ys.bufs[yi], of, H.sg[:, t * 512:(t + 1) * 512], ALU.mult,
                     [t_f, ys.free[yi], loaded[h][0]])
            rl_free[0] = t_y
            t_st = dma(P, "sp", "sty%d" % yi, scr.yl[h][:, t * 512:(t + 1) * 512], ys.bufs[yi], [t_y])
            ys.free[yi] = t_st
            hstores.append(t_st)
            C.last_stores.append(t_st)
            if t == 7:
                H.free = t_y
                if scr.yg is not None:
                    stores.append(coll(P, "ccy", "AllGather", ALU.bypass, GROUPS,
                                       scr.yl[h].rearrange("p (a t) -> (p a) t", a=4),
                                       scr.yg[h].rearrange("p (a t) -> (p a) t", a=4), list(hstores)))
                else:
                    stores.extend(hstores)
                del hstores[:]

    load_head(0)
    for i, (h, t, j) in enumerate(pairs):
        if t == 0 and j == 0 and h + 1 < 4:
            pass
        pv_q.append(front(h, t, j))
        if len(pv_q) > LAG:
            back(*pv_q.pop(0))
        if t == 1 and j == 0 and h + 1 < 4:
            while pv_q and pv_q[0][0] < h:
                back(*pv_q.pop(0))
            load_head(h + 1)
    while pv_q:
        back(*pv_q.pop(0))
    return stores


def phase_C(P, C, yT_d, wo_d, part_out, ec, deps=()):
    nc, A, K, PS = C.nc, C.arena, C.k, C.ps
    Wo = A.alloc([128, ec, D], BF16)
    t_w = load_w_bf16(P, "wC", Wo, wo_d, deps)
    Y = A.alloc([128, ec, S], BF16)
    t_y = []
    for q in range(4):
        t_y.append(dma(P, "sp", "yC%d" % q, Y[:, :, q * 1024:(q + 1) * 1024],
                       yT_d[:, :, q * 1024:(q + 1) * 1024].rearrange("e p t -> p e t"), deps))
    ost = Rot([A.alloc([128, D], F32) for _ in range(2)])
    pm = Rot([PS[0], PS[1], PS[2], PS[3]])
    stores = []
    n = 0
    for j in range(32):
        oi = ost.next()
        O = ost.bufs[oi]
        t_e = None
        for dt in range(4):
            b = pm.next()
            t_m = None
            for e_ in range(ec):
                t_m = mm(P, pm.bufs[b], Y[:, e_, j * 128:(j + 1) * 128], Wo[:, e_, dt * 512:(dt + 1) * 512],
                         e_ == 0, e_ == ec - 1, [t_w, t_y[j // 8], pm.free[b]])
            eng = "act" if n % 2 == 0 else "dve"
            n += 1
            t_e = cp(P, eng, O[:, dt * 512:(dt + 1) * 512], pm.bufs[b], [t_m, ost.free[oi] if dt == 0 else None])
            pm.free[b] = t_e
            if dt == 2:
                t_e2 = t_e
        t_st = dma(P, "sp", "stC%d" % oi, part_out[j * 128:(j + 1) * 128, :], O, [t_e, t_e2])
        ost.free[oi] = t_st
        stores.append(t_st)
    return stores


ARENA_BYTES = 204 * 1024


def new_prog():
    nc = bass.Bass("TRN2", target_bir_lowering=False)
    return nc


class Scr:
    pass


def setup(nc, st):
    C = Ctx()
    C.nc = nc
    ar = st.enter_context(nc.sbuf_tensor("arena", [128, ARENA_BYTES], U8))
    C.arena = Arena(ar, ARENA_BYTES)
    C.ps = [st.enter_context(nc.psum_tensor("ps%d" % i, [128, 512], F32)) for i in range(8)]
    C.ps = [p[:, :] for p in C.ps]
    return C


def build_D(first, last):
    nc = new_prog()
    x_in = nc.dram_tensor("x_in", [TOKC, D], F32, kind="ExternalInput").ap()
    if not first:
        parts = nc.dram_tensor("parts", [4, TOKC, D], F32, kind="ExternalInput").ap()
        lng = nc.dram_tensor("lng", [1, D], F32, kind="ExternalInput").ap()
        lnb = nc.dram_tensor("lnb", [1, D], F32, kind="ExternalInput").ap()
    xT_out = None if last else nc.dram_tensor("xT_out", [D, TOKC], BF16, kind="ExternalOutput").ap()
    x_out = None if first else nc.dram_tensor("x_out", [TOKC, D], F32, kind="ExternalOutput").ap()
    with contextlib.ExitStack() as st:
        C = setup(nc, st)
        P = Prog(nc)
        load_consts(P, C)
        xres = C.arena.alloc([128, 8, D], F32)
        if first:
            outs = phase_D(P, C, xres, x_in, None, 0, None, None, xT_out, None)
        else:
            outs = phase_D(P, C, xres, x_in, parts, 4, lng[0], lnb[0], xT_out, x_out)
        P.emit(final_waits=outs)
    return nc


def build_L_fox():
    nc = new_prog()
    xT_d = nc.dram_tensor("xT", [4, D, TOKC], BF16, kind="ExternalInput").ap()
    w_d = nc.dram_tensor("w_in", [D, 2052], F32, kind="ExternalInput").ap()
    bf_d = nc.dram_tensor("bf", [4, 1], F32, kind="ExternalInput").ap()
    gq_d = nc.dram_tensor("gq", [128, 1], F32, kind="ExternalInput").ap()
    gk_d = nc.dram_tensor("gk", [128, 1], F32, kind="ExternalInput").ap()
    gqr_d = nc.dram_tensor("gqr", [1, 128], F32, kind="ExternalInput").ap()
    gkr_d = nc.dram_tensor("gkr", [1, 128], F32, kind="ExternalInput").ap()
    wo_d = nc.dram_tensor("w_out", [512, D], F32, kind="ExternalInput").ap()
    part = nc.dram_tensor("part", [S, D], F32, kind="ExternalOutput").ap()
    scr = Scr()
    scr.qT = nc.dram_tensor("s_qT", [4, 128, S], BF16).ap()
    scr.kT = nc.dram_tensor("s_kT", [4, 128, S], BF16).ap()
    scr.v = nc.dram_tensor("s_v", [4, 128, 32, 128], BF16).ap()
    scr.sgT = nc.dram_tensor("s_sgT", [4, 128, S], BF16).ap()
    scr.yT = nc.dram_tensor("s_yT", [4, 128, S], BF16).ap()
    with contextlib.ExitStack() as st:
        C = setup(nc, st)
        P = Prog(nc)
        load_consts(P, C)
        C.ncr = C.arena.alloc([4, S], F32)
        m0 = C.arena.mark()
        stA = phase_A_fox(P, C, xT_d, w_d, bf_d, gq_d, gk_d, scr)
        P.barrier()
        C.arena.reset(m0)
        stB = phase_B_fox(P, C, gqr_d[0], gkr_d[0], scr, deps=stA)
        P.barrier()
        C.arena.reset(m0)
        stC = phase_C(P, C, scr.yT, wo_d, part, 4, deps=stB)
        P.emit(final_waits=stC)
    return nc


def phase_A_ret(P, C, xT_d, w_d, cos_d, sin_d, scr, deps=(), xdeps=None, after_w=None):
    nc, A, K, PS = C.nc, C.arena, C.k, C.ps
    m0 = A.mark()
    stores = []
    Wb = [A.alloc([128, 16, 1536], BF16) for _ in range(2)]
    xt = Rot([A.alloc([128, 16, 512], BF16) for _ in range(2)])
    cs = Rot([A.alloc([128, 2, 512], F32) for _ in range(2)])
    twb = []
    for hp_ in range(2):
        tw_ = []
        for s_ in range(3):
            tw_.append(load_w_bf16(P, "wR%d_%d" % (hp_, s_), Wb[hp_][:, :, s_ * 512:(s_ + 1) * 512],
                                   w_d[:, hp_ * 1536 + s_ * 512:hp_ * 1536 + (s_ + 1) * 512], list(deps)))
        twb.append(tw_)
    if after_w is not None:
        C.t_wo = after_w()
    tmp = Rot([A.alloc([128, 2, 512], F32) for _ in range(2)])
    st_r = Rot([A.alloc([128, 512], BF16) for _ in range(4)])
    st_g = Rot([A.alloc([128, 512], BF16) for _ in range(2)])
    st_v = Rot([A.alloc([128, 4, 512], BF16) for _ in range(2)])
    pm = Rot([PS[i] for i in range(6)])
    w_free = None
    for hp in range(2):
        tw = twb[hp]
        W = Wb[hp]
        for tt_i in range(8):
            r, half = tt_i // 2, tt_i % 2
            xk = xt.next()
            X = xt.bufs[xk]
            t_xt = dma(P, "sp", "xt%d" % xk, X,
                       xT_d[r][:, half * 512:(half + 1) * 512].rearrange("(c p) t -> p c t", p=128),
                       list(deps) + [xt.free[xk]] + ([xdeps[r]] if xdeps else []))
            ci = cs.next()
            CS = cs.bufs[ci]
            t_cs = dma(P, "sp", "cs%d" % ci, CS[:, 0, :], cos_d[:, tt_i * 512:(tt_i + 1) * 512], list(deps) + [cs.free[ci]])
            t_cs = dma(P, "sp", "cs%d" % ci, CS[:, 1, :], sin_d[:, tt_i * 512:(tt_i + 1) * 512], list(deps) + [cs.free[ci]])
            t_last_cs = None
            for kind in ("q", "k"):
                col0 = 0 if kind == "q" else 256
                scale = 1.0 if kind == "q" else 1.0 / 16.0
                banks = []
                t_m = None
                for hf in range(2):
                    b = pm.next()
                    banks.append(b)
                    for c in range(16):
                        t_m = mm(P, pm.bufs[b], W[:, c, col0 + hf * 128:col0 + (hf + 1) * 128], X[:, c, :],
                                 c == 0, c == 15, [t_xt, tw[0], pm.free[b]])
                pa, pb = pm.bufs[banks[0]], pm.bufs[banks[1]]
                dst = scr.qT if kind == "q" else scr.kT
                for oi, (ca, cb, op) in enumerate(((0, 1, ALU.subtract), (1, 0, ALU.add))):
                    ti = tmp.next()
                    T = tmp.bufs[ti]
                    t1 = stt(P, T[:, 0, :], pa, scale, CS[:, ca, :], ALU.mult, ALU.mult, [t_m, t_cs, tmp.free[ti]])
                    t2 = stt(P, T[:, 1, :], pb, scale, CS[:, cb, :], ALU.mult, ALU.mult, [t_m, t_cs])
                    si = st_r.next()
                    t3 = tt(P, "pool", st_r.bufs[si], T[:, 0, :], T[:, 1, :], op, [t1, t2, st_r.free[si]])
                    tmp.free[ti] = t3
                    t_st = dma(P, "sp", "str%d" % si, dst[hp, oi, :, tt_i * 512:(tt_i + 1) * 512], st_r.bufs[si], [t3])
                    st_r.free[si] = t_st
                    stores.append(t_st)
                pm.free[banks[0]] = t2
                pm.free[banks[1]] = t2
                t_last_cs = t2
            cs.free[ci] = t_last_cs
            for gt in range(4):
                b = pm.next()
                t_m = None
                for c in range(16):
                    t_m = mm(P, pm.bufs[b], W[:, c, 1024 + gt * 128:1024 + (gt + 1) * 128], X[:, c, :],
                             c == 0, c == 15, [t_xt, tw[2], pm.free[b]])
                sb = st_g.next()
                t_e = act(P, st_g.bufs[sb], pm.bufs[b], AF.Silu, [t_m, st_g.free[sb]])
                pm.free[b] = t_e
                t_st = dma(P, "sp", "stg%d" % sb, scr.sgT[hp, gt, :, tt_i * 512:(tt_i + 1) * 512], st_g.bufs[sb], [t_e])
                st_g.free[sb] = t_st
                stores.append(t_st)
            vi = st_v.next()
            VS = st_v.bufs[vi]
            t_ev = None
            for jj in range(4):
                b = pm.next()
                t_m = None
                for c in range(16):
                    t_m = mm(P, pm.bufs[b], X[:, c, jj * 128:(jj + 1) * 128], W[:, c, 512:1024],
                             c == 0, c == 15, [t_xt, tw[1], pm.free[b]])
                t_ev = cp(P, "act", VS[:, jj, :], pm.bufs[b], [t_m, st_v.free[vi] if jj == 0 else None])
                pm.free[b] = t_ev
            xt.free[xk] = t_m
            w_free = t_m
            t_st = dma(P, "sp", "stv%d" % vi,
                       scr.v[hp, tt_i * 512:(tt_i + 1) * 512, :].rearrange("(j p) d -> p j d", p=128), VS, [t_ev])
            st_v.free[vi] = t_st
            stores.append(t_st)
    A.reset(m0)
    return stores


def phase_B_ret(P, C, dec_d, scr, deps=()):
    nc, A, K, PS = C.nc, C.arena, C.k, C.ps
    m0 = A.mark()
    stores = []
    hstores = []
    HB = []
    for hp in range(2):
        h_ = Ctx()
        h_.Q = A.alloc([128, 2, S], BF16)
        h_.Kt = A.alloc([128, 2, S], BF16)
        h_.Qd = A.alloc([128, 2, S], BF16)
        h_.dec = A.alloc([128, 258], F32)
        HB.append(h_)
    SGr = Rot([A.alloc([128, 4, 1024], BF16) for _ in range(3)])
    Vr = Rot([A.alloc([128, 8, 512], BF16) for _ in range(2)])
    on = Rot([A.alloc([128, 512], BF16) for _ in range(4)])
    yst = Rot([A.alloc([128, 4, 4, 128], BF16) for _ in range(2)])
    st6 = A.alloc([128, 6], F32)
    mv = A.alloc([128, 2], F32)
    rstd = A.alloc([128, 1], F32)
    nmr = A.alloc([128, 1], F32)
    mvA = [A.alloc([128, 2], F32) for _ in range(4)]
    nmA = [A.alloc([128, 1], F32) for _ in range(4)]
    mv_free = [None] * 4
    pI, pK = PS[0], PS[1].bitcast(BF16)
    pO = [PS[2], PS[3]]
    pTt = PS[4].bitcast(BF16)
    pR = [PS[5], PS[6]]
    free = Ctx()
    free.pI = free.pK = free.pT = None
    free.pO = [None, None]
    free.pR = [None, None]
    head_free = None
    dd = list(deps)
    sg_tok = {}
    sg_buf = {}
    sg_order = [(hp_, q_) for q_ in range(4) for hp_ in range(2)]

    def load_sg(idx):
        hp_, q_ = sg_order[idx]
        k = SGr.next()
        sl_ = slice(q_ * 1024, (q_ + 1) * 1024)
        sg_buf[(hp_, q_)] = (k, SGr.bufs[k])
        sg_tok[(hp_, q_)] = dma(P, "sp", "rsg%d" % k, SGr.bufs[k],
                                scr.sgT[hp_, :, :, sl_].rearrange("g p t -> p g t"), dd + [SGr.free[k]])

    v_tok = {}
    v_buf = {}

    def load_v(idx):
        hp_, q_ = sg_order[idx]
        k = Vr.next()
        sl_ = slice(q_ * 1024, (q_ + 1) * 1024)
        v_buf[(hp_, q_)] = (k, Vr.bufs[k])
        v_tok[(hp_, q_)] = dma(P, "sp", "rv%d" % k, Vr.bufs[k],
                               scr.v[hp_, sl_, :].rearrange("(j p) d -> p j d", p=128), dd + [Vr.free[k]])

    v_next = [2]
    for hp in range(2):
        H = HB[hp]
        H.t_dec = dma(P, "sp", "dec%d" % hp, H.dec, dec_d[hp], dd)
        H.tq, H.tqd = [], []
        for q4 in range(4):
            sl = slice(q4 * 1024, (q4 + 1) * 1024)
            key = "rl%d_%d" % (hp, q4)
            t = dma(P, "sp", key, H.Q[:, :, sl], scr.qT[hp, :, :, sl].rearrange("h p t -> p h t"), dd)
            t = dma(P, "sp", key, H.Kt[:, :, sl], scr.kT[hp, :, :, sl].rearrange("h p t -> p h t"), dd)
            H.tq.append(t)
            if hp == 0 and q4 < 3:
                load_sg(q4)
            if hp == 0 and q4 < 2:
                load_v(q4)
    sg_next = [3]
    qd_tok = {}
    for hp_ in range(2):
        H_ = HB[hp_]
        for q_ in range(4):
            sl_ = slice(q_ * 1024, (q_ + 1) * 1024)
            t2 = None
            for hf in range(2):
                t2 = tt(P, "pool", H_.Qd[:, hf, sl_].rearrange("p (n q) -> p n q", q=128),
                        H_.Q[:, hf, sl_].rearrange("p (n q) -> p n q", q=128),
                        H_.dec[:, 128:256].unsqueeze(1).to_broadcast([128, 8, 128]), ALU.mult,
                        [H_.tq[q_], H_.t_dec])
            qd_tok[(hp_, q_)] = t2
    for hp in range(2):
        H = HB[hp]
        H.hp = hp
        H.Rf = A.alloc([128, 2, 512], F32)
        H.Rb = A.alloc([128, 2, 512], BF16)
        H.iT = Rot([A.alloc([128, 128], BF16) for _ in range(2)])
        H.Kd = Rot([A.alloc([128, 256], BF16) for _ in range(2)])
        H.t_rf = P.op("dve", (lambda r: lambda e: e.memset(r, 0.0))(H.Rf), [])
        H.t_rb = None
        H.pend = []
        H.nxt = None
        H.yi_cur = 0
        H.hst = []

    def inner(H, n):
        q4 = n // 8
        c0 = n * 128
        t_i = None
        for hf in range(2):
            t_i = mm(P, pI[:, 0:128], H.Kt[:, hf, c0:c0 + 128], H.Q[:, hf, c0:c0 + 128], hf == 0, hf == 1,
                     [H.tq[q4], free.pI])
        ii = H.iT.next()
        t_it = tt(P, "dve", H.iT.bufs[ii], pI[:, 0:128], H.dec[:, 0:128], ALU.mult, [t_i, H.t_dec, H.iT.free[ii]])
        free.pI = t_it
        t_k = None
        for hf in range(2):
            t_k = tr(P, pK[:, hf * 128:(hf + 1) * 128], H.Kt[:, hf, c0:c0 + 128], K.ident_bf, [K.tok, free.pK])
        ki = H.Kd.next()
        t_kd = ts(P, "dve", H.Kd.bufs[ki], pK[:, 0:256], H.dec[:, 256:257], None, ALU.mult, None,
                  [t_k, H.Kd.free[ki]])
        free.pK = t_kd
        return (ii, t_it, ki, t_kd)

    def finish(H, n, oi, t_n):
        hp = H.hp
        c0 = n * 128
        q4 = n // 8
        t_t = None
        for vc in range(4):
            t_t = tr(P, pTt[:, vc * 128:(vc + 1) * 128], on.bufs[oi][:, vc * 128:(vc + 1) * 128], K.ident_bf,
                     [t_n, free.pT])
        on.free[oi] = t_t
        nn = n % 4
        if nn == 0:
            H.yi_cur = yst.next()
        yi = H.yi_cur
        Y = yst.bufs[yi]
        sgk, SGq = sg_buf[(hp, q4)]
        cq = c0 - q4 * 1024
        t_y = tt(P, "dve", Y[:, :, nn, :], pTt[:, 0:512].rearrange("p (v q) -> p v q", q=128),
                 SGq[:, :, cq:cq + 128], ALU.mult, [t_t, sg_tok[(hp, q4)], yst.free[yi] if nn == 0 else None])
        free.pT = t_y
        if n % 8 == 7:
            SGr.free[sgk] = t_y
            if sg_next[0] < 8:
                load_sg(sg_next[0])
                sg_next[0] += 1
        if nn == 3:
            t0 = (n - 3) * 128
            qq, toff = t0 // 1024, t0 % 1024
            dstp = scr.yl[hp][qq].rearrange("(v p) t -> p v t", p=128)[:, :, toff:toff + 512]
            t_st = dma(P, "sp", "sty%d" % yi, dstp.rearrange("p v (n q) -> p v n q", q=128), Y, [t_y])
            yst.free[yi] = t_st
            H.hst.append(t_st)
            C.last_stores.append(t_st)
            if toff == 512:
                if scr.yg is not None:
                    stores.append(coll(P, "ccy", "AllGather", ALU.bypass, GROUPS, scr.yl[hp][qq], scr.yg[hp][qq],
                                       list(H.hst)))
                else:
                    stores.extend(H.hst)
                del H.hst[:]
        return t_y

    def chunk(H, n):
        hp = H.hp
        q4 = n // 8
        c0 = n * 128
        ii, t_it, ki, t_kd = H.nxt
        ob = n % 2
        cd = H.dec[:, 257:258]
        vk, Vq = v_buf[(hp, q4)]
        t_o = mm(P, pO[ob], H.iT.bufs[ii], Vq[:, n % 8, :], True, n == 0, [t_it, free.pO[ob], v_tok[(hp, q4)]])
        H.iT.free[ii] = t_o
        t_u = None
        if n < 31:
            for hf in range(2):
                t_u = mm(P, pR[hf], H.Kd.bufs[ki][:, hf * 128:(hf + 1) * 128], Vq[:, n % 8, :], True, True,
                         [t_kd, free.pR[hf]])
            H.Kd.free[ki] = t_u
            H.nxt = inner(H, n + 1)
        if n % 8 == 7:
            Vr.free[vk] = t_u if n < 31 else t_o
            if v_next[0] < 8:
                load_v(v_next[0])
                v_next[0] += 1
        if len(H.pend) == 2:
            finish(H, *H.pend.pop(0))
        if n > 0:
            for hf in range(2):
                t_o = mm(P, pO[ob], H.Qd[:, hf, c0:c0 + 128], H.Rb[:, hf, :], False, hf == 1,
                         [qd_tok[(hp, q4)], H.t_rb])
        t_ocross = t_o
        if n < 31:
            for hf in range(2):
                H.t_rf = stt(P, H.Rf[:, hf, :], H.Rf[:, hf, :], cd, pR[hf], ALU.mult, ALU.add,
                             [t_u, H.t_rf, H.t_dec])
                free.pR[hf] = H.t_rf
            H.t_rb = cp(P, "act", H.Rb, H.Rf, [H.t_rf, t_ocross])
        sl = n % 4
        mv_, nm_ = mvA[sl], nmA[sl]
        t_s = bnstats(P, st6, pO[ob], [t_ocross])
        t_s = bnaggr(P, mv_, st6, [t_s, mv_free[sl]])
        t_s = ts(P, "dve", nm_, mv_[:, 0:1], -1.0, None, ALU.mult, None, [t_s])
        t_a = act(P, rstd, mv_[:, 1:2], AF.Ln, [t_s, C.t_eps], bias=C.eps_gn)
        t_a = act(P, rstd, rstd, AF.Exp, [t_a], scale=-0.5)
        t_a = act(P, nmr, nm_, AF.Identity, [t_a], scale=rstd)
        mv_free[sl] = t_a
        oi = on.next()
        t_n = act(P, on.bufs[oi], pO[ob], AF.Identity, [t_a, on.free[oi]], bias=nmr, scale=rstd)
        free.pO[ob] = t_n
        H.pend.append((n, oi, t_n))

    for q in range(4):
        for hp in range(2):
            H = HB[hp]
            if q == 0:
                H.nxt = inner(H, 0)
            for n in range(q * 8, q * 8 + 8):
                chunk(H, n)
            while H.pend:
                finish(H, *H.pend.pop(0))
    A.reset(m0)
    return stores


def build_L_ret():
    nc = new_prog()
    xT_d = nc.dram_tensor("xT", [4, D, TOKC], BF16, kind="ExternalInput").ap()
    w_d = nc.dram_tensor("w_in", [D, 3072], F32, kind="ExternalInput").ap()
    cos_d = nc.dram_tensor("cos", [128, S], F32, kind="ExternalInput").ap()
    sin_d = nc.dram_tensor("sin", [128, S], F32, kind="ExternalInput").ap()
    dec_d = nc.dram_tensor("dec", [2, 128, 258], F32, kind="ExternalInput").ap()
    wo_d = nc.dram_tensor("w_out", [1024, D], F32, kind="ExternalInput").ap()
    part = nc.dram_tensor("part", [S, D], F32, kind="ExternalOutput").ap()
    scr = Scr()
    scr.qT = nc.dram_tensor("s_qT", [2, 2, 128, S], BF16).ap()
    scr.kT = nc.dram_tensor("s_kT", [2, 2, 128, S], BF16).ap()
    scr.v = nc.dram_tensor("s_v", [2, S, 512], BF16).ap()
    scr.sgT = nc.dram_tensor("s_sgT", [2, 4, 128, S], BF16).ap()
    scr.yT = nc.dram_tensor("s_yT", [8, 128, S], BF16).ap()
    with contextlib.ExitStack() as st:
        C = setup(nc, st)
        P = Prog(nc)
        load_consts(P, C)
        stA = phase_A_ret(P, C, xT_d, w_d, cos_d, sin_d, scr)
        P.barrier()
        stB = phase_B_ret(P, C, dec_d, scr, deps=stA)
        P.barrier()
        stC = phase_C(P, C, scr.yT, wo_d, part, 8, deps=stB)
        P.emit(final_waits=stC)
    return nc


def rotary_tables():
    inv_freq = (10000.0 ** (-np.arange(0, 256, 2, dtype=np.float32) / np.float32(256))).astype(np.float32)
    ang = np.arange(S, dtype=np.float32)[:, None] * inv_freq[None, :]
    return np.ascontiguousarray(np.cos(ang).T.astype(np.float32)), np.ascontiguousarray(np.sin(ang).T.astype(np.float32))


def decay_tables(head):
    lg = np.log1p(-np.exp2(np.float32(-5.0 - head))).astype(np.float32)
    pos = np.arange(128, dtype=np.float32)
    diff = pos[:, None] - pos[None, :]
    intra = np.where(diff >= 0, np.exp(np.maximum(diff, 0.0) * lg), 0.0).astype(np.float32)
    qd = np.exp((pos + 1.0) * lg).astype(np.float32)
    kd = np.exp((128 - 1.0 - pos) * lg).astype(np.float32)
    cdv = np.exp(np.float32(128) * lg).astype(np.float32)
    out = np.zeros((128, 258), np.float32)
    out[:, 0:128] = intra.T
    out[:, 128:256] = qd[None, :]
    out[:, 256] = kd
    out[:, 257] = cdv
    return out


CORES = list(range(8))
_PROGS = {}


def _prog(name, fn):
    if name not in _PROGS:
        _PROGS[name] = fn()
    return _PROGS[name]


def _run(nc, maps):
    return run_bass_kernel_spmd(nc, maps, core_ids=CORES).results


def _ca(a):
    return np.ascontiguousarray(a)


def fox_maps(xTf, w, bfv, gq, gk, wo):
    maps = []
    for c in CORES:
        b, g = c // 4, c % 4
        wl = np.concatenate([w[:, 512 * g:512 * g + 512], w[:, 2048 + 512 * g:2048 + 512 * g + 512],
                             w[:, 4096 + 512 * g:4096 + 512 * g + 512], w[:, 6144 + 512 * g:6144 + 512 * g + 512],
                             w[:, 8192 + 4 * g:8192 + 4 * g + 4]], axis=1)
        maps.append({"xT": xTf[b], "w_in": _ca(wl), "bf": _ca(bfv[4 * g:4 * g + 4].reshape(4, 1)),
                     "gq": _ca(gq.reshape(128, 1)), "gk": _ca(gk.reshape(128, 1)),
                     "gqr": _ca(gq.reshape(1, 128)), "gkr": _ca(gk.reshape(1, 128)),
                     "w_out": _ca(wo[512 * g:512 * g + 512, :])})
    return maps


def ret_wcols(w, g):
    cols = []
    for hp in range(2):
        h = 2 * g + hp
        cols += [w[:, h * 256:(h + 1) * 256], w[:, 2048 + h * 256:2048 + (h + 1) * 256],
                 w[:, 4096 + h * 512:4096 + (h + 1) * 512], w[:, 8192 + h * 512:8192 + (h + 1) * 512]]
    return _ca(np.concatenate(cols, 1))


def ret_maps(xTf, w, wo, cosT, sinT):
    maps = []
    for c in CORES:
        b, g = c // 4, c % 4
        maps.append({"xT": xTf[b], "w_in": ret_wcols(w, g), "cos": cosT, "sin": sinT,
                     "dec": _ca(np.stack([decay_tables(2 * g), decay_tables(2 * g + 1)], 0)),
                     "w_out": _ca(wo[1024 * g:1024 * (g + 1), :])})
    return maps


def kernel_unfused(x, fox_w_in, fox_b_f, fox_q_gain, fox_k_gain, fox_w_out, ret_w_in, ret_w_out,
                   ln_gain, ln_bias):
    x = np.asarray(x, np.float32)
    xs = [_ca(x[c // 4, (c % 4) * 1024:(c % 4 + 1) * 1024, :]) for c in CORES]
    cosT, sinT = rotary_tables()
    r = _run(_prog("D0", lambda: build_D(True, False)), [{"x_in": xs[c]} for c in CORES])
    xT = [r[c]["xT_out"] for c in CORES]
    for i in range(DEPTH):
        j = i // 2
        xTf = [_ca(np.stack(xT[4 * b:4 * b + 4], 0)) for b in range(2)]
        if i % 2 == 0:
            maps = fox_maps(xTf, np.asarray(fox_w_in[j]), np.asarray(fox_b_f[j]), np.asarray(fox_q_gain[j]),
                            np.asarray(fox_k_gain[j]), np.asarray(fox_w_out[j]))
            r = _run(_prog("Lfox", build_L_fox), maps)
        else:
            maps = ret_maps(xTf, np.asarray(ret_w_in[j]), np.asarray(ret_w_out[j]), cosT, sinT)
            r = _run(_prog("Lret", build_L_ret), maps)
        parts = [r[c]["part"] for c in CORES]
        last = (i == DEPTH - 1)
        maps = []
        for c in CORES:
            b, g = c // 4, c % 4
            pp = _ca(np.stack([parts[4 * b + q][g * 1024:(g + 1) * 1024] for q in range(4)], 0))
            maps.append({"x_in": xs[c], "parts": pp, "lng": _ca(np.asarray(ln_gain)[i:i + 1]),
                         "lnb": _ca(np.asarray(ln_bias)[i:i + 1])})
        r = _run(_prog("Dlast" if last else "D", (lambda: build_D(False, True)) if last else (lambda: build_D(False, False))), maps)
        xs = [r[c]["x_out"] for c in CORES]
        if not last:
            xT = [r[c]["xT_out"] for c in CORES]
    out = np.zeros((NB, S, D), np.float32)
    for c in CORES:
        out[c // 4, (c % 4) * 1024:(c % 4 + 1) * 1024, :] = xs[c]
    return out


def fused_maps(x, fox_w_in, fox_b_f, fox_q_gain, fox_k_gain, fox_w_out, ret_w_in, ret_w_out, ln_gain, ln_bias):
    cosT, sinT = rotary_tables()
    maps = []
    for c in CORES:
        b, g = c // 4, c % 4
        dsl = slice(g * DL, (g + 1) * DL)
        m = {"x_in": _ca(x[b][:, dsl]), "cos": cosT, "sin": sinT,
             "dec": _ca(np.stack([decay_tables(2 * g), decay_tables(2 * g + 1)], 0)),
             "lng": _ca(ln_gain[:, dsl]), "lnb": _ca(ln_bias[:, dsl])}
        for j in range(2):
            w = fox_w_in[j]
            m["fw_in%d" % j] = _ca(np.concatenate(
                [w[:, 512 * g:512 * g + 512], w[:, 2048 + 512 * g:2048 + 512 * g + 512],
                 w[:, 4096 + 512 * g:4096 + 512 * g + 512], w[:, 6144 + 512 * g:6144 + 512 * g + 512],
                 w[:, 8192 + 4 * g:8192 + 4 * g + 4]], axis=1))
            m["fbf%d" % j] = _ca(fox_b_f[j][4 * g:4 * g + 4].reshape(4, 1))
            m["fgq%d" % j] = _ca(fox_q_gain[j].reshape(128, 1))
            m["fgk%d" % j] = _ca(fox_k_gain[j].reshape(128, 1))
            m["fgqr%d" % j] = _ca(fox_q_gain[j].reshape(1, 128))
            m["fgkr%d" % j] = _ca(fox_k_gain[j].reshape(1, 128))
            m["fwo%d" % j] = _ca(fox_w_out[j][:, dsl])
            m["rw_in%d" % j] = ret_wcols(ret_w_in[j], g)
            m["rwo%d" % j] = _ca(ret_w_out[j][:, dsl])
        maps.append(m)
    return maps


def kernel(**inputs):
    inp = {k: np.asarray(v, np.float32) for k, v in inputs.items()}
    maps = fused_maps(**inp)
    nc = _prog("fused", build_fused)
    r = _run(nc, maps)
    out = np.zeros((NB, S, D), np.float32)
    for c in CORES:
        out[c // 4][:, (c % 4) * DL:(c % 4 + 1) * DL] = r[c]["out"]
    return out


GROUPS = [[0, 1, 2, 3], [4, 5, 6, 7]]
DL = 512


def phase_E(P, C, xres, mode, ysrc, ec, wo_d, lng_d, lnb_d, x_in, xs_l, xg_l, st_l, st_g, x_out, deps=(), ydeps=(), pre_wo=None):
    nc, A, K, PS = C.nc, C.arena, C.k, C.ps
    m0 = A.mark()
    outs = []
    xres = A.alloc([128, 32, DL], F32)
    t_xq = [None] * 4

    def load_x(q):
        t_xq[q] = dma(P, "sp", "ex%d" % q, xres[:, q * 8:(q + 1) * 8, :],
                      x_in[q * 1024:(q + 1) * 1024, :].rearrange("(j p) d -> p j d", p=128), deps)

    if mode == "init":
        for q_ in range(4):
            load_x(q_)
    C.t_xres = None
    if mode != "init":
        if pre_wo is None:
            Wo = A.alloc([128, ec, DL], BF16)
            t_w = load_w_bf16(P, "wE", Wo, wo_d, deps)
        else:
            Wo, t_w = pre_wo
        gb = A.alloc([128, DL], F32)
        bb = A.alloc([128, DL], F32)
        t_gb = dma(P, "sp", "egb", gb, lng_d.partition_broadcast(128), deps)
        t_gb = dma(P, "sp", "egb", bb, lnb_d.partition_broadcast(128), deps)
        Y = Rot([A.alloc([128, ec, 512], BF16) for _ in range(2)])
        S12 = [A.alloc([128, 16], F32) for _ in range(2)]
        SG_ = [A.alloc([128, 4, 16], F32) for _ in range(2)]
        ssum = A.alloc([128, 16], F32)
        mean = A.alloc([128, 8], F32)
        var = A.alloc([128, 8], F32)
        rstd = [A.alloc([128, 8], F32) for _ in range(2)]
        nmr = [A.alloc([128, 8], F32) for _ in range(2)]
        junk = A.alloc([128, DL], F32)
        tmpn = Rot([A.alloc([128, DL], F32) for _ in range(3)])
    xTs = Rot([A.alloc([128, 4, 1024], BF16) for _ in range(2)])
    pm = Rot([PS[0], PS[1], PS[2], PS[3]])
    pt = Rot([PS[4], PS[5]])
    st_free = [None, None]
    t_stats = [None] * 4
    t_h = [None] * 32

    def out_proj(q):
        for s2 in range(2):
            q2 = q * 2 + s2
            yi = Y.next()
            t_y = C.yload(P, "ey%d" % yi, q2, Y.bufs[yi], list(deps) + C.ydeps(q2) + [Y.free[yi]])
            if s2 == 0:
                load_x(q)
            for jj in range(4):
                j = q2 * 4 + jj
                b = pm.next()
                t_m = None
                for e_ in range(ec):
                    t_m = mm(P, pm.bufs[b], Y.bufs[yi][:, e_, jj * 128:(jj + 1) * 128], Wo[:, e_, :],
                             e_ == 0, e_ == ec - 1, [t_w, t_y, pm.free[b]])
                t_hh = stt(P, xres[:, j, :], xres[:, j, :], ALPHA, pm.bufs[b], ALU.mult, ALU.add, [t_m, t_xq[q]])
                pm.free[b] = t_hh
                jq = j % 8
                sb = S12[q % 2]
                t_a = P.op("act", (lambda o, i, a: lambda e: e.activation(out=o, in_=i, func=AF.Identity, accum_out=a))(
                    junk, xres[:, j, :], sb[:, jq:jq + 1]), [t_hh, st_free[q % 2] if jq == 0 else None])
                t_a = P.op("act", (lambda o, i, a: lambda e: e.activation(out=o, in_=i, func=AF.Square, accum_out=a))(
                    junk, xres[:, j, :], sb[:, 8 + jq:9 + jq]), [t_a])
                t_h[j] = t_a
            Y.free[yi] = t_m
        t1 = dma(P, "sp", "est%d" % (q % 2), st_l[q], S12[q % 2], [t_h[q * 8 + 7]])
        st_free[q % 2] = t1
        t2 = coll(P, "ccst", "AllGather", ALU.bypass, GROUPS, st_l[q], st_g[q], [t1])
        t3 = dma(P, "sp", "esg%d" % (q % 2), SG_[q % 2], st_g[q].rearrange("(r p) s -> p r s", p=128), [t2])
        t_stats[q] = t3

    def normalize(q):
        if mode != "init":
            G_ = SG_[q % 2]
            t_s = tt(P, "dve", ssum, G_[:, 0, :], G_[:, 1, :], ALU.add, [t_stats[q]])
            t_s = tt(P, "dve", ssum, ssum, G_[:, 2, :], ALU.add, [t_s])
            t_s = tt(P, "dve", ssum, ssum, G_[:, 3, :], ALU.add, [t_s])
            t_s = ts(P, "dve", mean, ssum[:, 0:8], 1.0 / D, None, ALU.mult, None, [t_s])
            t_s = tt(P, "dve", var, mean, mean, ALU.mult, [t_s])
            t_s = stt(P, var, ssum[:, 8:16], 1.0 / D, var, ALU.mult, ALU.subtract, [t_s])
            t_s = act(P, rstd[q % 2], var, AF.Sqrt, [t_s, C.t_eps], bias=C.eps_ln)
            t_s = P.op("dve", (lambda o: lambda e: e.reciprocal(out=o, in_=o))(rstd[q % 2]), [t_s])
            t_s = stt(P, nmr[q % 2], mean, -1.0, rstd[q % 2], ALU.mult, ALU.mult, [t_s])
        xi = xTs.next() if mode != "last" else None
        t_c = None
        t_fin = None
        st1 = {}

        def stage1(jq):
            j = q * 8 + jq
            if mode != "init":
                ti = tmpn.next()
                T = tmpn.bufs[ti]
                t_n = act(P, T, xres[:, j, :], AF.Identity, [t_s, tmpn.free[ti]],
                          bias=nmr[q % 2][:, jq:jq + 1], scale=rstd[q % 2][:, jq:jq + 1])
                t_n = tt(P, "dve", T, T, gb, ALU.mult, [t_n, t_gb])
                t_n = tt(P, "dve", xres[:, j, :], T, bb, ALU.add, [t_n])
                tmpn.free[ti] = t_n
                C.t_xres = t_n
                st1[jq] = t_n
            else:
                st1[jq] = t_xq[q]

        def stage2(jq):
            j = q * 8 + jq
            pi = pt.next()
            pb = pt.bufs[pi].rearrange("p (a b) -> p a b", b=128)
            t_t = None
            for c4 in range(4):
                t_t = tr(P, pb[:, c4, :], xres[:, j, c4 * 128:(c4 + 1) * 128], K.ident_f,
                         [st1[jq], K.tok, pt.free[pi]])
            t_cc = cp(P, "dve", xTs.bufs[xi][:, :, jq * 128:(jq + 1) * 128], pb,
                      [t_t, xTs.free[xi] if jq == 0 else None])
            pt.free[pi] = t_cc
            return t_cc

        for jq in range(8 + 2):
            if jq < 8:
                stage1(jq)
                t_fin = st1[jq]
            if mode != "last" and jq >= 2:
                t_c = stage2(jq - 2)
        if mode != "last":
            t1 = dma(P, "sp", "exs%d" % xi, xs_l[q].rearrange("(c p) t -> p c t", p=128), xTs.bufs[xi], [t_c])
            xTs.free[xi] = t1
            C.e_stores.append(t1)
            outs.append(coll(P, "ccx", "AllGather", ALU.bypass, GROUPS, xs_l[q], xg_l[q], [t1]))
            if mode == "mid":
                C.x_spill.append(dma(P, "sp", "espill", C.xsp[q * 1024:(q + 1) * 1024, :].rearrange("(j p) d -> p j d", p=128),
                                     xres[:, q * 8:(q + 1) * 8, :], [t_fin]))
        else:
            outs.append(dma(P, "sp", "eout", x_out[q * 1024:(q + 1) * 1024, :].rearrange("(j p) d -> p j d", p=128),
                            xres[:, q * 8:(q + 1) * 8, :], [t_fin]))

    if mode == "init":
        for q in range(4):
            normalize(q)
    else:
        out_proj(0)
        for q in range(1, 4):
            out_proj(q)
            normalize(q - 1)
        normalize(3)
    A.reset(m0)
    return outs


def build_fused(nlayers=DEPTH):
    nc = new_prog()
    def ext(name, shape, dt=F32):
        return nc.dram_tensor(name, shape, dt, kind="ExternalInput").ap()
    def scratch(name, shape, dt=BF16):
        return nc.dram_tensor(name, shape, dt).ap()
    x_in = ext("x_in", [S, DL])
    fox = []
    for j in range(2):
        f = Ctx()
        f.w = ext("fw_in%d" % j, [D, 2052]); f.bf = ext("fbf%d" % j, [4, 1])
        f.gq = ext("fgq%d" % j, [128, 1]); f.gk = ext("fgk%d" % j, [128, 1])
        f.gqr = ext("fgqr%d" % j, [1, 128]); f.gkr = ext("fgkr%d" % j, [1, 128])
        f.wo = ext("fwo%d" % j, [D, DL])
        fox.append(f)
    ret = []
    for j in range(2):
        r_ = Ctx()
        r_.w = ext("rw_in%d" % j, [D, 3072]); r_.wo = ext("rwo%d" % j, [2 * D, DL])
        ret.append(r_)
    cos_d = ext("cos", [128, S]); sin_d = ext("sin", [128, S]); dec_d = ext("dec", [2, 128, 258])
    lng = ext("lng", [DEPTH, DL]); lnb = ext("lnb", [DEPTH, DL])
    out = nc.dram_tensor("out", [S, DL], F32, kind="ExternalOutput").ap()
    fs = Scr()
    fs.qT = scratch("f_qT", [4, 128, S]); fs.kT = scratch("f_kT", [4, 128, S])
    fs.v = scratch("f_v", [4, 128, 32, 128]); fs.sgT = scratch("f_sgT", [4, 128, S])
    fs.yl = [scratch("f_yl%d" % h, [128, S]) for h in range(4)]
    fs.yg = [scratch("f_yg%d" % h, [512, S]) for h in range(4)]
    rs = Scr()
    rs.qT = scratch("r_qT", [2, 2, 128, S]); rs.kT = scratch("r_kT", [2, 2, 128, S])
    rs.v = scratch("r_v", [2, S, 512]); rs.sgT = scratch("r_sgT", [2, 4, 128, S])
    rs.yl = [[scratch("r_yl%d_%d" % (hp, q), [512, 1024]) for q in range(4)] for hp in range(2)]
    rs.yg = [[scratch("r_yg%d_%d" % (hp, q), [2048, 1024]) for q in range(4)] for hp in range(2)]
    xs_l = [scratch("xs%d" % q, [DL, 1024]) for q in range(4)]
    xg_l = [scratch("xg%d" % q, [D, 1024]) for q in range(4)]
    st_l = [scratch("stl%d" % q, [128, 16], F32) for q in range(4)]
    st_g = [scratch("stg%d" % q, [512, 16], F32) for q in range(4)]
    xsp = scratch("xsp", [S, DL], F32)

    def yload_fox(P, key, q2, Yb, deps):
        t = None
        Yv = Yb.rearrange("p (r h) t -> p r h t", h=4)
        for h in range(4):
            t = dma(P, "sp", key, Yv[:, :, h, :],
                    fs.yg[h][:, q2 * 512:(q2 + 1) * 512].rearrange("(r p) t -> p r t", p=128), deps)
        return t

    def yload_ret(P, key, q2, Yb, deps):
        t = None
        Yv = Yb.rearrange("p (r h v) t -> p r h v t", h=2, v=4)
        q, s2 = q2 // 2, q2 % 2
        for hp in range(2):
            for r in range(4):
                t = dma(P, "sp", key, Yv[:, r, hp, :, :],
                        rs.yg[hp][q][r * 512:(r + 1) * 512, s2 * 512:(s2 + 1) * 512].rearrange("(v p) t -> p v t", p=128),
                        deps)
        return t

    with contextlib.ExitStack() as st:
        C = setup(nc, st)
        P = Prog(nc)
        load_consts(P, C)
        C.xsp = xsp
        C.x_spill = []
        C.e_stores = []
        C.last_stores = []
        C.t_xres = None
        mtop = C.arena.mark()
        Wo_cur = C.arena.alloc([128, 16, DL], BF16)
        m_base = C.arena.mark()
        C.ncr = C.arena.alloc([4, S], F32)
        m_ncr = C.arena.mark()
        pre0 = preload_A_fox(P, C, fox[0].w)
        t_wo = load_w_bf16(P, "wE", Wo_cur, fox[0].wo)
        xg_tok = phase_E(P, C, None, "init", None, 0, None, None, None, x_in, xs_l, xg_l, st_l, st_g, None)
        P.barrier(extra=C.e_stores)
        C.e_stores = []
        outs = xg_tok
        for i in range(nlayers):
            j = i // 2
            last = (i == nlayers - 1)
            if i > 0:
                C.arena.reset(mtop)
                Wo_cur = C.arena.alloc([128, 16 if i % 2 == 0 else 32, DL], BF16)
                m_base = C.arena.mark()
            m0 = m_base
            if i % 2 == 0:
                f = fox[j]
                if i > 0:
                    C.ncr = C.arena.alloc([4, S], F32)
                    assert C.arena.mark() == m_ncr
                    pre_i = preload_A_fox(P, C, f.w)
                    t_wo = load_w_bf16(P, "wE", Wo_cur, f.wo)
                else:
                    pre_i = pre0
                m1 = m_ncr
                stA = phase_A_fox(P, C, xg_l, f.w, f.bf, f.gq, f.gk, fs, xdeps=xg_tok, pre=pre_i)
                P.barrier(extra=stA)
                C.arena.reset(m1)
                stB = phase_B_fox(P, C, f.gqr[0], f.gkr[0], fs, deps=stA)
                C.yload = yload_fox
                C.ydeps = (lambda toks: lambda q2: list(toks))(list(stB))
                ec, wo = 16, f.wo
            else:
                r_ = ret[j]
                stA = phase_A_ret(P, C, xg_l, r_.w, cos_d, sin_d, rs, xdeps=xg_tok,
                                  after_w=(lambda wo_=r_.wo, Wb_=Wo_cur: load_w_bf16(P, "wE", Wb_, wo_)))
                t_wo = C.t_wo
                P.barrier(extra=stA)
                stB = phase_B_ret(P, C, dec_d, rs, deps=stA)
                C.yload = yload_ret
                C.ydeps = (lambda toks: lambda q2: [toks[2 * (q2 // 2)], toks[2 * (q2 // 2) + 1]])(list(stB))
                ec, wo = 32, r_.wo
            assert len(stB) == (4 if i % 2 == 0 else 8)
            P.barrier(extra=C.last_stores)
            C.last_stores = []
            C.arena.reset(m0)
            spill = list(C.x_spill)
            C.x_spill = []
            outs = phase_E(P, C, None, "last" if last else "mid", None, ec, wo, lng[i], lnb[i],
                           x_in if i == 0 else xsp, xs_l, xg_l, st_l, st_g, out, deps=spill, pre_wo=(Wo_cur, t_wo))
            xg_tok = outs
            P.barrier(extra=list(C.x_spill) + list(C.e_stores))
            C.e_stores = []
        P.emit(final_waits=outs)
    return nc
```

```python
import contextlib
import numpy as np
import ml_dtypes
import concourse.bass as bass
import concourse.mybir as mybir
from concourse.bass_utils import run_bass_kernel_spmd

F32 = mybir.dt.float32
BF16 = mybir.dt.bfloat16
U8 = mybir.dt.uint8
AF = mybir.ActivationFunctionType
ALU = mybir.AluOpType
AX = mybir.AxisListType
NPBF = ml_dtypes.bfloat16

D = 2048
S = 4096
NB = 2
DEPTH = 4
TOKC = 1024
ALPHA = (2.0 * DEPTH) ** 0.25
LN_EPS = 1e-5
GN_EPS = 1e-6
QK_EPS = 1e-6
NEG = -30000.0

ENGS = ("pe", "act", "dve", "pool", "sp")


class Tok:
    __slots__ = ("eng", "idx", "sem", "val")

    def __init__(self, eng=None, idx=None, sem=None, val=None):
        self.eng, self.idx, self.sem, self.val = eng, idx, sem, val


class Prog:
    def __init__(self, nc):
        self.nc = nc
        self.ops = {e: [] for e in ENGS}
        self.dma_cnt = {}
        self.sig = {e: set() for e in ENGS}
        self.last = {e: None for e in ENGS}
        self.pending = {e: [] for e in ENGS}

    def _deps(self, eng, deps):
        deps = [d for d in deps if d is not None]
        if self.pending[eng]:
            deps = deps + self.pending[eng]
            self.pending[eng] = []
        for d in deps:
            if d.eng is not None and not (d.eng == "pe" and eng == "pe"):
                self.sig[d.eng].add(d.idx)
        return deps

    def op(self, eng, fn, deps=()):
        deps = self._deps(eng, deps)
        idx = len(self.ops[eng])
        self.ops[eng].append((fn, deps, None))
        t = Tok(eng=eng, idx=idx)
        self.last[eng] = t
        return t

    def dma(self, queue, semkey, fn, deps=(), inc=16):
        deps = self._deps(queue, deps)
        self.dma_cnt[semkey] = self.dma_cnt.get(semkey, 0) + inc
        self.ops[queue].append((fn, deps, (semkey, inc)))
        return Tok(sem=semkey, val=self.dma_cnt[semkey])

    def barrier(self, extra=()):
        best = {}
        for t in extra:
            if t is None:
                continue
            if t.sem is not None:
                if t.sem not in best or best[t.sem].val < t.val:
                    best[t.sem] = t
        toks = [t for t in self.last.values() if t is not None] + list(best.values())
        for e in ENGS:
            self.pending[e] = list(toks)

    def emit(self, final_waits=()):
        nc = self.nc
        with contextlib.ExitStack() as st:
            esem = {e: st.enter_context(nc.semaphore("s_" + e)) for e in ENGS}
            dsem = {k: st.enter_context(nc.semaphore("d_%s" % str(k))) for k in self.dma_cnt}
            sigval = {}
            for e in ENGS:
                c = 0
                for i in range(len(self.ops[e])):
                    if i in self.sig[e]:
                        c += 1
                        sigval[(e, i)] = c
            print("[prog] ops:", {e: len(self.ops[e]) for e in ENGS}, "signals:",
                  {e: len(self.sig[e]) for e in ENGS}, "dma sems:", len(self.dma_cnt),
                  "max dma cnt:", max(self.dma_cnt.values()) if self.dma_cnt else 0, flush=True)
            block = st.enter_context(nc.Block())

            def run(e, eng):
                waited = {}
                for i, (fn, deps, semkey) in enumerate(self.ops[e]):
                    for d in deps:
                        if d.eng is not None:
                            if d.eng == "pe" and e == "pe":
                                continue
                            key, v, s = ("e", d.eng), sigval[(d.eng, d.idx)], esem[d.eng]
                        else:
                            key, v, s = ("d", d.sem), d.val, dsem[d.sem]
                        if waited.get(key, 0) >= v:
                            continue
                        waited[key] = v
                        eng.wait_ge(s, v)
                    ins = fn(eng)
                    if semkey is not None:
                        ins.then_inc(dsem[semkey[0]], semkey[1])
                    elif (e, i) in sigval:
                        ins.then_inc(esem[e], 1)
                if e == "sp":
                    for t in final_waits:
                        eng.wait_ge(dsem[t.sem], t.val)

            block.tensor(lambda eng: run("pe", eng))
            block.scalar(lambda eng: run("act", eng))
            block.vector(lambda eng: run("dve", eng))
            block.gpsimd(lambda eng: run("pool", eng))
            block.sync(lambda eng: run("sp", eng))


DTSIZE = {F32: 4, BF16: 2, U8: 1}


class Arena:
    def __init__(self, ar, size):
        self.ar, self.size, self.off = ar, size, 0
        self.limit = size

    def mark(self):
        return self.off

    def reset(self, m):
        self.off = m

    def alloc(self, shape, dt, at=None):
        n = int(np.prod(shape[1:])) * DTSIZE[dt]
        if at is not None:
            off = at
            assert off + n <= self.size
        else:
            off = (self.off + 63) // 64 * 64
            assert off + n <= self.limit, ("SBUF arena overflow", off, n, self.limit)
            self.off = off + n
        ap = self.ar[0:shape[0], off:off + n].bitcast(dt)
        dims = list(shape[1:])
        if len(dims) > 1:
            names = "abcd"[:len(dims)]
            pat = "p (%s) -> p %s" % (" ".join(names), " ".join(names))
            kw = {names[i]: dims[i] for i in range(1, len(dims))}
            ap = ap.rearrange(pat, **kw)
        return ap


class Ctx:
    pass


def mm(P, out, lhsT, rhs, start, stop, deps=()):
    return P.op("pe", lambda e: e.matmul(out, lhsT, rhs, start=start, stop=stop), deps)


def tr(P, out, in_, ident, deps=()):
    return P.op("pe", lambda e: e.transpose(out, in_, ident), deps)


def act(P, out, in_, func, deps=(), bias=None, scale=None):
    kw = {}
    if bias is not None:
        kw["bias"] = bias
    if scale is not None:
        kw["scale"] = scale
    return P.op("act", lambda e: e.activation(out=out, in_=in_, func=func, **kw), deps)


def tt(P, eng, out, in0, in1, op, deps=()):
    return P.op(eng, lambda e: e.tensor_tensor(out=out, in0=in0, in1=in1, op=op), deps)


def ts(P, eng, out, in0, s1, s2, op0, op1=None, deps=()):
    if op1 is None:
        return P.op(eng, lambda e: e.tensor_scalar(out=out, in0=in0, scalar1=s1, scalar2=None, op0=op0), deps)
    return P.op(eng, lambda e: e.tensor_scalar(out=out, in0=in0, scalar1=s1, scalar2=s2, op0=op0, op1=op1), deps)


def stt(P, out, in0, scalar, in1, op0, op1, deps=()):
    return P.op("dve", lambda e: e.scalar_tensor_tensor(out=out, in0=in0, scalar=scalar, in1=in1, op0=op0, op1=op1), deps)


def cp(P, eng, out, in_, deps=()):
    if eng == "act":
        return P.op("act", lambda e: e.activation(out=out, in_=in_, func=AF.Copy), deps)
    return P.op(eng, lambda e: e.tensor_copy(out=out, in_=in_), deps)


def bnstats(P, out, in_, deps=()):
    return P.op("dve", lambda e: e.bn_stats(out=out, in_=in_), deps)


def bnaggr(P, out, in_, deps=()):
    return P.op("dve", lambda e: e.bn_aggr(out=out, in_=in_), deps)


def coll(P, key, kind, op, groups, in_, out, deps=()):
    return P.dma("pool", key, lambda e: e.collective_compute(kind, op, replica_groups=groups, ins=[in_], outs=[out]),
                 deps, inc=1)


def dma(P, q, key, out, in_, deps=()):
    return P.dma(q, key, lambda e: e.dma_start(out=out, in_=in_), deps)


def load_consts(P, C):
    nc, A = C.nc, C.arena
    ident = np.eye(128, dtype=np.float32)
    kk = np.arange(128)[:, None]
    qq = np.arange(128)[None, :]
    negmask = np.where(kk > qq, NEG, 0.0).astype(np.float32)
    sel = np.zeros((4, 4, 128), np.float32)
    for h in range(4):
        sel[h, h, :] = 1.0
    c = Ctx()
    c.ident_bf = A.alloc([128, 128], BF16)
    c.ones_bf = A.alloc([128, 128], BF16)
    c.negmask_bf = A.alloc([128, 128], BF16)
    c.ident_f = A.alloc([128, 128], F32)
    c.sel = A.alloc([4, 4, 128], F32)
    toks = []
    d_id = nc.inline_tensor(ident.astype(NPBF), "c_identbf").ap()
    d_on = nc.inline_tensor(np.ones((128, 128), NPBF), "c_onesbf").ap()
    d_nm = nc.inline_tensor(negmask.astype(NPBF), "c_negmask").ap()
    d_if = nc.inline_tensor(ident, "c_identf").ap()
    d_sel = nc.inline_tensor(sel.transpose(1, 0, 2).copy(), "c_sel").ap()
    toks.append(dma(P, "sp", "const", c.ident_bf, d_id[:, :]))
    toks.append(dma(P, "sp", "const", c.ones_bf, d_on[:, :]))
    toks.append(dma(P, "sp", "const", c.negmask_bf, d_nm[:, :]))
    toks.append(dma(P, "sp", "const", c.ident_f, d_if[:, :]))
    toks.append(dma(P, "sp", "const", c.sel, d_sel[:, :, :]))
    c.tok = toks[-1]
    C.k = c
    C.eps_qk = A.alloc([128, 1], F32)
    C.eps_ln = A.alloc([128, 1], F32)
    C.eps_gn = A.alloc([128, 1], F32)
    P.op("pool", lambda e: e.memset(C.eps_qk, 128.0 * QK_EPS))
    P.op("pool", lambda e: e.memset(C.eps_ln, LN_EPS))
    C.t_eps = P.op("pool", lambda e: e.memset(C.eps_gn, GN_EPS))


def phase_D(P, C, xres, x_in, parts, nparts, lng, lnb, xT_out, x_out, deps=()):
    nc, A, K, PS = C.nc, C.arena, C.k, C.ps
    m0 = A.mark()
    out_toks = []
    t_x = None
    if x_in is not None:
        t_x = dma(P, "sp", "dx", xres, x_in.rearrange("(j p) d -> p j d", p=128), deps)
    if parts is not None:
        gb = A.alloc([128, D], F32)
        bb = A.alloc([128, D], F32)
        t_g = dma(P, "sp", "dgb", gb, lng.partition_broadcast(128), deps)
        t_b = dma(P, "sp", "dgb", bb, lnb.partition_broadcast(128), deps)
        zt = Rot([A.alloc([128, D], F32) for _ in range(3)])
        hb = [A.alloc([128, D], F32) for _ in range(2)]
        st6 = A.alloc([128, 4, 6], F32)
        mv = A.alloc([128, 2], F32)
        rstd = A.alloc([128, 1], F32)
        nmr = A.alloc([128, 1], F32)
    xb = [A.alloc([128, D], BF16) for _ in range(2)]
    xTs = A.alloc([128, 16, TOKC], BF16) if xT_out is not None else None
    z_free = [None, None]
    h_free = [None, None]
    xb_free = [None, None]
    ps_free = [None, None]
    t_last_xres = None
    for j in range(8):
        b2 = j % 2
        if parts is not None:
            t_h = None
            for r in range(nparts):
                zi = zt.next()
                tz = dma(P, "sp", "dz%d" % zi, zt.bufs[zi], parts[r, j * 128:(j + 1) * 128, :],
                         list(deps) + [zt.free[zi]])
                if r == 0:
                    t_h = stt(P, hb[b2], xres[:, j, :], ALPHA, zt.bufs[zi], ALU.mult, ALU.add,
                              [t_x, tz, h_free[b2]])
                else:
                    t_h = tt(P, "pool", hb[b2], hb[b2], zt.bufs[zi], ALU.add, [t_h, tz])
                zt.free[zi] = t_h
            t_s = None
            for q in range(4):
                t_s = bnstats(P, st6[:, q, :], hb[b2][:, q * 512:(q + 1) * 512], [t_h, t_s])
            t_s = bnaggr(P, mv, st6, [t_s])
            t_s = act(P, rstd, mv[:, 1:2], AF.Sqrt, [t_s, C.t_eps], bias=C.eps_ln)
            t_s = P.op("dve", lambda e: e.reciprocal(out=rstd, in_=rstd), [t_s])
            t_s = ts(P, "dve", nmr, mv[:, 0:1], rstd, -1.0, ALU.mult, ALU.mult, [t_s])
            t_n = act(P, hb[b2], hb[b2], AF.Identity, [t_s, t_h], bias=nmr, scale=rstd)
            t_n = tt(P, "dve", hb[b2], hb[b2], gb, ALU.mult, [t_n, t_b])
            t_n = tt(P, "pool", xres[:, j, :], hb[b2], bb, ALU.add, [t_n, t_b, t_x])
            h_free[b2] = t_n
            t_xj = t_n
        else:
            t_xj = t_x
        t_last_xres = t_xj
        if xT_out is not None:
            t_c = cp(P, "act", xb[b2], xres[:, j, :], [t_xj, xb_free[b2]])
            for g in range(2):
                pb = PS[g].bitcast(BF16).rearrange("p (a b) -> p a b", b=128)
                t_t = None
                for c8 in range(8):
                    cc = g * 8 + c8
                    t_t = tr(P, pb[:, c8, :], xb[b2][:, cc * 128:(cc + 1) * 128], K.ident_bf,
                             [t_c, K.tok, ps_free[g]])
                ps_free[g] = cp(P, "dve", xTs[:, g * 8:(g + 1) * 8, j * 128:(j + 1) * 128], pb, [t_t])
            xb_free[b2] = t_t
    if xT_out is not None:
        out_toks.append(dma(P, "sp", "dxT", xT_out.rearrange("(c p) t -> p c t", p=128), xTs,
                            [ps_free[0], ps_free[1]]))
    if x_out is not None:
        out_toks.append(dma(P, "sp", "dxo", x_out.rearrange("(j p) d -> p j d", p=128), xres,
                            [t_last_xres]))
    A.reset(m0)
    return out_toks


def load_w_bf16(P, key, dst, src, deps=()):
    return dma(P, "pool", key, dst, src.rearrange("(c p) n -> p c n", p=128), deps)


class Rot:
    def __init__(self, bufs):
        self.bufs = bufs
        self.free = [None] * len(bufs)
        self.i = 0

    def next(self):
        k = self.i % len(self.bufs)
        self.i += 1
        return k


def preload_A_fox(P, C, w_d, deps=()):
    A = C.arena
    W = A.alloc([128, 16, 2048], BF16)
    Wf = A.alloc([128, 16, 4], BF16)
    tw = []
    for s in range(4):
        tw.append(load_w_bf16(P, "wA%d" % s, W[:, :, s * 512:(s + 1) * 512], w_d[:, s * 512:(s + 1) * 512], deps))
    twf = load_w_bf16(P, "wAf", Wf, w_d[:, 2048:2052], deps)
    return (W, Wf, tw, twf)


def phase_A_fox(P, C, xT_d, w_d, bf_d, gq_d, gk_d, scr, deps=(), xdeps=None, pre=None):
    nc, A, K, PS = C.nc, C.arena, C.k, C.ps
    if pre is None:
        pre = preload_A_fox(P, C, w_d, deps)
    W, Wf, tw, twf = pre
    gq = A.alloc([128, 1], F32)
    gk = A.alloc([128, 1], F32)
    bfc = A.alloc([4, 1], F32)
    t_small = dma(P, "sp", "smallA", gq, gq_d, deps)
    t_small = dma(P, "sp", "smallA", gk, gk_d, deps)
    t_small = dma(P, "sp", "smallA", bfc, bf_d, deps)
    t_gk = ts(P, "dve", gk, gk, float(np.sqrt(128.0)), None, ALU.mult, None, [t_small])
    xt = Rot([A.alloc([128, 16, 512], BF16) for _ in range(2)])
    sq = Rot([A.alloc([128, 512], BF16) for _ in range(2)])
    rr = Rot([A.alloc([128, 512], F32) for _ in range(2)])
    st_qk = Rot([A.alloc([128, 512], BF16) for _ in range(3)])
    st_g = Rot([A.alloc([128, 512], BF16) for _ in range(2)])
    st_v = Rot([A.alloc([128, 4, 4, 128], BF16) for _ in range(2)])
    fz = A.alloc([4, 512], F32)
    fa = A.alloc([4, 512], F32)
    fm = A.alloc([4, 512], F32)
    pm = Rot([PS[0], PS[1], PS[2], PS[3]])
    pq = Rot([PS[4], PS[5]])
    pf = PS[6]
    pf_free = [None]
    f_free = [None]
    stores = []
    ncr = C.ncr
    pending_ep = []

    def flush():
        while pending_ep:
            pending_ep.pop(0)()

    for tt_i in range(8):
        r, half = tt_i // 2, tt_i % 2
        xk = xt.next()
        X = xt.bufs[xk]
        t_xt = dma(P, "sp", "xt%d" % xk, X,
                   xT_d[r][:, half * 512:(half + 1) * 512].rearrange("(c p) t -> p c t", p=128),
                   list(deps) + [xt.free[xk]] + ([xdeps[r]] if xdeps else []))
        last_pe = None
        units = [("q", h) for h in range(4)] + [("k", h) for h in range(4)] + [("g", h) for h in range(4)]
        for kind, h in units:
            col0 = {"q": 0, "k": 512, "g": 1536}[kind] + h * 128
            wtok = tw[{"q": 0, "k": 1, "g": 3}[kind]]
            b = pm.next()
            pacc = pm.bufs[b]
            t_m = None
            for c in range(16):
                t_m = mm(P, pacc, W[:, c, col0:col0 + 128], X[:, c, :], c == 0, c == 15,
                         [t_xt, wtok, pm.free[b]])
            last_pe = t_m
            flush()
            if kind == "g":
                sb = st_g.next()
                t_e = act(P, st_g.bufs[sb], pacc, AF.Silu, [t_m, st_g.free[sb]])
                pm.free[b] = t_e
                t_st = dma(P, "sp", "stg%d" % sb, scr.sgT[h, :, tt_i * 512:(tt_i + 1) * 512], st_g.bufs[sb], [t_e])
                st_g.free[sb] = t_st
                stores.append(t_st)
            else:
                s_i = sq.next()
                t_sq = act(P, sq.bufs[s_i], pacc, AF.Square, [t_m, sq.free[s_i]])

                def ep(kind=kind, h=h, pacc=pacc, b=b, s_i=s_i, t_sq=t_sq, tt_i=tt_i):
                    qb = pq.next()
                    t_q = mm(P, pq.bufs[qb], K.ones_bf, sq.bufs[s_i], True, True, [t_sq, K.tok, pq.free[qb]])
                    sq.free[s_i] = t_q
                    ri = rr.next()
                    t_r = act(P, rr.bufs[ri], pq.bufs[qb], AF.Sqrt, [t_q, rr.free[ri], C.t_eps], bias=C.eps_qk)
                    pq.free[qb] = t_r
                    t_r = P.op("dve", (lambda o: lambda e: e.reciprocal(out=o, in_=o))(rr.bufs[ri]), [t_r])
                    si = st_qk.next()
                    gcol = gq if kind == "q" else gk
                    t_n = stt(P, st_qk.bufs[si], pacc, gcol, rr.bufs[ri], ALU.mult, ALU.mult,
                              [t_r, t_gk, st_qk.free[si]])
                    rr.free[ri] = t_n
                    pm.free[b] = t_n
                    dst = (scr.qT if kind == "q" else scr.kT)[h, :, tt_i * 512:(tt_i + 1) * 512]
                    t_st = dma(P, "sp", "stqk%d" % si, dst, st_qk.bufs[si], [t_n])
                    st_qk.free[si] = t_st
                    stores.append(t_st)
                pending_ep.append(ep)
        vi = st_v.next()
        VS = st_v.bufs[vi]
        t_ev = None
        for jj in range(4):
            b = pm.next()
            pacc = pm.bufs[b]
            t_m = None
            for c in range(16):
                t_m = mm(P, pacc, X[:, c, jj * 128:(jj + 1) * 128], W[:, c, 1024:1536], c == 0, c == 15,
                         [t_xt, tw[2], pm.free[b]])
            flush()
            t_ev = cp(P, "act", VS[:, :, jj, :], pacc.rearrange("p (h d) -> p h d", d=128),
                      [t_m, st_v.free[vi] if jj == 0 else None])
            pm.free[b] = t_ev
        t_st = dma(P, "sp", "stv%d" % vi, scr.v[:, :, tt_i * 4:(tt_i + 1) * 4, :].rearrange("h p j d -> p h j d"),
                   VS, [t_ev])
        st_v.free[vi] = t_st
        stores.append(t_st)
        t_m = None
        for c in range(16):
            t_m = mm(P, pf[0:4, :], Wf[:, c, :], X[:, c, :], c == 0, c == 15, [t_xt, twf, pf_free[0]])
        last_pe = t_m
        xt.free[xk] = t_m
        t_z = ts(P, "dve", fz, pf[0:4, :], bfc, None, ALU.add, None, [t_m, t_small, f_free[0]])
        pf_free[0] = t_z
        t_a = act(P, fa, fz, AF.Abs, [t_z])
        t_a = act(P, fa, fa, AF.Exp, [t_a], scale=-1.0)
        t_a = act(P, fa, fa, AF.Ln, [t_a], bias=1.0)
        t_mx = ts(P, "dve", fm, fz, -1.0, 0.0, ALU.mult, ALU.max, [t_z])
        t_f = tt(P, "dve", ncr[:, tt_i * 512:(tt_i + 1) * 512], fm, fa, ALU.add, [t_mx, t_a])
        f_free[0] = t_f
    flush()
    C.t_ncr = t_f
    return stores


def phase_B_fox(P, C, gqrow_d, gkrow_d, scr, deps=()):
    nc, A, K, PS = C.nc, C.arena, C.k, C.ps
    ncr = C.ncr
    ones1 = A.alloc([4, 1], F32)
    ncc = A.alloc([4, S], F32)
    t0 = P.op("pool", lambda e: e.memset(ones1, 1.0), list(deps))
    onesr = ones1.to_broadcast([4, S])
    t_scan = P.op("dve", lambda e: e.tensor_tensor_scan(out=ncc, data0=onesr, data1=ncr, initial=0.0,
                                                        op0=ALU.mult, op1=ALU.add), [t0, C.t_ncr])
    gqb = A.alloc([128, 128], F32)
    gkb = A.alloc([128, 128], F32)
    t_g = dma(P, "sp", "gB", gqb, gqrow_d.partition_broadcast(128), deps)
    t_g = dma(P, "sp", "gB", gkb, gkrow_d.partition_broadcast(128), deps)
    mq = A.alloc([128, 1], F32)
    mk = A.alloc([128, 1], F32)
    negM = A.alloc([128, 1], F32)
    t_m = act(P, gqb, gqb, AF.Abs, [t_g])
    t_m = P.op("dve", lambda e: e.reduce_max(out=mq, in_=gqb, axis=AX.X), [t_m])
    t_m = act(P, gkb, gkb, AF.Abs, [t_m])
    t_m = P.op("dve", lambda e: e.reduce_max(out=mk, in_=gkb, axis=AX.X), [t_m])
    t_m = ts(P, "dve", negM, mq, mk, -float(128.0 ** 0.5), ALU.mult, ALU.mult, [t_m])
    biasK = A.alloc([128, 32, 4], F32)
    pT = PS[7]
    t_t = None
    for j in range(32):
        t_t = tr(P, pT[:, j * 4:(j + 1) * 4], ncc[:, j * 128:(j + 1) * 128], K.ident_f[0:4, 0:4],
                 [t_scan, K.tok])
    t_bk = ts(P, "dve", biasK, pT[:, 0:128].rearrange("p (j h) -> p j h", h=4), negM, None, ALU.add, None,
              [t_t, t_m])
    hb = []
    for i in range(2):
        h_ = Ctx()
        h_.q = A.alloc([128, S], BF16)
        h_.k = A.alloc([128, S], BF16)
        h_.v = A.alloc([128, 32, 128], BF16)
        h_.sg = A.alloc([128, S], BF16)
        h_.ncq = A.alloc([128, S], F32)
        h_.free = None
        hb.append(h_)
    sa = Rot([A.alloc([128, 512], F32) for _ in range(4)])
    pt = Rot([A.alloc([128, 512], BF16) for _ in range(4)])
    rl = A.alloc([128, 512], F32)
    of = A.alloc([128, 512], F32)
    ys = Rot([A.alloc([128, 512], BF16) for _ in range(2)])
    pS = Rot([PS[0], PS[1], PS[2], PS[7]])
    pS.free[3] = t_bk
    pO = [PS[3], PS[4]]
    pL = [PS[5], PS[6]]
    pOL_free = [None, None]
    rl_free = [None]
    stores = []
    hstores = []
    LAG = 3

    pairs = []
    for h in range(4):
        for t in range(8):
            for j in range(4 * t + 4):
                pairs.append((h, t, j))
    loaded = {}
    pv_q = []

    def load_head(h):
        H = hb[h % 2]
        dd = list(deps) + [H.free]
        tl = dma(P, "sp", "hq%d" % (h % 2), H.q, scr.qT[h], dd)
        tl = dma(P, "sp", "hq%d" % (h % 2), H.k, scr.kT[h], dd)
        tl = dma(P, "sp", "hq%d" % (h % 2), H.v, scr.v[h], dd)
        tl = dma(P, "sp", "hq%d" % (h % 2), H.sg, scr.sgT[h], dd)
        t_c = None
        for t8 in range(8):
            b = pS.next()
            t_b = mm(P, pS.bufs[b], K.sel[:, h, :], ncc[:, t8 * 512:(t8 + 1) * 512], True, True,
                     [t_scan, K.tok, pS.free[b], H.free])
            t_c = cp(P, "act", H.ncq[:, t8 * 512:(t8 + 1) * 512], pS.bufs[b], [t_b, H.free])
            pS.free[b] = t_c
        loaded[h] = (tl, t_c)

    def front(h, t, j):
        H = hb[h % 2]
        tl, t_c = loaded[h]
        diag = j >= 4 * t
        off = (j - 4 * t) * 128 if diag else 0
        N = 512 - off
        q0 = t * 512 + off
        b = pS.next()
        ps = pS.bufs[b]
        t_s = mm(P, ps[:, 0:N], H.k[:, j * 128:(j + 1) * 128], H.q[:, q0:q0 + N], True, not diag,
                 [tl, pS.free[b]])
        if diag:
            t_s = mm(P, ps[:, 0:128], K.ident_bf, K.negmask_bf, False, True, [K.tok])
        si = sa.next()
        t_d = tt(P, "dve", sa.bufs[si][:, 0:N], ps[:, 0:N], H.ncq[:, q0:q0 + N], ALU.subtract,
                 [t_s, t_c, sa.free[si]])
        pS.free[b] = t_d
        pi = pt.next()
        t_e = act(P, pt.bufs[pi][:, 0:N], sa.bufs[si][:, 0:N], AF.Exp, [t_d, t_bk, pt.free[pi]],
                  bias=biasK[:, j, h:h + 1])
        sa.free[si] = t_e
        return (h, t, j, off, N, pi, t_e)

    def back(h, t, j, off, N, pi, t_e):
        H = hb[h % 2]
        ob = t % 2
        last = (j == 4 * t + 3)
        t_o = mm(P, pO[ob][:, off:512], H.v[:, j, :], pt.bufs[pi][:, 0:N], j == 0, last,
                 [t_e, pOL_free[ob] if j == 0 else None])
        t_l = mm(P, pL[ob][:, off:512], K.ones_bf, pt.bufs[pi][:, 0:N], j == 0, last, [])
        pt.free[pi] = t_l
        if last:
            t_r = P.op("dve", lambda e: e.reciprocal(out=rl, in_=pL[ob]), [t_l, rl_free[0]])
            t_f = tt(P, "dve", of, pO[ob], rl, ALU.mult, [t_r])
            pOL_free[ob] = t_f
            yi = ys.next()
            t_y = tt(P, "dve", ys.bufs[yi], of, H.sg[:, t * 512:(t + 1) * 512], ALU.mult,
                     [t_f, ys.free[yi], loaded[h][0]])
            rl_free[0] = t_y
            t_st = dma(P, "sp", "sty%d" % yi, scr.yl[h][:, t * 512:(t + 1) * 512], ys.bufs[yi], [t_y])
            ys.free[yi] = t_st
            hstores.append(t_st)
            C.last_stores.append(t_st)
            if t == 7:
                H.free = t_y
                if scr.yg is not None:
                    stores.append(coll(P, "ccy", "AllGather", ALU.bypass, GROUPS,
                                       scr.yl[h].rearrange("p (a t) -> (p a) t", a=4),
                                       scr.yg[h].rearrange("p (a t) -> (p a) t", a=4), list(hstores)))
                else:
                    stores.extend(hstores)
                del hstores[:]

    load_head(0)
    for i, (h, t, j) in enumerate(pairs):
        if t == 0 and j == 0 and h + 1 < 4:
            pass
        pv_q.append(front(h, t, j))
        if len(pv_q) > LAG:
            back(*pv_q.pop(0))
        if t == 1 and j == 0 and h + 1 < 4:
            while pv_q and pv_q[0][0] < h:
                back(*pv_q.pop(0))
            load_head(h + 1)
    while pv_q:
        back(*pv_q.pop(0))
    return stores


def phase_C(P, C, yT_d, wo_d, part_out, ec, deps=()):
    nc, A, K, PS = C.nc, C.arena, C.k, C.ps
    Wo = A.alloc([128, ec, D], BF16)
    t_w = load_w_bf16(P, "wC", Wo, wo_d, deps)
    Y = A.alloc([128, ec, S], BF16)
    t_y = []
    for q in range(4):
        t_y.append(dma(P, "sp", "yC%d" % q, Y[:, :, q * 1024:(q + 1) * 1024],
                       yT_d[:, :, q * 1024:(q + 1) * 1024].rearrange("e p t -> p e t"), deps))
    ost = Rot([A.alloc([128, D], F32) for _ in range(2)])
    pm = Rot([PS[0], PS[1], PS[2], PS[3]])
    stores = []
    n = 0
    for j in range(32):
        oi = ost.next()
        O = ost.bufs[oi]
        t_e = None
        for dt in range(4):
            b = pm.next()
            t_m = None
            for e_ in range(ec):
                t_m = mm(P, pm.bufs[b], Y[:, e_, j * 128:(j + 1) * 128], Wo[:, e_, dt * 512:(dt + 1) * 512],
                         e_ == 0, e_ == ec - 1, [t_w, t_y[j // 8], pm.free[b]])
            eng = "act" if n % 2 == 0 else "dve"
            n += 1
            t_e = cp(P, eng, O[:, dt * 512:(dt + 1) * 512], pm.bufs[b], [t_m, ost.free[oi] if dt == 0 else None])
            pm.free[b] = t_e
            if dt == 2:
                t_e2 = t_e
        t_st = dma(P, "sp", "stC%d" % oi, part_out[j * 128:(j + 1) * 128, :], O, [t_e, t_e2])
        ost.free[oi] = t_st
        stores.append(t_st)
    return stores


ARENA_BYTES = 204 * 1024


def new_prog():
    nc = bass.Bass("TRN2", target_bir_lowering=False)
    return nc


class Scr:
    pass


def setup(nc, st):
    C = Ctx()
    C.nc = nc
    ar = st.enter_context(nc.sbuf_tensor("arena", [128, ARENA_BYTES], U8))
    C.arena = Arena(ar, ARENA_BYTES)
    C.ps = [st.enter_context(nc.psum_tensor("ps%d" % i, [128, 512], F32)) for i in range(8)]
    C.ps = [p[:, :] for p in C.ps]
    return C


def build_D(first, last):
    nc = new_prog()
    x_in = nc.dram_tensor("x_in", [TOKC, D], F32, kind="ExternalInput").ap()
    if not first:
        parts = nc.dram_tensor("parts", [4, TOKC, D], F32, kind="ExternalInput").ap()
        lng = nc.dram_tensor("lng", [1, D], F32, kind="ExternalInput").ap()
        lnb = nc.dram_tensor("lnb", [1, D], F32, kind="ExternalInput").ap()
    xT_out = None if last else nc.dram_tensor("xT_out", [D, TOKC], BF16, kind="ExternalOutput").ap()
    x_out = None if first else nc.dram_tensor("x_out", [TOKC, D], F32, kind="ExternalOutput").ap()
    with contextlib.ExitStack() as st:
        C = setup(nc, st)
        P = Prog(nc)
        load_consts(P, C)
        xres = C.arena.alloc([128, 8, D], F32)
        if first:
            outs = phase_D(P, C, xres, x_in, None, 0, None, None, xT_out, None)
        else:
            outs = phase_D(P, C, xres, x_in, parts, 4, lng[0], lnb[0], xT_out, x_out)
        P.emit(final_waits=outs)
    return nc


def build_L_fox():
    nc = new_prog()
    xT_d = nc.dram_tensor("xT", [4, D, TOKC], BF16, kind="ExternalInput").ap()
    w_d = nc.dram_tensor("w_in", [D, 2052], F32, kind="ExternalInput").ap()
    bf_d = nc.dram_tensor("bf", [4, 1], F32, kind="ExternalInput").ap()
    gq_d = nc.dram_tensor("gq", [128, 1], F32, kind="ExternalInput").ap()
    gk_d = nc.dram_tensor("gk", [128, 1], F32, kind="ExternalInput").ap()
    gqr_d = nc.dram_tensor("gqr", [1, 128], F32, kind="ExternalInput").ap()
    gkr_d = nc.dram_tensor("gkr", [1, 128], F32, kind="ExternalInput").ap()
    wo_d = nc.dram_tensor("w_out", [512, D], F32, kind="ExternalInput").ap()
    part = nc.dram_tensor("part", [S, D], F32, kind="ExternalOutput").ap()
    scr = Scr()
    scr.qT = nc.dram_tensor("s_qT", [4, 128, S], BF16).ap()
    scr.kT = nc.dram_tensor("s_kT", [4, 128, S], BF16).ap()
    scr.v = nc.dram_tensor("s_v", [4, 128, 32, 128], BF16).ap()
    scr.sgT = nc.dram_tensor("s_sgT", [4, 128, S], BF16).ap()
    scr.yT = nc.dram_tensor("s_yT", [4, 128, S], BF16).ap()
    with contextlib.ExitStack() as st:
        C = setup(nc, st)
        P = Prog(nc)
        load_consts(P, C)
        C.ncr = C.arena.alloc([4, S], F32)
        m0 = C.arena.mark()
        stA = phase_A_fox(P, C, xT_d, w_d, bf_d, gq_d, gk_d, scr)
        P.barrier()
        C.arena.reset(m0)
        stB = phase_B_fox(P, C, gqr_d[0], gkr_d[0], scr, deps=stA)
        P.barrier()
        C.arena.reset(m0)
        stC = phase_C(P, C, scr.yT, wo_d, part, 4, deps=stB)
        P.emit(final_waits=stC)
    return nc


def load_ret_w(P, Wdst, w_d, hp_, deps=()):
    tw_ = []
    for s_ in range(3):
        tw_.append(load_w_bf16(P, "wR%d_%d" % (hp_, s_), Wdst[:, :, s_ * 512:(s_ + 1) * 512],
                               w_d[:, hp_ * 1536 + s_ * 512:hp_ * 1536 + (s_ + 1) * 512], list(deps)))
    return tw_


def phase_A_ret(P, C, xT_d, w_d, cos_d, sin_d, scr, deps=(), xdeps=None, after_w=None, pre_w0=None):
    nc, A, K, PS = C.nc, C.arena, C.k, C.ps
    m0 = A.mark()
    stores = []
    if pre_w0 is None:
        Wb = [A.alloc([128, 16, 1536], BF16) for _ in range(2)]
    else:
        Wb = [pre_w0[0], A.alloc([128, 16, 1536], BF16)]
    xt = Rot([A.alloc([128, 16, 512], BF16) for _ in range(2)])
    cs = Rot([A.alloc([128, 2, 512], F32) for _ in range(2)])
    twb = []
    for hp_ in range(2):
        if hp_ == 0 and pre_w0 is not None:
            twb.append(pre_w0[1])
            continue
        twb.append(load_ret_w(P, Wb[hp_], w_d, hp_, deps))
    if after_w is not None:
        C.t_wo = after_w()
    tmp = Rot([A.alloc([128, 2, 512], F32) for _ in range(2)])
    st_r = Rot([A.alloc([128, 512], BF16) for _ in range(4)])
    st_g = Rot([A.alloc([128, 512], BF16) for _ in range(2)])
    st_v = Rot([A.alloc([128, 4, 512], BF16) for _ in range(2)])
    pm = Rot([PS[i] for i in range(6)])
    w_free = None
    for hp in range(2):
        tw = twb[hp]
        W = Wb[hp]
        for tt_i in range(8):
            r, half = tt_i // 2, tt_i % 2
            xk = xt.next()
            X = xt.bufs[xk]
            t_xt = dma(P, "sp", "xt%d" % xk, X,
                       xT_d[r][:, half * 512:(half + 1) * 512].rearrange("(c p) t -> p c t", p=128),
                       list(deps) + [xt.free[xk]] + ([xdeps[r]] if xdeps else []))
            ci = cs.next()
            CS = cs.bufs[ci]
            t_cs = dma(P, "sp", "cs%d" % ci, CS[:, 0, :], cos_d[:, tt_i * 512:(tt_i + 1) * 512], list(deps) + [cs.free[ci]])
            t_cs = dma(P, "sp", "cs%d" % ci, CS[:, 1, :], sin_d[:, tt_i * 512:(tt_i + 1) * 512], list(deps) + [cs.free[ci]])
            t_last_cs = None
            for kind in ("q", "k"):
                col0 = 0 if kind == "q" else 256
                scale = 1.0 if kind == "q" else 1.0 / 16.0
                banks = []
                t_m = None
                for hf in range(2):
                    b = pm.next()
                    banks.append(b)
                    for c in range(16):
                        t_m = mm(P, pm.bufs[b], W[:, c, col0 + hf * 128:col0 + (hf + 1) * 128], X[:, c, :],
                                 c == 0, c == 15, [t_xt, tw[0], pm.free[b]])
                pa, pb = pm.bufs[banks[0]], pm.bufs[banks[1]]
                dst = scr.qT if kind == "q" else scr.kT
                for oi, (ca, cb, op) in enumerate(((0, 1, ALU.subtract), (1, 0, ALU.add))):
                    ti = tmp.next()
                    T = tmp.bufs[ti]
                    t1 = stt(P, T[:, 0, :], pa, scale, CS[:, ca, :], ALU.mult, ALU.mult, [t_m, t_cs, tmp.free[ti]])
                    t2 = stt(P, T[:, 1, :], pb, scale, CS[:, cb, :], ALU.mult, ALU.mult, [t_m, t_cs])
                    si = st_r.next()
                    t3 = tt(P, "pool", st_r.bufs[si], T[:, 0, :], T[:, 1, :], op, [t1, t2, st_r.free[si]])
                    tmp.free[ti] = t3
                    t_st = dma(P, "sp", "str%d" % si, dst[hp, oi, :, tt_i * 512:(tt_i + 1) * 512], st_r.bufs[si], [t3])
                    st_r.free[si] = t_st
                    stores.append(t_st)
                pm.free[banks[0]] = t2
                pm.free[banks[1]] = t2
                t_last_cs = t2
            cs.free[ci] = t_last_cs
            for gt in range(4):
                b = pm.next()
                t_m = None
                for c in range(16):
                    t_m = mm(P, pm.bufs[b], W[:, c, 1024 + gt * 128:1024 + (gt + 1) * 128], X[:, c, :],
                             c == 0, c == 15, [t_xt, tw[2], pm.free[b]])
                sb = st_g.next()
                t_e = act(P, st_g.bufs[sb], pm.bufs[b], AF.Silu, [t_m, st_g.free[sb]])
                pm.free[b] = t_e
                t_st = dma(P, "sp", "stg%d" % sb, scr.sgT[hp, gt, :, tt_i * 512:(tt_i + 1) * 512], st_g.bufs[sb], [t_e])
                st_g.free[sb] = t_st
                stores.append(t_st)
            vi = st_v.next()
            VS = st_v.bufs[vi]
            t_ev = None
            for jj in range(4):
                b = pm.next()
                t_m = None
                for c in range(16):
                    t_m = mm(P, pm.bufs[b], X[:, c, jj * 128:(jj + 1) * 128], W[:, c, 512:1024],
                             c == 0, c == 15, [t_xt, tw[1], pm.free[b]])
                t_ev = cp(P, "act", VS[:, jj, :], pm.bufs[b], [t_m, st_v.free[vi] if jj == 0 else None])
                pm.free[b] = t_ev
            xt.free[xk] = t_m
            w_free = t_m
            t_st = dma(P, "sp", "stv%d" % vi,
                       scr.v[hp, tt_i * 512:(tt_i + 1) * 512, :].rearrange("(j p) d -> p j d", p=128), VS, [t_ev])
            st_v.free[vi] = t_st
            stores.append(t_st)
    A.reset(m0)
    return stores


def phase_B_ret(P, C, dec_d, scr, deps=()):
    nc, A, K, PS = C.nc, C.arena, C.k, C.ps
    m0 = A.mark()
    stores = []
    hstores = []
    HB = []
    for hp in range(2):
        h_ = Ctx()
        h_.Q = A.alloc([128, 2, S], BF16)
        h_.Kt = A.alloc([128, 2, S], BF16)
        h_.Qd = A.alloc([128, 2, S], BF16)
        h_.dec = A.alloc([128, 258], F32)
        HB.append(h_)
    SGr = Rot([A.alloc([128, 4, 1024], BF16) for _ in range(3)])
    Vr = Rot([A.alloc([128, 8, 512], BF16) for _ in range(2)])
    Rf = A.alloc([128, 2, 512], F32)
    Rb = A.alloc([128, 2, 512], BF16)
    iT = Rot([A.alloc([128, 128], BF16) for _ in range(2)])
    Kd = Rot([A.alloc([128, 256], BF16) for _ in range(2)])
    on = Rot([A.alloc([128, 512], BF16) for _ in range(4)])
    yst = Rot([A.alloc([128, 4, 4, 128], BF16) for _ in range(2)])
    st6 = A.alloc([128, 6], F32)
    mv = A.alloc([128, 2], F32)
    rstd = A.alloc([128, 1], F32)
    nmr = A.alloc([128, 1], F32)
    mvA = [A.alloc([128, 2], F32) for _ in range(4)]
    nmA = [A.alloc([128, 1], F32) for _ in range(4)]
    mv_free = [None] * 4
    pI, pK = PS[0], PS[1].bitcast(BF16)
    pO = [PS[2], PS[3]]
    pTt = PS[4].bitcast(BF16)
    pR = [PS[5], PS[6]]
    free = Ctx()
    free.pI = free.pK = free.pT = None
    free.pO = [None, None]
    free.pR = [None, None]
    head_free = None
    dd = list(deps)
    sg_tok = {}
    sg_buf = {}
    sg_order = [(hp_, q_) for hp_ in range(2) for q_ in range(4)]

    def load_sg(idx):
        hp_, q_ = sg_order[idx]
        k = SGr.next()
        sl_ = slice(q_ * 1024, (q_ + 1) * 1024)
        sg_buf[(hp_, q_)] = (k, SGr.bufs[k])
        sg_tok[(hp_, q_)] = dma(P, "sp", "rsg%d" % k, SGr.bufs[k],
                                scr.sgT[hp_, :, :, sl_].rearrange("g p t -> p g t"), dd + [SGr.free[k]])

    v_tok = {}
    v_buf = {}

    def load_v(idx):
        hp_, q_ = sg_order[idx]
        k = Vr.next()
        sl_ = slice(q_ * 1024, (q_ + 1) * 1024)
        v_buf[(hp_, q_)] = (k, Vr.bufs[k])
        v_tok[(hp_, q_)] = dma(P, "sp", "rv%d" % k, Vr.bufs[k],
                               scr.v[hp_, sl_, :].rearrange("(j p) d -> p j d", p=128), dd + [Vr.free[k]])

    v_next = [2]
    for hp in range(2):
        H = HB[hp]
        H.t_dec = dma(P, "sp", "dec%d" % hp, H.dec, dec_d[hp], dd)
        H.tq, H.tqd = [], []
        for q4 in range(4):
            sl = slice(q4 * 1024, (q4 + 1) * 1024)
            key = "rl%d_%d" % (hp, q4)
            t = dma(P, "sp", key, H.Q[:, :, sl], scr.qT[hp, :, :, sl].rearrange("h p t -> p h t"), dd)
            t = dma(P, "sp", key, H.Kt[:, :, sl], scr.kT[hp, :, :, sl].rearrange("h p t -> p h t"), dd)
            H.tq.append(t)
            if hp == 0 and q4 < 3:
                load_sg(q4)
            if hp == 0 and q4 < 2:
                load_v(q4)
    sg_next = [3]
    qd_tok = {}
    for hp_ in range(2):
        H_ = HB[hp_]
        for q_ in range(4):
            sl_ = slice(q_ * 1024, (q_ + 1) * 1024)
            t2 = None
            for hf in range(2):
                t2 = tt(P, "pool", H_.Qd[:, hf, sl_].rearrange("p (n q) -> p n q", q=128),
                        H_.Q[:, hf, sl_].rearrange("p (n q) -> p n q", q=128),
                        H_.dec[:, 128:256].unsqueeze(1).to_broadcast([128, 8, 128]), ALU.mult,
                        [H_.tq[q_], H_.t_dec])
            qd_tok[(hp_, q_)] = t2
    for hp in range(2):
        H = HB[hp]
        Q, Kt, dec = H.Q, H.Kt, H.dec
        tq, t_dec = H.tq, H.t_dec
        DT = dec[:, 0:128]
        kd = dec[:, 256:257]
        cd = dec[:, 257:258]
        t_r0 = P.op("dve", lambda e: e.memset(Rf, 0.0), [head_free])
        t_rb = None
        t_rf = t_r0

        def inner(n):
            q4 = n // 8
            c0 = n * 128
            t_i = None
            for hf in range(2):
                t_i = mm(P, pI[:, 0:128], Kt[:, hf, c0:c0 + 128], Q[:, hf, c0:c0 + 128], hf == 0, hf == 1,
                         [tq[q4], free.pI])
            ii = iT.next()
            t_it = tt(P, "dve", iT.bufs[ii], pI[:, 0:128], DT, ALU.mult, [t_i, t_dec, iT.free[ii]])
            free.pI = t_it
            t_k = None
            for hf in range(2):
                t_k = tr(P, pK[:, hf * 128:(hf + 1) * 128], Kt[:, hf, c0:c0 + 128], K.ident_bf, [K.tok, free.pK])
            ki = Kd.next()
            t_kd = ts(P, "dve", Kd.bufs[ki], pK[:, 0:256], kd, None, ALU.mult, None, [t_k, Kd.free[ki]])
            free.pK = t_kd
            return (ii, t_it, ki, t_kd)

        nxt = inner(0)
        pend = []

        def finish(n, oi, t_n):
            nonlocal yi_cur
            c0 = n * 128
            q4 = n // 8
            t_t = None
            for vc in range(4):
                t_t = tr(P, pTt[:, vc * 128:(vc + 1) * 128], on.bufs[oi][:, vc * 128:(vc + 1) * 128], K.ident_bf,
                         [t_n, free.pT])
            on.free[oi] = t_t
            nn = n % 4
            if nn == 0:
                yi_cur = yst.next()
            yi = yi_cur
            Y = yst.bufs[yi]
            sgk, SGq = sg_buf[(hp, q4)]
            cq = c0 - q4 * 1024
            t_y = tt(P, "dve", Y[:, :, nn, :], pTt[:, 0:512].rearrange("p (v q) -> p v q", q=128),
                     SGq[:, :, cq:cq + 128], ALU.mult, [t_t, sg_tok[(hp, q4)], yst.free[yi] if nn == 0 else None])
            free.pT = t_y
            if n % 8 == 7:
                SGr.free[sgk] = t_y
                if sg_next[0] < 8:
                    load_sg(sg_next[0])
                    sg_next[0] += 1
            if nn == 3:
                t0 = (n - 3) * 128
                qq, toff = t0 // 1024, t0 % 1024
                dstp = scr.yl[hp][qq].rearrange("(v p) t -> p v t", p=128)[:, :, toff:toff + 512]
                t_st = dma(P, "sp", "sty%d" % yi, dstp.rearrange("p v (n q) -> p v n q", q=128), Y, [t_y])
                yst.free[yi] = t_st
                hstores.append(t_st)
                C.last_stores.append(t_st)
                if toff == 512:
                    if scr.yg is not None:
                        stores.append(coll(P, "ccy", "AllGather", ALU.bypass, GROUPS, scr.yl[hp][qq], scr.yg[hp][qq],
                                           list(hstores)))
                    else:
                        stores.extend(hstores)
                    del hstores[:]
            return t_y

        yi_cur = 0
        t_y = None
        for n in range(32):
            q4 = n // 8
            c0 = n * 128
            ii, t_it, ki, t_kd = nxt
            ob = n % 2
            vk, Vq = v_buf[(hp, q4)]
            t_o = mm(P, pO[ob], iT.bufs[ii], Vq[:, n % 8, :], True, n == 0, [t_it, free.pO[ob], v_tok[(hp, q4)]])
            iT.free[ii] = t_o
            t_u = None
            if n < 31:
                for hf in range(2):
                    t_u = mm(P, pR[hf], Kd.bufs[ki][:, hf * 128:(hf + 1) * 128], Vq[:, n % 8, :], True, True,
                             [t_kd, free.pR[hf]])
                Kd.free[ki] = t_u
                nxt = inner(n + 1)
            if n % 8 == 7:
                Vr.free[vk] = t_u if n < 31 else t_o
                if v_next[0] < 8:
                    load_v(v_next[0])
                    v_next[0] += 1
            if len(pend) == 2:
                t_y = finish(*pend.pop(0))
            if n > 0:
                for hf in range(2):
                    t_o = mm(P, pO[ob], H.Qd[:, hf, c0:c0 + 128], Rb[:, hf, :], False, hf == 1,
                             [qd_tok[(hp, q4)], t_rb])
            t_ocross = t_o
            if n < 31:
                for hf in range(2):
                    t_rf = stt(P, Rf[:, hf, :], Rf[:, hf, :], cd, pR[hf], ALU.mult, ALU.add, [t_u, t_rf, t_dec])
                    free.pR[hf] = t_rf
                t_rb = cp(P, "act", Rb, Rf, [t_rf, t_ocross])
            sl = n % 4
            mv_, nm_ = mvA[sl], nmA[sl]
            t_s = bnstats(P, st6, pO[ob], [t_ocross])
            t_s = bnaggr(P, mv_, st6, [t_s, mv_free[sl]])
            t_s = ts(P, "dve", nm_, mv_[:, 0:1], -1.0, None, ALU.mult, None, [t_s])
            t_a = act(P, rstd, mv_[:, 1:2], AF.Ln, [t_s, C.t_eps], bias=C.eps_gn)
            t_a = act(P, rstd, rstd, AF.Exp, [t_a], scale=-0.5)
            t_a = act(P, nmr, nm_, AF.Identity, [t_a], scale=rstd)
            mv_free[sl] = t_a
            oi = on.next()
            t_n = act(P, on.bufs[oi], pO[ob], AF.Identity, [t_a, on.free[oi]], bias=nmr, scale=rstd)
            free.pO[ob] = t_n
            pend.append((n, oi, t_n))
        while pend:
            t_y = finish(*pend.pop(0))
        head_free = t_y
    A.reset(m0)
    return stores


def build_L_ret():
    nc = new_prog()
    xT_d = nc.dram_tensor("xT", [4, D, TOKC], BF16, kind="ExternalInput").ap()
    w_d = nc.dram_tensor("w_in", [D, 3072], F32, kind="ExternalInput").ap()
    cos_d = nc.dram_tensor("cos", [128, S], F32, kind="ExternalInput").ap()
    sin_d = nc.dram_tensor("sin", [128, S], F32, kind="ExternalInput").ap()
    dec_d = nc.dram_tensor("dec", [2, 128, 258], F32, kind="ExternalInput").ap()
    wo_d = nc.dram_tensor("w_out", [1024, D], F32, kind="ExternalInput").ap()
    part = nc.dram_tensor("part", [S, D], F32, kind="ExternalOutput").ap()
    scr = Scr()
    scr.qT = nc.dram_tensor("s_qT", [2, 2, 128, S], BF16).ap()
    scr.kT = nc.dram_tensor("s_kT", [2, 2, 128, S], BF16).ap()
    scr.v = nc.dram_tensor("s_v", [2, S, 512], BF16).ap()
    scr.sgT = nc.dram_tensor("s_sgT", [2, 4, 128, S], BF16).ap()
    scr.yT = nc.dram_tensor("s_yT", [8, 128, S], BF16).ap()
    with contextlib.ExitStack() as st:
        C = setup(nc, st)
        P = Prog(nc)
        load_consts(P, C)
        stA = phase_A_ret(P, C, xT_d, w_d, cos_d, sin_d, scr)
        P.barrier()
        stB = phase_B_ret(P, C, dec_d, scr, deps=stA)
        P.barrier()
        stC = phase_C(P, C, scr.yT, wo_d, part, 8, deps=stB)
        P.emit(final_waits=stC)
    return nc


def rotary_tables():
    inv_freq = (10000.0 ** (-np.arange(0, 256, 2, dtype=np.float32) / np.float32(256))).astype(np.float32)
    ang = np.arange(S, dtype=np.float32)[:, None] * inv_freq[None, :]
    return np.ascontiguousarray(np.cos(ang).T.astype(np.float32)), np.ascontiguousarray(np.sin(ang).T.astype(np.float32))


def decay_tables(head):
    lg = np.log1p(-np.exp2(np.float32(-5.0 - head))).astype(np.float32)
    pos = np.arange(128, dtype=np.float32)
    diff = pos[:, None] - pos[None, :]
    intra = np.where(diff >= 0, np.exp(np.maximum(diff, 0.0) * lg), 0.0).astype(np.float32)
    qd = np.exp((pos + 1.0) * lg).astype(np.float32)
    kd = np.exp((128 - 1.0 - pos) * lg).astype(np.float32)
    cdv = np.exp(np.float32(128) * lg).astype(np.float32)
    out = np.zeros((128, 258), np.float32)
    out[:, 0:128] = intra.T
    out[:, 128:256] = qd[None, :]
    out[:, 256] = kd
    out[:, 257] = cdv
    return out


CORES = list(range(8))
_PROGS = {}


def _prog(name, fn):
    if name not in _PROGS:
        _PROGS[name] = fn()
    return _PROGS[name]


def _run(nc, maps):
    return run_bass_kernel_spmd(nc, maps, core_ids=CORES).results


def _ca(a):
    return np.ascontiguousarray(a)


def fox_maps(xTf, w, bfv, gq, gk, wo):
    maps = []
    for c in CORES:
        b, g = c // 4, c % 4
        wl = np.concatenate([w[:, 512 * g:512 * g + 512], w[:, 2048 + 512 * g:2048 + 512 * g + 512],
                             w[:, 4096 + 512 * g:4096 + 512 * g + 512], w[:, 6144 + 512 * g:6144 + 512 * g + 512],
                             w[:, 8192 + 4 * g:8192 + 4 * g + 4]], axis=1)
        maps.append({"xT": xTf[b], "w_in": _ca(wl), "bf": _ca(bfv[4 * g:4 * g + 4].reshape(4, 1)),
                     "gq": _ca(gq.reshape(128, 1)), "gk": _ca(gk.reshape(128, 1)),
                     "gqr": _ca(gq.reshape(1, 128)), "gkr": _ca(gk.reshape(1, 128)),
                     "w_out": _ca(wo[512 * g:512 * g + 512, :])})
    return maps


def ret_wcols(w, g):
    cols = []
    for hp in range(2):
        h = 2 * g + hp
        cols += [w[:, h * 256:(h + 1) * 256], w[:, 2048 + h * 256:2048 + (h + 1) * 256],
                 w[:, 4096 + h * 512:4096 + (h + 1) * 512], w[:, 8192 + h * 512:8192 + (h + 1) * 512]]
    return _ca(np.concatenate(cols, 1))


def ret_maps(xTf, w, wo, cosT, sinT):
    maps = []
    for c in CORES:
        b, g = c // 4, c % 4
        maps.append({"xT": xTf[b], "w_in": ret_wcols(w, g), "cos": cosT, "sin": sinT,
                     "dec": _ca(np.stack([decay_tables(2 * g), decay_tables(2 * g + 1)], 0)),
                     "w_out": _ca(wo[1024 * g:1024 * (g + 1), :])})
    return maps


def kernel_unfused(x, fox_w_in, fox_b_f, fox_q_gain, fox_k_gain, fox_w_out, ret_w_in, ret_w_out,
                   ln_gain, ln_bias):
    x = np.asarray(x, np.float32)
    xs = [_ca(x[c // 4, (c % 4) * 1024:(c % 4 + 1) * 1024, :]) for c in CORES]
    cosT, sinT = rotary_tables()
    r = _run(_prog("D0", lambda: build_D(True, False)), [{"x_in": xs[c]} for c in CORES])
    xT = [r[c]["xT_out"] for c in CORES]
    for i in range(DEPTH):
        j = i // 2
        xTf = [_ca(np.stack(xT[4 * b:4 * b + 4], 0)) for b in range(2)]
        if i % 2 == 0:
            maps = fox_maps(xTf, np.asarray(fox_w_in[j]), np.asarray(fox_b_f[j]), np.asarray(fox_q_gain[j]),
                            np.asarray(fox_k_gain[j]), np.asarray(fox_w_out[j]))
            r = _run(_prog("Lfox", build_L_fox), maps)
        else:
            maps = ret_maps(xTf, np.asarray(ret_w_in[j]), np.asarray(ret_w_out[j]), cosT, sinT)
            r = _run(_prog("Lret", build_L_ret), maps)
        parts = [r[c]["part"] for c in CORES]
        last = (i == DEPTH - 1)
        maps = []
        for c in CORES:
            b, g = c // 4, c % 4
            pp = _ca(np.stack([parts[4 * b + q][g * 1024:(g + 1) * 1024] for q in range(4)], 0))
            maps.append({"x_in": xs[c], "parts": pp, "lng": _ca(np.asarray(ln_gain)[i:i + 1]),
                         "lnb": _ca(np.asarray(ln_bias)[i:i + 1])})
        r = _run(_prog("Dlast" if last else "D", (lambda: build_D(False, True)) if last else (lambda: build_D(False, False))), maps)
        xs = [r[c]["x_out"] for c in CORES]
        if not last:
            xT = [r[c]["xT_out"] for c in CORES]
    out = np.zeros((NB, S, D), np.float32)
    for c in CORES:
        out[c // 4, (c % 4) * 1024:(c % 4 + 1) * 1024, :] = xs[c]
    return out


def fused_maps(x, fox_w_in, fox_b_f, fox_q_gain, fox_k_gain, fox_w_out, ret_w_in, ret_w_out, ln_gain, ln_bias):
    cosT, sinT = rotary_tables()
    maps = []
    for c in CORES:
        b, g = c // 4, c % 4
        dsl = slice(g * DL, (g + 1) * DL)
        m = {"x_in": _ca(x[b][:, dsl]), "cos": cosT, "sin": sinT,
             "dec": _ca(np.stack([decay_tables(2 * g), decay_tables(2 * g + 1)], 0)),
             "lng": _ca(ln_gain[:, dsl]), "lnb": _ca(ln_bias[:, dsl])}
        for j in range(2):
            w = fox_w_in[j]
            m["fw_in%d" % j] = _ca(np.concatenate(
                [w[:, 512 * g:512 * g + 512], w[:, 2048 + 512 * g:2048 + 512 * g + 512],
                 w[:, 4096 + 512 * g:4096 + 512 * g + 512], w[:, 6144 + 512 * g:6144 + 512 * g + 512],
                 w[:, 8192 + 4 * g:8192 + 4 * g + 4]], axis=1))
            m["fbf%d" % j] = _ca(fox_b_f[j][4 * g:4 * g + 4].reshape(4, 1))
            m["fgq%d" % j] = _ca(fox_q_gain[j].reshape(128, 1))
            m["fgk%d" % j] = _ca(fox_k_gain[j].reshape(128, 1))
            m["fgqr%d" % j] = _ca(fox_q_gain[j].reshape(1, 128))
            m["fgkr%d" % j] = _ca(fox_k_gain[j].reshape(1, 128))
            m["fwo%d" % j] = _ca(fox_w_out[j][:, dsl])
            m["rw_in%d" % j] = ret_wcols(ret_w_in[j], g)
            m["rwo%d" % j] = _ca(ret_w_out[j][:, dsl])
        maps.append(m)
    return maps


def kernel(**inputs):
    inp = {k: np.asarray(v, np.float32) for k, v in inputs.items()}
    maps = fused_maps(**inp)
    nc = _prog("fused", build_fused)
    r = _run(nc, maps)
    out = np.zeros((NB, S, D), np.float32)
    for c in CORES:
        out[c // 4][:, (c % 4) * DL:(c % 4 + 1) * DL] = r[c]["out"]
    return out


GROUPS = [[0, 1, 2, 3], [4, 5, 6, 7]]
DL = 512


def phase_E(P, C, xres, mode, ysrc, ec, wo_d, lng_d, lnb_d, x_in, xs_l, xg_l, st_l, st_g, x_out, deps=(), ydeps=(), pre_wo=None):
    nc, A, K, PS = C.nc, C.arena, C.k, C.ps
    m0 = A.mark()
    outs = []
    xres = A.alloc([128, 32, DL], F32)
    t_xq = [None] * 4

    def load_x(q):
        t_xq[q] = dma(P, "sp", "ex%d" % q, xres[:, q * 8:(q + 1) * 8, :],
                      x_in[q * 1024:(q + 1) * 1024, :].rearrange("(j p) d -> p j d", p=128), deps)

    if mode == "init":
        for q_ in range(4):
            load_x(q_)
    C.t_xres = None
    if mode != "init":
        if pre_wo is None:
            Wo = A.alloc([128, ec, DL], BF16)
            t_w = load_w_bf16(P, "wE", Wo, wo_d, deps)
        else:
            Wo, t_w = pre_wo
        gb = A.alloc([128, DL], F32)
        bb = A.alloc([128, DL], F32)
        t_gb = dma(P, "sp", "egb", gb, lng_d.partition_broadcast(128), deps)
        t_gb = dma(P, "sp", "egb", bb, lnb_d.partition_broadcast(128), deps)
        Y = Rot([A.alloc([128, ec, 512], BF16) for _ in range(2)])
        S12 = [A.alloc([128, 16], F32) for _ in range(2)]
        SG_ = [A.alloc([128, 4, 16], F32) for _ in range(2)]
        ssum = A.alloc([128, 16], F32)
        mean = A.alloc([128, 8], F32)
        var = A.alloc([128, 8], F32)
        rstd = [A.alloc([128, 8], F32) for _ in range(2)]
        nmr = [A.alloc([128, 8], F32) for _ in range(2)]
        junk = A.alloc([128, DL], F32)
        tmpn = Rot([A.alloc([128, DL], F32) for _ in range(3)])
    xTs = Rot([A.alloc([128, 4, 1024], BF16) for _ in range(2)])
    pm = Rot([PS[0], PS[1], PS[2], PS[3]])
    pt = Rot([PS[4], PS[5]])
    st_free = [None, None]
    t_stats = [None] * 4
    t_h = [None] * 32

    def out_proj(q):
        for s2 in range(2):
            q2 = q * 2 + s2
            yi = Y.next()
            t_y = C.yload(P, "ey%d" % yi, q2, Y.bufs[yi], list(deps) + C.ydeps(q2) + [Y.free[yi]])
            if s2 == 0:
                load_x(q)
            for jj in range(4):
                j = q2 * 4 + jj
                b = pm.next()
                t_m = None
                for e_ in range(ec):
                    t_m = mm(P, pm.bufs[b], Y.bufs[yi][:, e_, jj * 128:(jj + 1) * 128], Wo[:, e_, :],
                             e_ == 0, e_ == ec - 1, [t_w, t_y, pm.free[b]])
                t_hh = stt(P, xres[:, j, :], xres[:, j, :], ALPHA, pm.bufs[b], ALU.mult, ALU.add, [t_m, t_xq[q]])
                pm.free[b] = t_hh
                jq = j % 8
                sb = S12[q % 2]
                t_a = P.op("act", (lambda o, i, a: lambda e: e.activation(out=o, in_=i, func=AF.Identity, accum_out=a))(
                    junk, xres[:, j, :], sb[:, jq:jq + 1]), [t_hh, st_free[q % 2] if jq == 0 else None])
                t_a = P.op("act", (lambda o, i, a: lambda e: e.activation(out=o, in_=i, func=AF.Square, accum_out=a))(
                    junk, xres[:, j, :], sb[:, 8 + jq:9 + jq]), [t_a])
                t_h[j] = t_a
            Y.free[yi] = t_m
        t1 = dma(P, "sp", "est%d" % (q % 2), st_l[q], S12[q % 2], [t_h[q * 8 + 7]])
        st_free[q % 2] = t1
        t2 = coll(P, "ccst", "AllGather", ALU.bypass, GROUPS, st_l[q], st_g[q], [t1])
        t3 = dma(P, "sp", "esg%d" % (q % 2), SG_[q % 2], st_g[q].rearrange("(r p) s -> p r s", p=128), [t2])
        t_stats[q] = t3

    def normalize(q):
        if mode != "init":
            G_ = SG_[q % 2]
            t_s = tt(P, "dve", ssum, G_[:, 0, :], G_[:, 1, :], ALU.add, [t_stats[q]])
            t_s = tt(P, "dve", ssum, ssum, G_[:, 2, :], ALU.add, [t_s])
            t_s = tt(P, "dve", ssum, ssum, G_[:, 3, :], ALU.add, [t_s])
            t_s = ts(P, "dve", mean, ssum[:, 0:8], 1.0 / D, None, ALU.mult, None, [t_s])
            t_s = tt(P, "dve", var, mean, mean, ALU.mult, [t_s])
            t_s = stt(P, var, ssum[:, 8:16], 1.0 / D, var, ALU.mult, ALU.subtract, [t_s])
            t_s = act(P, rstd[q % 2], var, AF.Sqrt, [t_s, C.t_eps], bias=C.eps_ln)
            t_s = P.op("dve", (lambda o: lambda e: e.reciprocal(out=o, in_=o))(rstd[q % 2]), [t_s])
            t_s = stt(P, nmr[q % 2], mean, -1.0, rstd[q % 2], ALU.mult, ALU.mult, [t_s])
        xi = xTs.next() if mode != "last" else None
        t_c = None
        t_fin = None
        st1 = {}

        def stage1(jq):
            j = q * 8 + jq
            if mode != "init":
                ti = tmpn.next()
                T = tmpn.bufs[ti]
                t_n = act(P, T, xres[:, j, :], AF.Identity, [t_s, tmpn.free[ti]],
                          bias=nmr[q % 2][:, jq:jq + 1], scale=rstd[q % 2][:, jq:jq + 1])
                t_n = tt(P, "dve", T, T, gb, ALU.mult, [t_n, t_gb])
                t_n = tt(P, "dve", xres[:, j, :], T, bb, ALU.add, [t_n])
                tmpn.free[ti] = t_n
                C.t_xres = t_n
                st1[jq] = t_n
            else:
                st1[jq] = t_xq[q]

        def stage2(jq):
            j = q * 8 + jq
            pi = pt.next()
            pb = pt.bufs[pi].rearrange("p (a b) -> p a b", b=128)
            t_t = None
            for c4 in range(4):
                t_t = tr(P, pb[:, c4, :], xres[:, j, c4 * 128:(c4 + 1) * 128], K.ident_f,
                         [st1[jq], K.tok, pt.free[pi]])
            t_cc = cp(P, "dve", xTs.bufs[xi][:, :, jq * 128:(jq + 1) * 128], pb,
                      [t_t, xTs.free[xi] if jq == 0 else None])
            pt.free[pi] = t_cc
            return t_cc

        for jq in range(8 + 2):
            if jq < 8:
                stage1(jq)
                t_fin = st1[jq]
            if mode != "last" and jq >= 2:
                t_c = stage2(jq - 2)
        if mode != "last":
            t1 = dma(P, "sp", "exs%d" % xi, xs_l[q].rearrange("(c p) t -> p c t", p=128), xTs.bufs[xi], [t_c])
            xTs.free[xi] = t1
            C.e_stores.append(t1)
            outs.append(coll(P, "ccx", "AllGather", ALU.bypass, GROUPS, xs_l[q], xg_l[q], [t1]))
            if mode == "mid":
                C.x_spill.append(dma(P, "sp", "espill", C.xsp[q * 1024:(q + 1) * 1024, :].rearrange("(j p) d -> p j d", p=128),
                                     xres[:, q * 8:(q + 1) * 8, :], [t_fin]))
        else:
            outs.append(dma(P, "sp", "eout", x_out[q * 1024:(q + 1) * 1024, :].rearrange("(j p) d -> p j d", p=128),
                            xres[:, q * 8:(q + 1) * 8, :], [t_fin]))

    if mode == "init":
        for q in range(4):
            normalize(q)
    else:
        out_proj(0)
        for q in range(1, 4):
            out_proj(q)
            normalize(q - 1)
        normalize(3)
    A.reset(m0)
    return outs


def build_fused(nlayers=DEPTH):
    nc = new_prog()
    def ext(name, shape, dt=F32):
        return nc.dram_tensor(name, shape, dt, kind="ExternalInput").ap()
    def scratch(name, shape, dt=BF16):
        return nc.dram_tensor(name, shape, dt).ap()
    x_in = ext("x_in", [S, DL])
    fox = []
    for j in range(2):
        f = Ctx()
        f.w = ext("fw_in%d" % j, [D, 2052]); f.bf = ext("fbf%d" % j, [4, 1])
        f.gq = ext("fgq%d" % j, [128, 1]); f.gk = ext("fgk%d" % j, [128, 1])
        f.gqr = ext("fgqr%d" % j, [1, 128]); f.gkr = ext("fgkr%d" % j, [1, 128])
        f.wo = ext("fwo%d" % j, [D, DL])
        fox.append(f)
    ret = []
    for j in range(2):
        r_ = Ctx()
        r_.w = ext("rw_in%d" % j, [D, 3072]); r_.wo = ext("rwo%d" % j, [2 * D, DL])
        ret.append(r_)
    cos_d = ext("cos", [128, S]); sin_d = ext("sin", [128, S]); dec_d = ext("dec", [2, 128, 258])
    lng = ext("lng", [DEPTH, DL]); lnb = ext("lnb", [DEPTH, DL])
    out = nc.dram_tensor("out", [S, DL], F32, kind="ExternalOutput").ap()
    fs = Scr()
    fs.qT = scratch("f_qT", [4, 128, S]); fs.kT = scratch("f_kT", [4, 128, S])
    fs.v = scratch("f_v", [4, 128, 32, 128]); fs.sgT = scratch("f_sgT", [4, 128, S])
    fs.yl = [scratch("f_yl%d" % h, [128, S]) for h in range(4)]
    fs.yg = [scratch("f_yg%d" % h, [512, S]) for h in range(4)]
    rs = Scr()
    rs.qT = scratch("r_qT", [2, 2, 128, S]); rs.kT = scratch("r_kT", [2, 2, 128, S])
    rs.v = scratch("r_v", [2, S, 512]); rs.sgT = scratch("r_sgT", [2, 4, 128, S])
    rs.yl = [[scratch("r_yl%d_%d" % (hp, q), [512, 1024]) for q in range(4)] for hp in range(2)]
    rs.yg = [[scratch("r_yg%d_%d" % (hp, q), [2048, 1024]) for q in range(4)] for hp in range(2)]
    xs_l = [scratch("xs%d" % q, [DL, 1024]) for q in range(4)]
    xg_l = [scratch("xg%d" % q, [D, 1024]) for q in range(4)]
    st_l = [scratch("stl%d" % q, [128, 16], F32) for q in range(4)]
    st_g = [scratch("stg%d" % q, [512, 16], F32) for q in range(4)]
    xsp = scratch("xsp", [S, DL], F32)

    def yload_fox(P, key, q2, Yb, deps):
        t = None
        Yv = Yb.rearrange("p (r h) t -> p r h t", h=4)
        for h in range(4):
            t = dma(P, "sp", key, Yv[:, :, h, :],
                    fs.yg[h][:, q2 * 512:(q2 + 1) * 512].rearrange("(r p) t -> p r t", p=128), deps)
        return t

    def yload_ret(P, key, q2, Yb, deps):
        t = None
        Yv = Yb.rearrange("p (r h v) t -> p r h v t", h=2, v=4)
        q, s2 = q2 // 2, q2 % 2
        for hp in range(2):
            for r in range(4):
                t = dma(P, "sp", key, Yv[:, r, hp, :, :],
                        rs.yg[hp][q][r * 512:(r + 1) * 512, s2 * 512:(s2 + 1) * 512].rearrange("(v p) t -> p v t", p=128),
                        deps)
        return t

    with contextlib.ExitStack() as st:
        C = setup(nc, st)
        P = Prog(nc)
        load_consts(P, C)
        C.xsp = xsp
        C.x_spill = []
        C.e_stores = []
        C.last_stores = []
        C.t_xres = None
        C.pre_ret = None
        mtop = C.arena.mark()
        Wo_cur = C.arena.alloc([128, 16, DL], BF16)
        m_base = C.arena.mark()
        C.ncr = C.arena.alloc([4, S], F32)
        m_ncr = C.arena.mark()
        pre0 = preload_A_fox(P, C, fox[0].w)
        t_wo = load_w_bf16(P, "wE", Wo_cur, fox[0].wo)
        xg_tok = phase_E(P, C, None, "init", None, 0, None, None, None, x_in, xs_l, xg_l, st_l, st_g, None)
        P.barrier(extra=C.e_stores)
        C.e_stores = []
        outs = xg_tok
        for i in range(nlayers):
            j = i // 2
            last = (i == nlayers - 1)
            if i > 0:
                C.arena.reset(mtop)
                Wo_cur = C.arena.alloc([128, 16 if i % 2 == 0 else 32, DL], BF16)
                m_base = C.arena.mark()
            m0 = m_base
            if i % 2 == 0:
                f = fox[j]
                if i > 0:
                    C.ncr = C.arena.alloc([4, S], F32)
                    assert C.arena.mark() == m_ncr
                    pre_i = preload_A_fox(P, C, f.w)
                    t_wo = load_w_bf16(P, "wE", Wo_cur, f.wo)
                else:
                    pre_i = pre0
                m1 = m_ncr
                stA = phase_A_fox(P, C, xg_l, f.w, f.bf, f.gq, f.gk, fs, xdeps=xg_tok, pre=pre_i)
                P.barrier(extra=stA)
                C.arena.reset(m1)
                stB = phase_B_fox(P, C, f.gqr[0], f.gkr[0], fs, deps=stA)
                C.yload = yload_fox
                C.ydeps = (lambda toks: lambda q2: list(toks))(list(stB))
                ec, wo = 16, f.wo
            else:
                r_ = ret[j]
                stA = phase_A_ret(P, C, xg_l, r_.w, cos_d, sin_d, rs, xdeps=xg_tok,
                                  after_w=(lambda wo_=r_.wo, Wb_=Wo_cur: load_w_bf16(P, "wE", Wb_, wo_)),
                                  pre_w0=C.pre_ret)
                t_wo = C.t_wo
                C.arena.limit = ARENA_BYTES
                P.barrier(extra=stA)
                stB = phase_B_ret(P, C, dec_d, rs, deps=stA)
                C.yload = yload_ret
                C.ydeps = (lambda toks: lambda q2: [toks[q2 // 2], toks[4 + q2 // 2]])(list(stB))
                ec, wo = 32, r_.wo
            assert len(stB) == (4 if i % 2 == 0 else 8)
            P.barrier(extra=C.last_stores)
            C.last_stores = []
            C.arena.reset(m0)
            pre_ret = None
            if i % 2 == 0 and not last:
                WTOP = ARENA_BYTES - 128 * 16 * 1536 * 2 // 128
                Wtop = C.arena.alloc([128, 16, 1536], BF16, at=WTOP)
                C.arena.limit = WTOP
                pre_ret = (Wtop, load_ret_w(P, Wtop, ret[j].w, 0))
            spill = list(C.x_spill)
            C.x_spill = []
            outs = phase_E(P, C, None, "last" if last else "mid", None, ec, wo, lng[i], lnb[i],
                           x_in if i == 0 else xsp, xs_l, xg_l, st_l, st_g, out, deps=spill, pre_wo=(Wo_cur, t_wo))
            xg_tok = outs
            P.barrier(extra=list(C.x_spill) + list(C.e_stores))
            C.e_stores = []
            C.pre_ret = pre_ret
        P.emit(final_waits=outs)
    return nc
```

```python
import contextlib
import numpy as np
import ml_dtypes
import concourse.bass as bass
import concourse.mybir as mybir
from concourse.bass_utils import run_bass_kernel_spmd

F32 = mybir.dt.float32
BF16 = mybir.dt.bfloat16
U8 = mybir.dt.uint8
AF = mybir.ActivationFunctionType
ALU = mybir.AluOpType
AX = mybir.AxisListType
NPBF = ml_dtypes.bfloat16

D = 2048
S = 4096
NB = 2
DEPTH = 4
TOKC = 1024
ALPHA = (2.0 * DEPTH) ** 0.25
LN_EPS = 1e-5
GN_EPS = 1e-6
QK_EPS = 1e-6
NEG = -30000.0

ENGS = ("pe", "act", "dve", "pool", "sp")


class Tok:
    __slots__ = ("eng", "idx", "sem", "val")

    def __init__(self, eng=None, idx=None, sem=None, val=None):
        self.eng, self.idx, self.sem, self.val = eng, idx, sem, val


class Prog:
    def __init__(self, nc):
        self.nc = nc
        self.ops = {e: [] for e in ENGS}
        self.dma_cnt = {}
        self.sig = {e: set() for e in ENGS}
        self.last = {e: None for e in ENGS}
        self.pending = {e: [] for e in ENGS}

    def _deps(self, eng, deps):
        deps = [d for d in deps if d is not None]
        if self.pending[eng]:
            deps = deps + self.pending[eng]
            self.pending[eng] = []
        for d in deps:
            if d.eng is not None and not (d.eng == "pe" and eng == "pe"):
                self.sig[d.eng].add(d.idx)
        return deps

    def op(self, eng, fn, deps=()):
        deps = self._deps(eng, deps)
        idx = len(self.ops[eng])
        self.ops[eng].append((fn, deps, None))
        t = Tok(eng=eng, idx=idx)
        self.last[eng] = t
        return t

    def dma(self, queue, semkey, fn, deps=(), inc=16):
        deps = self._deps(queue, deps)
        self.dma_cnt[semkey] = self.dma_cnt.get(semkey, 0) + inc
        self.ops[queue].append((fn, deps, (semkey, inc)))
        return Tok(sem=semkey, val=self.dma_cnt[semkey])

    def barrier(self, extra=()):
        best = {}
        for t in extra:
            if t is None:
                continue
            if t.sem is not None:
                if t.sem not in best or best[t.sem].val < t.val:
                    best[t.sem] = t
        toks = [t for t in self.last.values() if t is not None] + list(best.values())
        for e in ENGS:
            self.pending[e] = list(toks)

    def emit(self, final_waits=()):
        nc = self.nc
        with contextlib.ExitStack() as st:
            esem = {e: st.enter_context(nc.semaphore("s_" + e)) for e in ENGS}
            dsem = {k: st.enter_context(nc.semaphore("d_%s" % str(k))) for k in self.dma_cnt}
            sigval = {}
            for e in ENGS:
                c = 0
                for i in range(len(self.ops[e])):
                    if i in self.sig[e]:
                        c += 1
                        sigval[(e, i)] = c
            print("[prog] ops:", {e: len(self.ops[e]) for e in ENGS}, "signals:",
                  {e: len(self.sig[e]) for e in ENGS}, "dma sems:", len(self.dma_cnt),
                  "max dma cnt:", max(self.dma_cnt.values()) if self.dma_cnt else 0, flush=True)
            block = st.enter_context(nc.Block())

            def run(e, eng):
                waited = {}
                for i, (fn, deps, semkey) in enumerate(self.ops[e]):
                    for d in deps:
                        if d.eng is not None:
                            if d.eng == "pe" and e == "pe":
                                continue
                            key, v, s = ("e", d.eng), sigval[(d.eng, d.idx)], esem[d.eng]
                        else:
                            key, v, s = ("d", d.sem), d.val, dsem[d.sem]
                        if waited.get(key, 0) >= v:
                            continue
                        waited[key] = v
                        eng.wait_ge(s, v)
                    ins = fn(eng)
                    if semkey is not None:
                        ins.then_inc(dsem[semkey[0]], semkey[1])
                    elif (e, i) in sigval:
                        ins.then_inc(esem[e], 1)
                if e == "sp":
                    for t in final_waits:
                        eng.wait_ge(dsem[t.sem], t.val)

            block.tensor(lambda eng: run("pe", eng))
            block.scalar(lambda eng: run("act", eng))
            block.vector(lambda eng: run("dve", eng))
            block.gpsimd(lambda eng: run("pool", eng))
            block.sync(lambda eng: run("sp", eng))


DTSIZE = {F32: 4, BF16: 2, U8: 1}


class Arena:
    def __init__(self, ar, size):
        self.ar, self.size, self.off = ar, size, 0
        self.limit = size

    def mark(self):
        return self.off

    def reset(self, m):
        self.off = m

    def alloc(self, shape, dt, at=None):
        n = int(np.prod(shape[1:])) * DTSIZE[dt]
        if at is not None:
            off = at
            assert off + n <= self.size
        else:
            off = (self.off + 63) // 64 * 64
            assert off + n <= self.limit, ("SBUF arena overflow", off, n, self.limit)
            self.off = off + n
        ap = self.ar[0:shape[0], off:off + n].bitcast(dt)
        dims = list(shape[1:])
        if len(dims) > 1:
            names = "abcd"[:len(dims)]
            pat = "p (%s) -> p %s" % (" ".join(names), " ".join(names))
            kw = {names[i]: dims[i] for i in range(1, len(dims))}
            ap = ap.rearrange(pat, **kw)
        return ap


class Ctx:
    pass


def mm(P, out, lhsT, rhs, start, stop, deps=()):
    return P.op("pe", lambda e: e.matmul(out, lhsT, rhs, start=start, stop=stop), deps)


def tr(P, out, in_, ident, deps=()):
    return P.op("pe", lambda e: e.transpose(out, in_, ident), deps)


def act(P, out, in_, func, deps=(), bias=None, scale=None):
    kw = {}
    if bias is not None:
        kw["bias"] = bias
    if scale is not None:
        kw["scale"] = scale
    return P.op("act", lambda e: e.activation(out=out, in_=in_, func=func, **kw), deps)


def tt(P, eng, out, in0, in1, op, deps=()):
    return P.op(eng, lambda e: e.tensor_tensor(out=out, in0=in0, in1=in1, op=op), deps)


def ts(P, eng, out, in0, s1, s2, op0, op1=None, deps=()):
    if op1 is None:
        return P.op(eng, lambda e: e.tensor_scalar(out=out, in0=in0, scalar1=s1, scalar2=None, op0=op0), deps)
    return P.op(eng, lambda e: e.tensor_scalar(out=out, in0=in0, scalar1=s1, scalar2=s2, op0=op0, op1=op1), deps)


def stt(P, out, in0, scalar, in1, op0, op1, deps=()):
    return P.op("dve", lambda e: e.scalar_tensor_tensor(out=out, in0=in0, scalar=scalar, in1=in1, op0=op0, op1=op1), deps)


def cp(P, eng, out, in_, deps=()):
    if eng == "act":
        return P.op("act", lambda e: e.activation(out=out, in_=in_, func=AF.Copy), deps)
    return P.op(eng, lambda e: e.tensor_copy(out=out, in_=in_), deps)


def bnstats(P, out, in_, deps=()):
    return P.op("dve", lambda e: e.bn_stats(out=out, in_=in_), deps)


def bnaggr(P, out, in_, deps=()):
    return P.op("dve", lambda e: e.bn_aggr(out=out, in_=in_), deps)


def coll(P, key, kind, op, groups, in_, out, deps=()):
    return P.dma("pool", key, lambda e: e.collective_compute(kind, op, replica_groups=groups, ins=[in_], outs=[out]),
                 deps, inc=1)


def dma(P, q, key, out, in_, deps=()):
    return P.dma(q, key, lambda e: e.dma_start(out=out, in_=in_), deps)


def load_consts(P, C):
    nc, A = C.nc, C.arena
    ident = np.eye(128, dtype=np.float32)
    kk = np.arange(128)[:, None]
    qq = np.arange(128)[None, :]
    negmask = np.where(kk > qq, NEG, 0.0).astype(np.float32)
    sel = np.zeros((4, 4, 128), np.float32)
    for h in range(4):
        sel[h, h, :] = 1.0
    c = Ctx()
    c.ident_bf = A.alloc([128, 128], BF16)
    c.ones_bf = A.alloc([128, 128], BF16)
    c.negmask_bf = A.alloc([128, 128], BF16)
    c.ident_f = A.alloc([128, 128], F32)
    c.sel = A.alloc([4, 4, 128], F32)
    toks = []
    d_id = nc.inline_tensor(ident.astype(NPBF), "c_identbf").ap()
    d_on = nc.inline_tensor(np.ones((128, 128), NPBF), "c_onesbf").ap()
    d_nm = nc.inline_tensor(negmask.astype(NPBF), "c_negmask").ap()
    d_if = nc.inline_tensor(ident, "c_identf").ap()
    d_sel = nc.inline_tensor(sel.transpose(1, 0, 2).copy(), "c_sel").ap()
    toks.append(dma(P, "sp", "const", c.ident_bf, d_id[:, :]))
    toks.append(dma(P, "sp", "const", c.ones_bf, d_on[:, :]))
    toks.append(dma(P, "sp", "const", c.negmask_bf, d_nm[:, :]))
    toks.append(dma(P, "sp", "const", c.ident_f, d_if[:, :]))
    toks.append(dma(P, "sp", "const", c.sel, d_sel[:, :, :]))
    c.tok = toks[-1]
    C.k = c
    C.eps_qk = A.alloc([128, 1], F32)
    C.eps_ln = A.alloc([128, 1], F32)
    C.eps_gn = A.alloc([128, 1], F32)
    P.op("pool", lambda e: e.memset(C.eps_qk, 128.0 * QK_EPS))
    P.op("pool", lambda e: e.memset(C.eps_ln, LN_EPS))
    C.t_eps = P.op("pool", lambda e: e.memset(C.eps_gn, GN_EPS))


def phase_D(P, C, xres, x_in, parts, nparts, lng, lnb, xT_out, x_out, deps=()):
    nc, A, K, PS = C.nc, C.arena, C.k, C.ps
    m0 = A.mark()
    out_toks = []
    t_x = None
    if x_in is not None:
        t_x = dma(P, "sp", "dx", xres, x_in.rearrange("(j p) d -> p j d", p=128), deps)
    if parts is not None:
        gb = A.alloc([128, D], F32)
        bb = A.alloc([128, D], F32)
        t_g = dma(P, "sp", "dgb", gb, lng.partition_broadcast(128), deps)
        t_b = dma(P, "sp", "dgb", bb, lnb.partition_broadcast(128), deps)
        zt = Rot([A.alloc([128, D], F32) for _ in range(3)])
        hb = [A.alloc([128, D], F32) for _ in range(2)]
        st6 = A.alloc([128, 4, 6], F32)
        mv = A.alloc([128, 2], F32)
        rstd = A.alloc([128, 1], F32)
        nmr = A.alloc([128, 1], F32)
    xb = [A.alloc([128, D], BF16) for _ in range(2)]
    xTs = A.alloc([128, 16, TOKC], BF16) if xT_out is not None else None
    z_free = [None, None]
    h_free = [None, None]
    xb_free = [None, None]
    ps_free = [None, None]
    t_last_xres = None
    for j in range(8):
        b2 = j % 2
        if parts is not None:
            t_h = None
            for r in range(nparts):
                zi = zt.next()
                tz = dma(P, "sp", "dz%d" % zi, zt.bufs[zi], parts[r, j * 128:(j + 1) * 128, :],
                         list(deps) + [zt.free[zi]])
                if r == 0:
                    t_h = stt(P, hb[b2], xres[:, j, :], ALPHA, zt.bufs[zi], ALU.mult, ALU.add,
                              [t_x, tz, h_free[b2]])
                else:
                    t_h = tt(P, "pool", hb[b2], hb[b2], zt.bufs[zi], ALU.add, [t_h, tz])
                zt.free[zi] = t_h
            t_s = None
            for q in range(4):
                t_s = bnstats(P, st6[:, q, :], hb[b2][:, q * 512:(q + 1) * 512], [t_h, t_s])
            t_s = bnaggr(P, mv, st6, [t_s])
            t_s = act(P, rstd, mv[:, 1:2], AF.Sqrt, [t_s, C.t_eps], bias=C.eps_ln)
            t_s = P.op("dve", lambda e: e.reciprocal(out=rstd, in_=rstd), [t_s])
            t_s = ts(P, "dve", nmr, mv[:, 0:1], rstd, -1.0, ALU.mult, ALU.mult, [t_s])
            t_n = act(P, hb[b2], hb[b2], AF.Identity, [t_s, t_h], bias=nmr, scale=rstd)
            t_n = tt(P, "dve", hb[b2], hb[b2], gb, ALU.mult, [t_n, t_b])
            t_n = tt(P, "pool", xres[:, j, :], hb[b2], bb, ALU.add, [t_n, t_b, t_x])
            h_free[b2] = t_n
            t_xj = t_n
        else:
            t_xj = t_x
        t_last_xres = t_xj
        if xT_out is not None:
            t_c = cp(P, "act", xb[b2], xres[:, j, :], [t_xj, xb_free[b2]])
            for g in range(2):
                pb = PS[g].bitcast(BF16).rearrange("p (a b) -> p a b", b=128)
                t_t = None
                for c8 in range(8):
                    cc = g * 8 + c8
                    t_t = tr(P, pb[:, c8, :], xb[b2][:, cc * 128:(cc + 1) * 128], K.ident_bf,
                             [t_c, K.tok, ps_free[g]])
                ps_free[g] = cp(P, "dve", xTs[:, g * 8:(g + 1) * 8, j * 128:(j + 1) * 128], pb, [t_t])
            xb_free[b2] = t_t
    if xT_out is not None:
        out_toks.append(dma(P, "sp", "dxT", xT_out.rearrange("(c p) t -> p c t", p=128), xTs,
                            [ps_free[0], ps_free[1]]))
    if x_out is not None:
        out_toks.append(dma(P, "sp", "dxo", x_out.rearrange("(j p) d -> p j d", p=128), xres,
                            [t_last_xres]))
    A.reset(m0)
    return out_toks


def load_w_bf16(P, key, dst, src, deps=()):
    return dma(P, "pool", key, dst, src.rearrange("(c p) n -> p c n", p=128), deps)


class Rot:
    def __init__(self, bufs):
        self.bufs = bufs
        self.free = [None] * len(bufs)
        self.i = 0

    def next(self):
        k = self.i % len(self.bufs)
        self.i += 1
        return k


def preload_A_fox(P, C, w_d, deps=()):
    A = C.arena
    W = A.alloc([128, 16, 2048], BF16)
    Wf = A.alloc([128, 16, 4], BF16)
    tw = []
    for s in range(4):
        tw.append(load_w_bf16(P, "wA%d" % s, W[:, :, s * 512:(s + 1) * 512], w_d[:, s * 512:(s + 1) * 512], deps))
    twf = load_w_bf16(P, "wAf", Wf, w_d[:, 2048:2052], deps)
    return (W, Wf, tw, twf)


def phase_A_fox(P, C, xT_d, w_d, bf_d, gq_d, gk_d, scr, deps=(), xdeps=None, pre=None):
    nc, A, K, PS = C.nc, C.arena, C.k, C.ps
    if pre is None:
        pre = preload_A_fox(P, C, w_d, deps)
    W, Wf, tw, twf = pre
    gq = A.alloc([128, 1], F32)
    gk = A.alloc([128, 1], F32)
    bfc = A.alloc([4, 1], F32)
    t_small = dma(P, "sp", "smallA", gq, gq_d, deps)
    t_small = dma(P, "sp", "smallA", gk, gk_d, deps)
    t_small = dma(P, "sp", "smallA", bfc, bf_d, deps)
    t_gk = ts(P, "dve", gk, gk, float(np.sqrt(128.0)), None, ALU.mult, None, [t_small])
    xt = Rot([A.alloc([128, 16, 512], BF16) for _ in range(2)])
    sq = Rot([A.alloc([128, 512], BF16) for _ in range(2)])
    rr = Rot([A.alloc([128, 512], F32) for _ in range(2)])
    st_qk = Rot([A.alloc([128, 512], BF16) for _ in range(3)])
    st_g = Rot([A.alloc([128, 512], BF16) for _ in range(2)])
    st_v = Rot([A.alloc([128, 4, 4, 128], BF16) for _ in range(2)])
    fz = A.alloc([4, 512], F32)
    fa = A.alloc([4, 512], F32)
    fm = A.alloc([4, 512], F32)
    pm = Rot([PS[0], PS[1], PS[2], PS[3]])
    pq = Rot([PS[4], PS[5]])
    pf = PS[6]
    pf_free = [None]
    f_free = [None]
    stores = []
    ncr = C.ncr
    pending_ep = []

    def flush():
        while pending_ep:
            pending_ep.pop(0)()

    for tt_i in range(8):
        r, half = tt_i // 2, tt_i % 2
        xk = xt.next()
        X = xt.bufs[xk]
        t_xt = dma(P, "sp", "xt%d" % xk, X,
                   xT_d[r][:, half * 512:(half + 1) * 512].rearrange("(c p) t -> p c t", p=128),
                   list(deps) + [xt.free[xk]] + ([xdeps[r]] if xdeps else []))
        last_pe = None
        units = [("q", h) for h in range(4)] + [("k", h) for h in range(4)] + [("g", h) for h in range(4)]
        for kind, h in units:
            col0 = {"q": 0, "k": 512, "g": 1536}[kind] + h * 128
            wtok = tw[{"q": 0, "k": 1, "g": 3}[kind]]
            b = pm.next()
            pacc = pm.bufs[b]
            t_m = None
            for c in range(16):
                t_m = mm(P, pacc, W[:, c, col0:col0 + 128], X[:, c, :], c == 0, c == 15,
                         [t_xt, wtok, pm.free[b]])
            last_pe = t_m
            flush()
            if kind == "g":
                sb = st_g.next()
                t_e = act(P, st_g.bufs[sb], pacc, AF.Silu, [t_m, st_g.free[sb]])
                pm.free[b] = t_e
                t_st = dma(P, "sp", "stg%d" % sb, scr.sgT[h, :, tt_i * 512:(tt_i + 1) * 512], st_g.bufs[sb], [t_e])
                st_g.free[sb] = t_st
                stores.append(t_st)
            else:
                s_i = sq.next()
                t_sq = act(P, sq.bufs[s_i], pacc, AF.Square, [t_m, sq.free[s_i]])

                def ep(kind=kind, h=h, pacc=pacc, b=b, s_i=s_i, t_sq=t_sq, tt_i=tt_i):
                    qb = pq.next()
                    t_q = mm(P, pq.bufs[qb], K.ones_bf, sq.bufs[s_i], True, True, [t_sq, K.tok, pq.free[qb]])
                    sq.free[s_i] = t_q
                    ri = rr.next()
                    t_r = act(P, rr.bufs[ri], pq.bufs[qb], AF.Sqrt, [t_q, rr.free[ri], C.t_eps], bias=C.eps_qk)
                    pq.free[qb] = t_r
                    t_r = P.op("dve", (lambda o: lambda e: e.reciprocal(out=o, in_=o))(rr.bufs[ri]), [t_r])
                    si = st_qk.next()
                    gcol = gq if kind == "q" else gk
                    t_n = stt(P, st_qk.bufs[si], pacc, gcol, rr.bufs[ri], ALU.mult, ALU.mult,
                              [t_r, t_gk, st_qk.free[si]])
                    rr.free[ri] = t_n
                    pm.free[b] = t_n
                    dst = (scr.qT if kind == "q" else scr.kT)[h, :, tt_i * 512:(tt_i + 1) * 512]
                    t_st = dma(P, "sp", "stqk%d" % si, dst, st_qk.bufs[si], [t_n])
                    st_qk.free[si] = t_st
                    stores.append(t_st)
                pending_ep.append(ep)
        vi = st_v.next()
        VS = st_v.bufs[vi]
        t_ev = None
        for jj in range(4):
            b = pm.next()
            pacc = pm.bufs[b]
            t_m = None
            for c in range(16):
                t_m = mm(P, pacc, X[:, c, jj * 128:(jj + 1) * 128], W[:, c, 1024:1536], c == 0, c == 15,
                         [t_xt, tw[2], pm.free[b]])
            flush()
            t_ev = cp(P, "act", VS[:, :, jj, :], pacc.rearrange("p (h d) -> p h d", d=128),
                      [t_m, st_v.free[vi] if jj == 0 else None])
            pm.free[b] = t_ev
        t_st = dma(P, "sp", "stv%d" % vi, scr.v[:, :, tt_i * 4:(tt_i + 1) * 4, :].rearrange("h p j d -> p h j d"),
                   VS, [t_ev])
        st_v.free[vi] = t_st
        stores.append(t_st)
        t_m = None
        for c in range(16):
            t_m = mm(P, pf[0:4, :], Wf[:, c, :], X[:, c, :], c == 0, c == 15, [t_xt, twf, pf_free[0]])
        last_pe = t_m
        xt.free[xk] = t_m
        t_z = ts(P, "dve", fz, pf[0:4, :], bfc, None, ALU.add, None, [t_m, t_small, f_free[0]])
        pf_free[0] = t_z
        t_a = act(P, fa, fz, AF.Abs, [t_z])
        t_a = act(P, fa, fa, AF.Exp, [t_a], scale=-1.0)
        t_a = act(P, fa, fa, AF.Ln, [t_a], bias=1.0)
        t_mx = ts(P, "dve", fm, fz, -1.0, 0.0, ALU.mult, ALU.max, [t_z])
        t_f = tt(P, "dve", ncr[:, tt_i * 512:(tt_i + 1) * 512], fm, fa, ALU.add, [t_mx, t_a])
        f_free[0] = t_f
    flush()
    C.t_ncr = t_f
    return stores


def phase_B_fox(P, C, gqrow_d, gkrow_d, scr, deps=()):
    nc, A, K, PS = C.nc, C.arena, C.k, C.ps
    ncr = C.ncr
    ones1 = A.alloc([4, 1], F32)
    ncc = A.alloc([4, S], F32)
    t0 = P.op("pool", lambda e: e.memset(ones1, 1.0), list(deps))
    onesr = ones1.to_broadcast([4, S])
    t_scan = P.op("dve", lambda e: e.tensor_tensor_scan(out=ncc, data0=onesr, data1=ncr, initial=0.0,
                                                        op0=ALU.mult, op1=ALU.add), [t0, C.t_ncr])
    gqb = A.alloc([128, 128], F32)
    gkb = A.alloc([128, 128], F32)
    t_g = dma(P, "sp", "gB", gqb, gqrow_d.partition_broadcast(128), deps)
    t_g = dma(P, "sp", "gB", gkb, gkrow_d.partition_broadcast(128), deps)
    mq = A.alloc([128, 1], F32)
    mk = A.alloc([128, 1], F32)
    negM = A.alloc([128, 1], F32)
    t_m = act(P, gqb, gqb, AF.Abs, [t_g])
    t_m = P.op("dve", lambda e: e.reduce_max(out=mq, in_=gqb, axis=AX.X), [t_m])
    t_m = act(P, gkb, gkb, AF.Abs, [t_m])
    t_m = P.op("dve", lambda e: e.reduce_max(out=mk, in_=gkb, axis=AX.X), [t_m])
    t_m = ts(P, "dve", negM, mq, mk, -float(128.0 ** 0.5), ALU.mult, ALU.mult, [t_m])
    biasK = A.alloc([128, 32, 4], F32)
    pT = PS[7]
    t_t = None
    for j in range(32):
        t_t = tr(P, pT[:, j * 4:(j + 1) * 4], ncc[:, j * 128:(j + 1) * 128], K.ident_f[0:4, 0:4],
                 [t_scan, K.tok])
    t_bk = ts(P, "dve", biasK, pT[:, 0:128].rearrange("p (j h) -> p j h", h=4), negM, None, ALU.add, None,
              [t_t, t_m])
    hb = []
    for i in range(2):
        h_ = Ctx()
        h_.q = A.alloc([128, S], BF16)
        h_.k = A.alloc([128, S], BF16)
        h_.v = A.alloc([128, 32, 128], BF16)
        h_.sg = A.alloc([128, S], BF16)
        h_.ncq = A.alloc([128, S], F32)
        h_.free = None
        hb.append(h_)
    sa = Rot([A.alloc([128, 512], F32) for _ in range(4)])
    pt = Rot([A.alloc([128, 512], BF16) for _ in range(4)])
    rl = A.alloc([128, 512], F32)
    of = A.alloc([128, 512], F32)
    ys = Rot([A.alloc([128, 512], BF16) for _ in range(2)])
    pS = Rot([PS[0], PS[1], PS[2], PS[7]])
    pS.free[3] = t_bk
    pO = [PS[3], PS[4]]
    pL = [PS[5], PS[6]]
    pOL_free = [None, None]
    rl_free = [None]
    stores = []
    hstores = []
    LAG = 3

    pairs = []
    for h in range(4):
        for t in range(8):
            for j in range(4 * t + 4):
                pairs.append((h, t, j))
    loaded = {}
    pv_q = []

    def load_head(h):
        H = hb[h % 2]
        dd = list(deps) + [H.free]
        tl = dma(P, "sp", "hq%d" % (h % 2), H.q, scr.qT[h], dd)
        tl = dma(P, "sp", "hq%d" % (h % 2), H.k, scr.kT[h], dd)
        tl = dma(P, "sp", "hq%d" % (h % 2), H.v, scr.v[h], dd)
        tl = dma(P, "sp", "hq%d" % (h % 2), H.sg, scr.sgT[h], dd)
        t_c = None
        for t8 in range(8):
            b = pS.next()
            t_b = mm(P, pS.bufs[b], K.sel[:, h, :], ncc[:, t8 * 512:(t8 + 1) * 512], True, True,
                     [t_scan, K.tok, pS.free[b], H.free])
            t_c = cp(P, "act", H.ncq[:, t8 * 512:(t8 + 1) * 512], pS.bufs[b], [t_b, H.free])
            pS.free[b] = t_c
        loaded[h] = (tl, t_c)

    def front(h, t, j):
        H = hb[h % 2]
        tl, t_c = loaded[h]
        diag = j >= 4 * t
        off = (j - 4 * t) * 128 if diag else 0
        N = 512 - off
        q0 = t * 512 + off
        b = pS.next()
        ps = pS.bufs[b]
        t_s = mm(P, ps[:, 0:N], H.k[:, j * 128:(j + 1) * 128], H.q[:, q0:q0 + N], True, not diag,
                 [tl, pS.free[b]])
        if diag:
            t_s = mm(P, ps[:, 0:128], K.ident_bf, K.negmask_bf, False, True, [K.tok])
        si = sa.next()
        t_d = tt(P, "dve", sa.bufs[si][:, 0:N], ps[:, 0:N], H.ncq[:, q0:q0 + N], ALU.subtract,
                 [t_s, t_c, sa.free[si]])
        pS.free[b] = t_d
        pi = pt.next()
        t_e = act(P, pt.bufs[pi][:, 0:N], sa.bufs[si][:, 0:N], AF.Exp, [t_d, t_bk, pt.free[pi]],
                  bias=biasK[:, j, h:h + 1])
        sa.free[si] = t_e
        return (h, t, j, off, N, pi, t_e)

    def back(h, t, j, off, N, pi, t_e):
        H = hb[h % 2]
        ob = t % 2
        last = (j == 4 * t + 3)
        t_o = mm(P, pO[ob][:, off:512], H.v[:, j, :], pt.bufs[pi][:, 0:N], j == 0, last,
                 [t_e, pOL_free[ob] if j == 0 else None])
        t_l = mm(P, pL[ob][:, off:512], K.ones_bf, pt.bufs[pi][:, 0:N], j == 0, last, [])
        pt.free[pi] = t_l
        if last:
            t_r = P.op("dve", lambda e: e.reciprocal(out=rl, in_=pL[ob]), [t_l, rl_free[0]])
            t_f = tt(P, "dve", of, pO[ob], rl, ALU.mult, [t_r])
            pOL_free[ob] = t_f
            yi = ys.next()
            t_y = tt(P, "dve", ys.bufs[yi], of, H.sg[:, t * 512:(t + 1) * 512], ALU.mult,
                     [t_f, ys.free[yi], loaded[h][0]])
            rl_free[0] = t_y
            split = (h == 3 and getattr(scr, "yl3", None) is not None)
            if split:
                dst = scr.yl3[t // 4][:, (t % 4) * 512:(t % 4 + 1) * 512]
            else:
                dst = scr.yl[h][:, t * 512:(t + 1) * 512]
            t_st = dma(P, "sp", "sty%d" % yi, dst, ys.bufs[yi], [t_y])
            ys.free[yi] = t_st
            hstores.append(t_st)
            C.last_stores.append(t_st)
            if split and t == 3:
                stores.append(coll(P, "ccy", "AllGather", ALU.bypass, GROUPS,
                                   scr.yl3[0].rearrange("p (a t) -> (p a) t", a=2),
                                   scr.yg3[0].rearrange("p (a t) -> (p a) t", a=2), list(hstores)))
                del hstores[:]
            if t == 7:
                H.free = t_y
                if split:
                    stores.append(coll(P, "ccy", "AllGather", ALU.bypass, GROUPS,
                                       scr.yl3[1].rearrange("p (a t) -> (p a) t", a=2),
                                       scr.yg3[1].rearrange("p (a t) -> (p a) t", a=2), list(hstores)))
                elif scr.yg is not None:
                    stores.append(coll(P, "ccy", "AllGather", ALU.bypass, GROUPS,
                                       scr.yl[h].rearrange("p (a t) -> (p a) t", a=4),
                                       scr.yg[h].rearrange("p (a t) -> (p a) t", a=4), list(hstores)))
                else:
                    stores.extend(hstores)
                del hstores[:]

    load_head(0)
    for i, (h, t, j) in enumerate(pairs):
        if t == 0 and j == 0 and h + 1 < 4:
            pass
        pv_q.append(front(h, t, j))
        if len(pv_q) > LAG:
            back(*pv_q.pop(0))
        if t == 1 and j == 0 and h + 1 < 4:
            while pv_q and pv_q[0][0] < h:
                back(*pv_q.pop(0))
            load_head(h + 1)
    while pv_q:
        back(*pv_q.pop(0))
    return stores


def phase_C(P, C, yT_d, wo_d, part_out, ec, deps=()):
    nc, A, K, PS = C.nc, C.arena, C.k, C.ps
    Wo = A.alloc([128, ec, D], BF16)
    t_w = load_w_bf16(P, "wC", Wo, wo_d, deps)
    Y = A.alloc([128, ec, S], BF16)
    t_y = []
    for q in range(4):
        t_y.append(dma(P, "sp", "yC%d" % q, Y[:, :, q * 1024:(q + 1) * 1024],
                       yT_d[:, :, q * 1024:(q + 1) * 1024].rearrange("e p t -> p e t"), deps))
    ost = Rot([A.alloc([128, D], F32) for _ in range(2)])
    pm = Rot([PS[0], PS[1], PS[2], PS[3]])
    stores = []
    n = 0
    for j in range(32):
        oi = ost.next()
        O = ost.bufs[oi]
        t_e = None
        for dt in range(4):
            b = pm.next()
            t_m = None
            for e_ in range(ec):
                t_m = mm(P, pm.bufs[b], Y[:, e_, j * 128:(j + 1) * 128], Wo[:, e_, dt * 512:(dt + 1) * 512],
                         e_ == 0, e_ == ec - 1, [t_w, t_y[j // 8], pm.free[b]])
            eng = "act" if n % 2 == 0 else "dve"
            n += 1
            t_e = cp(P, eng, O[:, dt * 512:(dt + 1) * 512], pm.bufs[b], [t_m, ost.free[oi] if dt == 0 else None])
            pm.free[b] = t_e
            if dt == 2:
                t_e2 = t_e
        t_st = dma(P, "sp", "stC%d" % oi, part_out[j * 128:(j + 1) * 128, :], O, [t_e, t_e2])
        ost.free[oi] = t_st
        stores.append(t_st)
    return stores


ARENA_BYTES = 204 * 1024


def new_prog():
    nc = bass.Bass("TRN2", target_bir_lowering=False)
    return nc


class Scr:
    pass


def setup(nc, st):
    C = Ctx()
    C.nc = nc
    ar = st.enter_context(nc.sbuf_tensor("arena", [128, ARENA_BYTES], U8))
    C.arena = Arena(ar, ARENA_BYTES)
    C.ps = [st.enter_context(nc.psum_tensor("ps%d" % i, [128, 512], F32)) for i in range(8)]
    C.ps = [p[:, :] for p in C.ps]
    return C


def build_D(first, last):
    nc = new_prog()
    x_in = nc.dram_tensor("x_in", [TOKC, D], F32, kind="ExternalInput").ap()
    if not first:
        parts = nc.dram_tensor("parts", [4, TOKC, D], F32, kind="ExternalInput").ap()
        lng = nc.dram_tensor("lng", [1, D], F32, kind="ExternalInput").ap()
        lnb = nc.dram_tensor("lnb", [1, D], F32, kind="ExternalInput").ap()
    xT_out = None if last else nc.dram_tensor("xT_out", [D, TOKC], BF16, kind="ExternalOutput").ap()
    x_out = None if first else nc.dram_tensor("x_out", [TOKC, D], F32, kind="ExternalOutput").ap()
    with contextlib.ExitStack() as st:
        C = setup(nc, st)
        P = Prog(nc)
        load_consts(P, C)
        xres = C.arena.alloc([128, 8, D], F32)
        if first:
            outs = phase_D(P, C, xres, x_in, None, 0, None, None, xT_out, None)
        else:
            outs = phase_D(P, C, xres, x_in, parts, 4, lng[0], lnb[0], xT_out, x_out)
        P.emit(final_waits=outs)
    return nc


def build_L_fox():
    nc = new_prog()
    xT_d = nc.dram_tensor("xT", [4, D, TOKC], BF16, kind="ExternalInput").ap()
    w_d = nc.dram_tensor("w_in", [D, 2052], F32, kind="ExternalInput").ap()
    bf_d = nc.dram_tensor("bf", [4, 1], F32, kind="ExternalInput").ap()
    gq_d = nc.dram_tensor("gq", [128, 1], F32, kind="ExternalInput").ap()
    gk_d = nc.dram_tensor("gk", [128, 1], F32, kind="ExternalInput").ap()
    gqr_d = nc.dram_tensor("gqr", [1, 128], F32, kind="ExternalInput").ap()
    gkr_d = nc.dram_tensor("gkr", [1, 128], F32, kind="ExternalInput").ap()
    wo_d = nc.dram_tensor("w_out", [512, D], F32, kind="ExternalInput").ap()
    part = nc.dram_tensor("part", [S, D], F32, kind="ExternalOutput").ap()
    scr = Scr()
    scr.qT = nc.dram_tensor("s_qT", [4, 128, S], BF16).ap()
    scr.kT = nc.dram_tensor("s_kT", [4, 128, S], BF16).ap()
    scr.v = nc.dram_tensor("s_v", [4, 128, 32, 128], BF16).ap()
    scr.sgT = nc.dram_tensor("s_sgT", [4, 128, S], BF16).ap()
    scr.yT = nc.dram_tensor("s_yT", [4, 128, S], BF16).ap()
    with contextlib.ExitStack() as st:
        C = setup(nc, st)
        P = Prog(nc)
        load_consts(P, C)
        C.ncr = C.arena.alloc([4, S], F32)
        m0 = C.arena.mark()
        stA = phase_A_fox(P, C, xT_d, w_d, bf_d, gq_d, gk_d, scr)
        P.barrier()
        C.arena.reset(m0)
        stB = phase_B_fox(P, C, gqr_d[0], gkr_d[0], scr, deps=stA)
        P.barrier()
        C.arena.reset(m0)
        stC = phase_C(P, C, scr.yT, wo_d, part, 4, deps=stB)
        P.emit(final_waits=stC)
    return nc


def load_ret_w(P, Wdst, w_d, hp_, deps=()):
    tw_ = []
    for s_ in range(3):
        tw_.append(load_w_bf16(P, "wR%d_%d" % (hp_, s_), Wdst[:, :, s_ * 512:(s_ + 1) * 512],
                               w_d[:, hp_ * 1536 + s_ * 512:hp_ * 1536 + (s_ + 1) * 512], list(deps)))
    return tw_


def phase_A_ret(P, C, xT_d, w_d, cos_d, sin_d, scr, deps=(), xdeps=None, after_w=None, pre_w0=None):
    nc, A, K, PS = C.nc, C.arena, C.k, C.ps
    m0 = A.mark()
    stores = []
    if pre_w0 is None:
        Wb = [A.alloc([128, 16, 1536], BF16) for _ in range(2)]
    else:
        Wb = [pre_w0[0], A.alloc([128, 16, 1536], BF16)]
    xt = Rot([A.alloc([128, 16, 512], BF16) for _ in range(2)])
    cs = Rot([A.alloc([128, 2, 512], F32) for _ in range(2)])
    twb = []
    for hp_ in range(2):
        if hp_ == 0 and pre_w0 is not None:
            twb.append(pre_w0[1])
            continue
        twb.append(load_ret_w(P, Wb[hp_], w_d, hp_, deps))
    if after_w is not None:
        C.t_wo = after_w()
    tmp = Rot([A.alloc([128, 2, 512], F32) for _ in range(2)])
    st_r = Rot([A.alloc([128, 512], BF16) for _ in range(4)])
    st_g = Rot([A.alloc([128, 512], BF16) for _ in range(2)])
    st_v = Rot([A.alloc([128, 4, 512], BF16) for _ in range(2)])
    pm = Rot([PS[i] for i in range(6)])
    w_free = None
    for hp in range(2):
        tw = twb[hp]
        W = Wb[hp]
        for tt_i in range(8):
            r, half = tt_i // 2, tt_i % 2
            xk = xt.next()
            X = xt.bufs[xk]
            t_xt = dma(P, "sp", "xt%d" % xk, X,
                       xT_d[r][:, half * 512:(half + 1) * 512].rearrange("(c p) t -> p c t", p=128),
                       list(deps) + [xt.free[xk]] + ([xdeps[r]] if xdeps else []))
            ci = cs.next()
            CS = cs.bufs[ci]
            t_cs = dma(P, "sp", "cs%d" % ci, CS[:, 0, :], cos_d[:, tt_i * 512:(tt_i + 1) * 512], list(deps) + [cs.free[ci]])
            t_cs = dma(P, "sp", "cs%d" % ci, CS[:, 1, :], sin_d[:, tt_i * 512:(tt_i + 1) * 512], list(deps) + [cs.free[ci]])
            t_last_cs = None
            for kind in ("q", "k"):
                col0 = 0 if kind == "q" else 256
                scale = 1.0 if kind == "q" else 1.0 / 16.0
                banks = []
                t_m = None
                for hf in range(2):
                    b = pm.next()
                    banks.append(b)
                    for c in range(16):
                        t_m = mm(P, pm.bufs[b], W[:, c, col0 + hf * 128:col0 + (hf + 1) * 128], X[:, c, :],
                                 c == 0, c == 15, [t_xt, tw[0], pm.free[b]])
                pa, pb = pm.bufs[banks[0]], pm.bufs[banks[1]]
                dst = scr.qT if kind == "q" else scr.kT
                for oi, (ca, cb, op) in enumerate(((0, 1, ALU.subtract), (1, 0, ALU.add))):
                    ti = tmp.next()
                    T = tmp.bufs[ti]
                    t1 = stt(P, T[:, 0, :], pa, scale, CS[:, ca, :], ALU.mult, ALU.mult, [t_m, t_cs, tmp.free[ti]])
                    t2 = stt(P, T[:, 1, :], pb, scale, CS[:, cb, :], ALU.mult, ALU.mult, [t_m, t_cs])
                    si = st_r.next()
                    t3 = tt(P, "pool", st_r.bufs[si], T[:, 0, :], T[:, 1, :], op, [t1, t2, st_r.free[si]])
                    tmp.free[ti] = t3
                    t_st = dma(P, "sp", "str%d" % si, dst[hp, oi, :, tt_i * 512:(tt_i + 1) * 512], st_r.bufs[si], [t3])
                    st_r.free[si] = t_st
                    stores.append(t_st)
                pm.free[banks[0]] = t2
                pm.free[banks[1]] = t2
                t_last_cs = t2
            cs.free[ci] = t_last_cs
            for gt in range(4):
                b = pm.next()
                t_m = None
                for c in range(16):
                    t_m = mm(P, pm.bufs[b], W[:, c, 1024 + gt * 128:1024 + (gt + 1) * 128], X[:, c, :],
                             c == 0, c == 15, [t_xt, tw[2], pm.free[b]])
                sb = st_g.next()
                t_e = act(P, st_g.bufs[sb], pm.bufs[b], AF.Silu, [t_m, st_g.free[sb]])
                pm.free[b] = t_e
                t_st = dma(P, "sp", "stg%d" % sb, scr.sgT[hp, gt, :, tt_i * 512:(tt_i + 1) * 512], st_g.bufs[sb], [t_e])
                st_g.free[sb] = t_st
                stores.append(t_st)
            vi = st_v.next()
            VS = st_v.bufs[vi]
            t_ev = None
            for jj in range(4):
                b = pm.next()
                t_m = None
                for c in range(16):
                    t_m = mm(P, pm.bufs[b], X[:, c, jj * 128:(jj + 1) * 128], W[:, c, 512:1024],
                             c == 0, c == 15, [t_xt, tw[1], pm.free[b]])
                t_ev = cp(P, "act", VS[:, jj, :], pm.bufs[b], [t_m, st_v.free[vi] if jj == 0 else None])
                pm.free[b] = t_ev
            xt.free[xk] = t_m
            w_free = t_m
            t_st = dma(P, "sp", "stv%d" % vi,
                       scr.v[hp, tt_i * 512:(tt_i + 1) * 512, :].rearrange("(j p) d -> p j d", p=128), VS, [t_ev])
            st_v.free[vi] = t_st
            stores.append(t_st)
    A.reset(m0)
    return stores


def phase_B_ret(P, C, dec_d, scr, deps=()):
    nc, A, K, PS = C.nc, C.arena, C.k, C.ps
    m0 = A.mark()
    stores = []
    hstores = []
    HB = []
    for hp in range(2):
        h_ = Ctx()
        h_.Q = A.alloc([128, 2, S], BF16)
        h_.Kt = A.alloc([128, 2, S], BF16)
        h_.Qd = A.alloc([128, 2, S], BF16)
        h_.dec = A.alloc([128, 258], F32)
        HB.append(h_)
    SGr = Rot([A.alloc([128, 4, 1024], BF16) for _ in range(3)])
    Vr = Rot([A.alloc([128, 8, 512], BF16) for _ in range(2)])
    Rf = A.alloc([128, 2, 512], F32)
    Rb = A.alloc([128, 2, 512], BF16)
    iT = Rot([A.alloc([128, 128], BF16) for _ in range(2)])
    Kd = Rot([A.alloc([128, 256], BF16) for _ in range(2)])
    on = Rot([A.alloc([128, 512], BF16) for _ in range(4)])
    yst = Rot([A.alloc([128, 4, 4, 128], BF16) for _ in range(2)])
    st6 = A.alloc([128, 6], F32)
    mv = A.alloc([128, 2], F32)
    rstd = A.alloc([128, 1], F32)
    nmr = A.alloc([128, 1], F32)
    mvA = [A.alloc([128, 2], F32) for _ in range(4)]
    nmA = [A.alloc([128, 1], F32) for _ in range(4)]
    mv_free = [None] * 4
    pI, pK = PS[0], PS[1].bitcast(BF16)
    pO = [PS[2], PS[3]]
    pTt = PS[4].bitcast(BF16)
    pR = [PS[5], PS[6]]
    free = Ctx()
    free.pI = free.pK = free.pT = None
    free.pO = [None, None]
    free.pR = [None, None]
    head_free = None
    dd = list(deps)
    sg_tok = {}
    sg_buf = {}
    sg_order = [(hp_, q_) for hp_ in range(2) for q_ in range(4)]

    def load_sg(idx):
        hp_, q_ = sg_order[idx]
        k = SGr.next()
        sl_ = slice(q_ * 1024, (q_ + 1) * 1024)
        sg_buf[(hp_, q_)] = (k, SGr.bufs[k])
        sg_tok[(hp_, q_)] = dma(P, "sp", "rsg%d" % k, SGr.bufs[k],
                                scr.sgT[hp_, :, :, sl_].rearrange("g p t -> p g t"), dd + [SGr.free[k]])

    v_tok = {}
    v_buf = {}

    def load_v(idx):
        hp_, q_ = sg_order[idx]
        k = Vr.next()
        sl_ = slice(q_ * 1024, (q_ + 1) * 1024)
        v_buf[(hp_, q_)] = (k, Vr.bufs[k])
        v_tok[(hp_, q_)] = dma(P, "sp", "rv%d" % k, Vr.bufs[k],
                               scr.v[hp_, sl_, :].rearrange("(j p) d -> p j d", p=128), dd + [Vr.free[k]])

    v_next = [2]
    for hp in range(2):
        H = HB[hp]
        H.t_dec = dma(P, "sp", "dec%d" % hp, H.dec, dec_d[hp], dd)
        H.tq, H.tqd = [], []
        for q4 in range(4):
            sl = slice(q4 * 1024, (q4 + 1) * 1024)
            key = "rl%d_%d" % (hp, q4)
            t = dma(P, "sp", key, H.Q[:, :, sl], scr.qT[hp, :, :, sl].rearrange("h p t -> p h t"), dd)
            t = dma(P, "sp", key, H.Kt[:, :, sl], scr.kT[hp, :, :, sl].rearrange("h p t -> p h t"), dd)
            H.tq.append(t)
            if hp == 0 and q4 < 3:
                load_sg(q4)
            if hp == 0 and q4 < 2:
                load_v(q4)
    sg_next = [3]
    qd_tok = {}
    for hp_ in range(2):
        H_ = HB[hp_]
        for q_ in range(4):
            sl_ = slice(q_ * 1024, (q_ + 1) * 1024)
            t2 = None
            for hf in range(2):
                t2 = tt(P, "pool", H_.Qd[:, hf, sl_].rearrange("p (n q) -> p n q", q=128),
                        H_.Q[:, hf, sl_].rearrange("p (n q) -> p n q", q=128),
                        H_.dec[:, 128:256].unsqueeze(1).to_broadcast([128, 8, 128]), ALU.mult,
                        [H_.tq[q_], H_.t_dec])
            qd_tok[(hp_, q_)] = t2
    for hp in range(2):
        H = HB[hp]
        Q, Kt, dec = H.Q, H.Kt, H.dec
        tq, t_dec = H.tq, H.t_dec
        DT = dec[:, 0:128]
        kd = dec[:, 256:257]
        cd = dec[:, 257:258]
        t_r0 = P.op("dve", lambda e: e.memset(Rf, 0.0), [head_free])
        t_rb = None
        t_rf = t_r0

        def inner(n):
            q4 = n // 8
            c0 = n * 128
            t_i = None
            for hf in range(2):
                t_i = mm(P, pI[:, 0:128], Kt[:, hf, c0:c0 + 128], Q[:, hf, c0:c0 + 128], hf == 0, hf == 1,
                         [tq[q4], free.pI])
            ii = iT.next()
            t_it = tt(P, "dve", iT.bufs[ii], pI[:, 0:128], DT, ALU.mult, [t_i, t_dec, iT.free[ii]])
            free.pI = t_it
            t_k = None
            for hf in range(2):
                t_k = tr(P, pK[:, hf * 128:(hf + 1) * 128], Kt[:, hf, c0:c0 + 128], K.ident_bf, [K.tok, free.pK])
            ki = Kd.next()
            t_kd = ts(P, "dve", Kd.bufs[ki], pK[:, 0:256], kd, None, ALU.mult, None, [t_k, Kd.free[ki]])
            free.pK = t_kd
            return (ii, t_it, ki, t_kd)

        nxt = inner(0)
        pend = []

        def finish(n, oi, t_n):
            nonlocal yi_cur
            c0 = n * 128
            q4 = n // 8
            t_t = None
            for vc in range(4):
                t_t = tr(P, pTt[:, vc * 128:(vc + 1) * 128], on.bufs[oi][:, vc * 128:(vc + 1) * 128], K.ident_bf,
                         [t_n, free.pT])
            on.free[oi] = t_t
            nn = n % 4
            if nn == 0:
                yi_cur = yst.next()
            yi = yi_cur
            Y = yst.bufs[yi]
            sgk, SGq = sg_buf[(hp, q4)]
            cq = c0 - q4 * 1024
            t_y = tt(P, "dve", Y[:, :, nn, :], pTt[:, 0:512].rearrange("p (v q) -> p v q", q=128),
                     SGq[:, :, cq:cq + 128], ALU.mult, [t_t, sg_tok[(hp, q4)], yst.free[yi] if nn == 0 else None])
            free.pT = t_y
            if n % 8 == 7:
                SGr.free[sgk] = t_y
                if sg_next[0] < 8:
                    load_sg(sg_next[0])
                    sg_next[0] += 1
            if nn == 3:
                t0 = (n - 3) * 128
                qq, toff = t0 // 1024, t0 % 1024
                dstp = scr.yl[hp][qq].rearrange("(v p) t -> p v t", p=128)[:, :, toff:toff + 512]
                t_st = dma(P, "sp", "sty%d" % yi, dstp.rearrange("p v (n q) -> p v n q", q=128), Y, [t_y])
                yst.free[yi] = t_st
                hstores.append(t_st)
                C.last_stores.append(t_st)
                if toff == 512:
                    if scr.yg is not None:
                        stores.append(coll(P, "ccy", "AllGather", ALU.bypass, GROUPS, scr.yl[hp][qq], scr.yg[hp][qq],
                                           list(hstores)))
                    else:
                        stores.extend(hstores)
                    del hstores[:]
            return t_y

        yi_cur = 0
        t_y = None
        for n in range(32):
            q4 = n // 8
            c0 = n * 128
            ii, t_it, ki, t_kd = nxt
            ob = n % 2
            vk, Vq = v_buf[(hp, q4)]
            t_o = mm(P, pO[ob], iT.bufs[ii], Vq[:, n % 8, :], True, n == 0, [t_it, free.pO[ob], v_tok[(hp, q4)]])
            iT.free[ii] = t_o
            t_u = None
            if n < 31:
                for hf in range(2):
                    t_u = mm(P, pR[hf], Kd.bufs[ki][:, hf * 128:(hf + 1) * 128], Vq[:, n % 8, :], True, True,
                             [t_kd, free.pR[hf]])
                Kd.free[ki] = t_u
                nxt = inner(n + 1)
            if n % 8 == 7:
                Vr.free[vk] = t_u if n < 31 else t_o
                if v_next[0] < 8:
                    load_v(v_next[0])
                    v_next[0] += 1
            if len(pend) == 2:
                t_y = finish(*pend.pop(0))
            if n > 0:
                for hf in range(2):
                    t_o = mm(P, pO[ob], H.Qd[:, hf, c0:c0 + 128], Rb[:, hf, :], False, hf == 1,
                             [qd_tok[(hp, q4)], t_rb])
            t_ocross = t_o
            if n < 31:
                for hf in range(2):
                    t_rf = stt(P, Rf[:, hf, :], Rf[:, hf, :], cd, pR[hf], ALU.mult, ALU.add, [t_u, t_rf, t_dec])
                    free.pR[hf] = t_rf
                t_rb = cp(P, "act", Rb, Rf, [t_rf, t_ocross])
            sl = n % 4
            mv_, nm_ = mvA[sl], nmA[sl]
            t_s = bnstats(P, st6, pO[ob], [t_ocross])
            t_s = bnaggr(P, mv_, st6, [t_s, mv_free[sl]])
            t_s = ts(P, "dve", nm_, mv_[:, 0:1], -1.0, None, ALU.mult, None, [t_s])
            t_a = act(P, rstd, mv_[:, 1:2], AF.Ln, [t_s, C.t_eps], bias=C.eps_gn)
            t_a = act(P, rstd, rstd, AF.Exp, [t_a], scale=-0.5)
            t_a = act(P, nmr, nm_, AF.Identity, [t_a], scale=rstd)
            mv_free[sl] = t_a
            oi = on.next()
            t_n = act(P, on.bufs[oi], pO[ob], AF.Identity, [t_a, on.free[oi]], bias=nmr, scale=rstd)
            free.pO[ob] = t_n
            pend.append((n, oi, t_n))
        while pend:
            t_y = finish(*pend.pop(0))
        head_free = t_y
    A.reset(m0)
    return stores


def build_L_ret():
    nc = new_prog()
    xT_d = nc.dram_tensor("xT", [4, D, TOKC], BF16, kind="ExternalInput").ap()
    w_d = nc.dram_tensor("w_in", [D, 3072], F32, kind="ExternalInput").ap()
    cos_d = nc.dram_tensor("cos", [128, S], F32, kind="ExternalInput").ap()
    sin_d = nc.dram_tensor("sin", [128, S], F32, kind="ExternalInput").ap()
    dec_d = nc.dram_tensor("dec", [2, 128, 258], F32, kind="ExternalInput").ap()
    wo_d = nc.dram_tensor("w_out", [1024, D], F32, kind="ExternalInput").ap()
    part = nc.dram_tensor("part", [S, D], F32, kind="ExternalOutput").ap()
    scr = Scr()
    scr.qT = nc.dram_tensor("s_qT", [2, 2, 128, S], BF16).ap()
    scr.kT = nc.dram_tensor("s_kT", [2, 2, 128, S], BF16).ap()
    scr.v = nc.dram_tensor("s_v", [2, S, 512], BF16).ap()
    scr.sgT = nc.dram_tensor("s_sgT", [2, 4, 128, S], BF16).ap()
    scr.yT = nc.dram_tensor("s_yT", [8, 128, S], BF16).ap()
    with contextlib.ExitStack() as st:
        C = setup(nc, st)
        P = Prog(nc)
        load_consts(P, C)
        stA = phase_A_ret(P, C, xT_d, w_d, cos_d, sin_d, scr)
        P.barrier()
        stB = phase_B_ret(P, C, dec_d, scr, deps=stA)
        P.barrier()
        stC = phase_C(P, C, scr.yT, wo_d, part, 8, deps=stB)
        P.emit(final_waits=stC)
    return nc


def rotary_tables():
    inv_freq = (10000.0 ** (-np.arange(0, 256, 2, dtype=np.float32) / np.float32(256))).astype(np.float32)
    ang = np.arange(S, dtype=np.float32)[:, None] * inv_freq[None, :]
    return np.ascontiguousarray(np.cos(ang).T.astype(np.float32)), np.ascontiguousarray(np.sin(ang).T.astype(np.float32))


def decay_tables(head):
    lg = np.log1p(-np.exp2(np.float32(-5.0 - head))).astype(np.float32)
    pos = np.arange(128, dtype=np.float32)
    diff = pos[:, None] - pos[None, :]
    intra = np.where(diff >= 0, np.exp(np.maximum(diff, 0.0) * lg), 0.0).astype(np.float32)
    qd = np.exp((pos + 1.0) * lg).astype(np.float32)
    kd = np.exp((128 - 1.0 - pos) * lg).astype(np.float32)
    cdv = np.exp(np.float32(128) * lg).astype(np.float32)
    out = np.zeros((128, 258), np.float32)
    out[:, 0:128] = intra.T
    out[:, 128:256] = qd[None, :]
    out[:, 256] = kd
    out[:, 257] = cdv
    return out


CORES = list(range(8))
_PROGS = {}


def _prog(name, fn):
    if name not in _PROGS:
        _PROGS[name] = fn()
    return _PROGS[name]


def _run(nc, maps):
    return run_bass_kernel_spmd(nc, maps, core_ids=CORES).results


def _ca(a):
    return np.ascontiguousarray(a)


def fox_maps(xTf, w, bfv, gq, gk, wo):
    maps = []
    for c in CORES:
        b, g = c // 4, c % 4
        wl = np.concatenate([w[:, 512 * g:512 * g + 512], w[:, 2048 + 512 * g:2048 + 512 * g + 512],
                             w[:, 4096 + 512 * g:4096 + 512 * g + 512], w[:, 6144 + 512 * g:6144 + 512 * g + 512],
                             w[:, 8192 + 4 * g:8192 + 4 * g + 4]], axis=1)
        maps.append({"xT": xTf[b], "w_in": _ca(wl), "bf": _ca(bfv[4 * g:4 * g + 4].reshape(4, 1)),
                     "gq": _ca(gq.reshape(128, 1)), "gk": _ca(gk.reshape(128, 1)),
                     "gqr": _ca(gq.reshape(1, 128)), "gkr": _ca(gk.reshape(1, 128)),
                     "w_out": _ca(wo[512 * g:512 * g + 512, :])})
    return maps


def ret_wcols(w, g):
    cols = []
    for hp in range(2):
        h = 2 * g + hp
        cols += [w[:, h * 256:(h + 1) * 256], w[:, 2048 + h * 256:2048 + (h + 1) * 256],
                 w[:, 4096 + h * 512:4096 + (h + 1) * 512], w[:, 8192 + h * 512:8192 + (h + 1) * 512]]
    return _ca(np.concatenate(cols, 1))


def ret_maps(xTf, w, wo, cosT, sinT):
    maps = []
    for c in CORES:
        b, g = c // 4, c % 4
        maps.append({"xT": xTf[b], "w_in": ret_wcols(w, g), "cos": cosT, "sin": sinT,
                     "dec": _ca(np.stack([decay_tables(2 * g), decay_tables(2 * g + 1)], 0)),
                     "w_out": _ca(wo[1024 * g:1024 * (g + 1), :])})
    return maps


def kernel_unfused(x, fox_w_in, fox_b_f, fox_q_gain, fox_k_gain, fox_w_out, ret_w_in, ret_w_out,
                   ln_gain, ln_bias):
    x = np.asarray(x, np.float32)
    xs = [_ca(x[c // 4, (c % 4) * 1024:(c % 4 + 1) * 1024, :]) for c in CORES]
    cosT, sinT = rotary_tables()
    r = _run(_prog("D0", lambda: build_D(True, False)), [{"x_in": xs[c]} for c in CORES])
    xT = [r[c]["xT_out"] for c in CORES]
    for i in range(DEPTH):
        j = i // 2
        xTf = [_ca(np.stack(xT[4 * b:4 * b + 4], 0)) for b in range(2)]
        if i % 2 == 0:
            maps = fox_maps(xTf, np.asarray(fox_w_in[j]), np.asarray(fox_b_f[j]), np.asarray(fox_q_gain[j]),
                            np.asarray(fox_k_gain[j]), np.asarray(fox_w_out[j]))
            r = _run(_prog("Lfox", build_L_fox), maps)
        else:
            maps = ret_maps(xTf, np.asarray(ret_w_in[j]), np.asarray(ret_w_out[j]), cosT, sinT)
            r = _run(_prog("Lret", build_L_ret), maps)
        parts = [r[c]["part"] for c in CORES]
        last = (i == DEPTH - 1)
        maps = []
        for c in CORES:
            b, g = c // 4, c % 4
            pp = _ca(np.stack([parts[4 * b + q][g * 1024:(g + 1) * 1024] for q in range(4)], 0))
            maps.append({"x_in": xs[c], "parts": pp, "lng": _ca(np.asarray(ln_gain)[i:i + 1]),
                         "lnb": _ca(np.asarray(ln_bias)[i:i + 1])})
        r = _run(_prog("Dlast" if last else "D", (lambda: build_D(False, True)) if last else (lambda: build_D(False, False))), maps)
        xs = [r[c]["x_out"] for c in CORES]
        if not last:
            xT = [r[c]["xT_out"] for c in CORES]
    out = np.zeros((NB, S, D), np.float32)
    for c in CORES:
        out[c // 4, (c % 4) * 1024:(c % 4 + 1) * 1024, :] = xs[c]
    return out


def fused_maps(x, fox_w_in, fox_b_f, fox_q_gain, fox_k_gain, fox_w_out, ret_w_in, ret_w_out, ln_gain, ln_bias):
    cosT, sinT = rotary_tables()
    maps = []
    for c in CORES:
        b, g = c // 4, c % 4
        dsl = slice(g * DL, (g + 1) * DL)
        m = {"x_in": _ca(x[b][:, dsl]), "cos": cosT, "sin": sinT,
             "dec": _ca(np.stack([decay_tables(2 * g), decay_tables(2 * g + 1)], 0)),
             "lng": _ca(ln_gain[:, dsl]), "lnb": _ca(ln_bias[:, dsl])}
        for j in range(2):
            w = fox_w_in[j]
            m["fw_in%d" % j] = _ca(np.concatenate(
                [w[:, 512 * g:512 * g + 512], w[:, 2048 + 512 * g:2048 + 512 * g + 512],
                 w[:, 4096 + 512 * g:4096 + 512 * g + 512], w[:, 6144 + 512 * g:6144 + 512 * g + 512],
                 w[:, 8192 + 4 * g:8192 + 4 * g + 4]], axis=1))
            m["fbf%d" % j] = _ca(fox_b_f[j][4 * g:4 * g + 4].reshape(4, 1))
            m["fgq%d" % j] = _ca(fox_q_gain[j].reshape(128, 1))
            m["fgk%d" % j] = _ca(fox_k_gain[j].reshape(128, 1))
            m["fgqr%d" % j] = _ca(fox_q_gain[j].reshape(1, 128))
            m["fgkr%d" % j] = _ca(fox_k_gain[j].reshape(1, 128))
            m["fwo%d" % j] = _ca(fox_w_out[j][:, dsl])
            m["rw_in%d" % j] = ret_wcols(ret_w_in[j], g)
            m["rwo%d" % j] = _ca(ret_w_out[j][:, dsl])
        maps.append(m)
    return maps


def kernel(**inputs):
    inp = {k: np.asarray(v, np.float32) for k, v in inputs.items()}
    maps = fused_maps(**inp)
    nc = _prog("fused", build_fused)
    r = _run(nc, maps)
    out = np.zeros((NB, S, D), np.float32)
    for c in CORES:
        out[c // 4][:, (c % 4) * DL:(c % 4 + 1) * DL] = r[c]["out"]
    return out


GROUPS = [[0, 1, 2, 3], [4, 5, 6, 7]]
DL = 512


def phase_E(P, C, xres, mode, ysrc, ec, wo_d, lng_d, lnb_d, x_in, xs_l, xg_l, st_l, st_g, x_out, deps=(), ydeps=(), pre_wo=None):
    nc, A, K, PS = C.nc, C.arena, C.k, C.ps
    m0 = A.mark()
    outs = []
    xres = A.alloc([128, 32, DL], F32)
    t_xq = [None] * 4

    def load_x(q):
        t_xq[q] = dma(P, "sp", "ex%d" % q, xres[:, q * 8:(q + 1) * 8, :],
                      x_in[q * 1024:(q + 1) * 1024, :].rearrange("(j p) d -> p j d", p=128), deps)

    if mode == "init":
        for q_ in range(4):
            load_x(q_)
    C.t_xres = None
    if mode != "init":
        if pre_wo is None:
            Wo = A.alloc([128, ec, DL], BF16)
            t_w = load_w_bf16(P, "wE", Wo, wo_d, deps)
        else:
            Wo, t_w = pre_wo
        gb = A.alloc([128, DL], F32)
        bb = A.alloc([128, DL], F32)
        t_gb = dma(P, "sp", "egb", gb, lng_d.partition_broadcast(128), deps)
        t_gb = dma(P, "sp", "egb", bb, lnb_d.partition_broadcast(128), deps)
        Y = Rot([A.alloc([128, ec, 512], BF16) for _ in range(2)])
        S12 = [A.alloc([128, 16], F32) for _ in range(2)]
        SG_ = [A.alloc([128, 4, 16], F32) for _ in range(2)]
        ssum = A.alloc([128, 16], F32)
        mean = A.alloc([128, 8], F32)
        var = A.alloc([128, 8], F32)
        rstd = [A.alloc([128, 8], F32) for _ in range(2)]
        nmr = [A.alloc([128, 8], F32) for _ in range(2)]
        junk = A.alloc([128, DL], F32)
        tmpn = Rot([A.alloc([128, DL], F32) for _ in range(3)])
    xTs = Rot([A.alloc([128, 4, 1024], BF16) for _ in range(2)])
    pm = Rot([PS[0], PS[1], PS[2], PS[3]])
    pt = Rot([PS[4], PS[5]])
    st_free = [None, None]
    t_stats = [None] * 4
    t_h = [None] * 32

    def out_proj(q):
        for s2 in range(2):
            q2 = q * 2 + s2
            yi = Y.next()
            t_y = C.yload(P, "ey%d" % yi, q2, Y.bufs[yi], list(deps) + C.ydeps(q2) + [Y.free[yi]])
            if s2 == 0:
                load_x(q)
            for jj in range(4):
                j = q2 * 4 + jj
                b = pm.next()
                t_m = None
                for e_ in range(ec):
                    t_m = mm(P, pm.bufs[b], Y.bufs[yi][:, e_, jj * 128:(jj + 1) * 128], Wo[:, e_, :],
                             e_ == 0, e_ == ec - 1, [t_w, t_y, pm.free[b]])
                t_hh = stt(P, xres[:, j, :], xres[:, j, :], ALPHA, pm.bufs[b], ALU.mult, ALU.add, [t_m, t_xq[q]])
                pm.free[b] = t_hh
                jq = j % 8
                sb = S12[q % 2]
                t_a = P.op("act", (lambda o, i, a: lambda e: e.activation(out=o, in_=i, func=AF.Identity, accum_out=a))(
                    junk, xres[:, j, :], sb[:, jq:jq + 1]), [t_hh, st_free[q % 2] if jq == 0 else None])
                t_a = P.op("act", (lambda o, i, a: lambda e: e.activation(out=o, in_=i, func=AF.Square, accum_out=a))(
                    junk, xres[:, j, :], sb[:, 8 + jq:9 + jq]), [t_a])
                t_h[j] = t_a
            Y.free[yi] = t_m
        t1 = dma(P, "sp", "est%d" % (q % 2), st_l[q], S12[q % 2], [t_h[q * 8 + 7]])
        st_free[q % 2] = t1
        t2 = coll(P, "ccst", "AllGather", ALU.bypass, GROUPS, st_l[q], st_g[q], [t1])
        t3 = dma(P, "sp", "esg%d" % (q % 2), SG_[q % 2], st_g[q].rearrange("(r p) s -> p r s", p=128), [t2])
        t_stats[q] = t3

    def normalize(q):
        if mode != "init":
            G_ = SG_[q % 2]
            t_s = tt(P, "dve", ssum, G_[:, 0, :], G_[:, 1, :], ALU.add, [t_stats[q]])
            t_s = tt(P, "dve", ssum, ssum, G_[:, 2, :], ALU.add, [t_s])
            t_s = tt(P, "dve", ssum, ssum, G_[:, 3, :], ALU.add, [t_s])
            t_s = ts(P, "dve", mean, ssum[:, 0:8], 1.0 / D, None, ALU.mult, None, [t_s])
            t_s = tt(P, "dve", var, mean, mean, ALU.mult, [t_s])
            t_s = stt(P, var, ssum[:, 8:16], 1.0 / D, var, ALU.mult, ALU.subtract, [t_s])
            t_s = act(P, rstd[q % 2], var, AF.Sqrt, [t_s, C.t_eps], bias=C.eps_ln)
            t_s = P.op("dve", (lambda o: lambda e: e.reciprocal(out=o, in_=o))(rstd[q % 2]), [t_s])
            t_s = stt(P, nmr[q % 2], mean, -1.0, rstd[q % 2], ALU.mult, ALU.mult, [t_s])
        xi = xTs.next() if mode != "last" else None
        t_c = None
        t_fin = None
        st1 = {}

        def stage1(jq):
            j = q * 8 + jq
            if mode != "init":
                ti = tmpn.next()
                T = tmpn.bufs[ti]
                t_n = act(P, T, xres[:, j, :], AF.Identity, [t_s, tmpn.free[ti]],
                          bias=nmr[q % 2][:, jq:jq + 1], scale=rstd[q % 2][:, jq:jq + 1])
                t_n = tt(P, "dve", T, T, gb, ALU.mult, [t_n, t_gb])
                t_n = tt(P, "dve", xres[:, j, :], T, bb, ALU.add, [t_n])
                tmpn.free[ti] = t_n
                C.t_xres = t_n
                st1[jq] = t_n
            else:
                st1[jq] = t_xq[q]

        def stage2(jq):
            j = q * 8 + jq
            pi = pt.next()
            pb = pt.bufs[pi].rearrange("p (a b) -> p a b", b=128)
            t_t = None
            for c4 in range(4):
                t_t = tr(P, pb[:, c4, :], xres[:, j, c4 * 128:(c4 + 1) * 128], K.ident_f,
                         [st1[jq], K.tok, pt.free[pi]])
            t_cc = cp(P, "dve", xTs.bufs[xi][:, :, jq * 128:(jq + 1) * 128], pb,
                      [t_t, xTs.free[xi] if jq == 0 else None])
            pt.free[pi] = t_cc
            return t_cc

        for jq in range(8 + 2):
            if jq < 8:
                stage1(jq)
                t_fin = st1[jq]
            if mode != "last" and jq >= 2:
                t_c = stage2(jq - 2)
        if mode != "last":
            t1 = dma(P, "sp", "exs%d" % xi, xs_l[q].rearrange("(c p) t -> p c t", p=128), xTs.bufs[xi], [t_c])
            xTs.free[xi] = t1
            C.e_stores.append(t1)
            outs.append(coll(P, "ccx", "AllGather", ALU.bypass, GROUPS, xs_l[q], xg_l[q], [t1]))
            if mode == "mid":
                C.x_spill.append(dma(P, "sp", "espill", C.xsp[q * 1024:(q + 1) * 1024, :].rearrange("(j p) d -> p j d", p=128),
                                     xres[:, q * 8:(q + 1) * 8, :], [t_fin]))
        else:
            outs.append(dma(P, "sp", "eout", x_out[q * 1024:(q + 1) * 1024, :].rearrange("(j p) d -> p j d", p=128),
                            xres[:, q * 8:(q + 1) * 8, :], [t_fin]))

    if mode == "init":
        for q in range(4):
            normalize(q)
    else:
        out_proj(0)
        for q in range(1, 4):
            out_proj(q)
            normalize(q - 1)
        normalize(3)
    A.reset(m0)
    return outs


def build_fused(nlayers=DEPTH):
    nc = new_prog()
    def ext(name, shape, dt=F32):
        return nc.dram_tensor(name, shape, dt, kind="ExternalInput").ap()
    def scratch(name, shape, dt=BF16):
        return nc.dram_tensor(name, shape, dt).ap()
    x_in = ext("x_in", [S, DL])
    fox = []
    for j in range(2):
        f = Ctx()
        f.w = ext("fw_in%d" % j, [D, 2052]); f.bf = ext("fbf%d" % j, [4, 1])
        f.gq = ext("fgq%d" % j, [128, 1]); f.gk = ext("fgk%d" % j, [128, 1])
        f.gqr = ext("fgqr%d" % j, [1, 128]); f.gkr = ext("fgkr%d" % j, [1, 128])
        f.wo = ext("fwo%d" % j, [D, DL])
        fox.append(f)
    ret = []
    for j in range(2):
        r_ = Ctx()
        r_.w = ext("rw_in%d" % j, [D, 3072]); r_.wo = ext("rwo%d" % j, [2 * D, DL])
        ret.append(r_)
    cos_d = ext("cos", [128, S]); sin_d = ext("sin", [128, S]); dec_d = ext("dec", [2, 128, 258])
    lng = ext("lng", [DEPTH, DL]); lnb = ext("lnb", [DEPTH, DL])
    out = nc.dram_tensor("out", [S, DL], F32, kind="ExternalOutput").ap()
    fs = Scr()
    fs.qT = scratch("f_qT", [4, 128, S]); fs.kT = scratch("f_kT", [4, 128, S])
    fs.v = scratch("f_v", [4, 128, 32, 128]); fs.sgT = scratch("f_sgT", [4, 128, S])
    fs.yl = [scratch("f_yl%d" % h, [128, S]) for h in range(4)]
    fs.yg = [scratch("f_yg%d" % h, [512, S]) for h in range(4)]
    fs.yl3 = [scratch("f_yl3_%d" % k, [128, S // 2]) for k in range(2)]
    fs.yg3 = [scratch("f_yg3_%d" % k, [512, S // 2]) for k in range(2)]
    rs = Scr()
    rs.qT = scratch("r_qT", [2, 2, 128, S]); rs.kT = scratch("r_kT", [2, 2, 128, S])
    rs.v = scratch("r_v", [2, S, 512]); rs.sgT = scratch("r_sgT", [2, 4, 128, S])
    rs.yl = [[scratch("r_yl%d_%d" % (hp, q), [512, 1024]) for q in range(4)] for hp in range(2)]
    rs.yg = [[scratch("r_yg%d_%d" % (hp, q), [2048, 1024]) for q in range(4)] for hp in range(2)]
    xs_l = [scratch("xs%d" % q, [DL, 1024]) for q in range(4)]
    xg_l = [scratch("xg%d" % q, [D, 1024]) for q in range(4)]
    st_l = [scratch("stl%d" % q, [128, 16], F32) for q in range(4)]
    st_g = [scratch("stg%d" % q, [512, 16], F32) for q in range(4)]
    xsp = scratch("xsp", [S, DL], F32)

    def yload_fox(P, key, q2, Yb, deps):
        t = None
        Yv = Yb.rearrange("p (r h) t -> p r h t", h=4)
        for h in range(4):
            if h == 3:
                src = fs.yg3[q2 // 4][:, (q2 % 4) * 512:(q2 % 4 + 1) * 512]
            else:
                src = fs.yg[h][:, q2 * 512:(q2 + 1) * 512]
            t = dma(P, "sp", key, Yv[:, :, h, :], src.rearrange("(r p) t -> p r t", p=128), deps)
        return t

    def yload_ret(P, key, q2, Yb, deps):
        t = None
        Yv = Yb.rearrange("p (r h v) t -> p r h v t", h=2, v=4)
        q, s2 = q2 // 2, q2 % 2
        for hp in range(2):
            for r in range(4):
                t = dma(P, "sp", key, Yv[:, r, hp, :, :],
                        rs.yg[hp][q][r * 512:(r + 1) * 512, s2 * 512:(s2 + 1) * 512].rearrange("(v p) t -> p v t", p=128),
                        deps)
        return t

    with contextlib.ExitStack() as st:
        C = setup(nc, st)
        P = Prog(nc)
        load_consts(P, C)
        C.xsp = xsp
        C.x_spill = []
        C.e_stores = []
        C.last_stores = []
        C.t_xres = None
        C.pre_ret = None
        mtop = C.arena.mark()
        Wo_cur = C.arena.alloc([128, 16, DL], BF16)
        m_base = C.arena.mark()
        C.ncr = C.arena.alloc([4, S], F32)
        m_ncr = C.arena.mark()
        pre0 = preload_A_fox(P, C, fox[0].w)
        t_wo = load_w_bf16(P, "wE", Wo_cur, fox[0].wo)
        xg_tok = phase_E(P, C, None, "init", None, 0, None, None, None, x_in, xs_l, xg_l, st_l, st_g, None)
        P.barrier(extra=C.e_stores)
        C.e_stores = []
        outs = xg_tok
        for i in range(nlayers):
            j = i // 2
            last = (i == nlayers - 1)
            if i > 0:
                C.arena.reset(mtop)
                Wo_cur = C.arena.alloc([128, 16 if i % 2 == 0 else 32, DL], BF16)
                m_base = C.arena.mark()
            m0 = m_base
            if i % 2 == 0:
                f = fox[j]
                if i > 0:
                    C.ncr = C.arena.alloc([4, S], F32)
                    assert C.arena.mark() == m_ncr
                    pre_i = preload_A_fox(P, C, f.w)
                    t_wo = load_w_bf16(P, "wE", Wo_cur, f.wo)
                else:
                    pre_i = pre0
                m1 = m_ncr
                stA = phase_A_fox(P, C, xg_l, f.w, f.bf, f.gq, f.gk, fs, xdeps=xg_tok, pre=pre_i)
                P.barrier(extra=stA)
                C.arena.reset(m1)
                stB = phase_B_fox(P, C, f.gqr[0], f.gkr[0], fs, deps=stA)
                C.yload = yload_fox
                C.ydeps = (lambda toks: lambda q2: toks[0:3] + [toks[3 + q2 // 4]])(list(stB))
                ec, wo = 16, f.wo
            else:
                r_ = ret[j]
                stA = phase_A_ret(P, C, xg_l, r_.w, cos_d, sin_d, rs, xdeps=xg_tok,
                                  after_w=(lambda wo_=r_.wo, Wb_=Wo_cur: load_w_bf16(P, "wE", Wb_, wo_)),
                                  pre_w0=C.pre_ret)
                t_wo = C.t_wo
                C.arena.limit = ARENA_BYTES
                P.barrier(extra=stA)
                stB = phase_B_ret(P, C, dec_d, rs, deps=stA)
                C.yload = yload_ret
                C.ydeps = (lambda toks: lambda q2: [toks[q2 // 2], toks[4 + q2 // 2]])(list(stB))
                ec, wo = 32, r_.wo
            assert len(stB) == (5 if i % 2 == 0 else 8)
            P.barrier(extra=C.last_stores)
            C.last_stores = []
            C.arena.reset(m0)
            pre_ret = None
            if i % 2 == 0 and not last:
                WTOP = ARENA_BYTES - 128 * 16 * 1536 * 2 // 128
                Wtop = C.arena.alloc([128, 16, 1536], BF16, at=WTOP)
                C.arena.limit = WTOP
                pre_ret = (Wtop, load_ret_w(P, Wtop, ret[j].w, 0))
            spill = list(C.x_spill)
            C.x_spill = []
            outs = phase_E(P, C, None, "last" if last else "mid", None, ec, wo, lng[i], lnb[i],
                           x_in if i == 0 else xsp, xs_l, xg_l, st_l, st_g, out, deps=spill, pre_wo=(Wo_cur, t_wo))
            xg_tok = outs
            P.barrier(extra=list(C.x_spill) + list(C.e_stores))
            C.e_stores = []
            C.pre_ret = pre_ret
        P.emit(final_waits=outs)
    return nc
```
